# Optimizing a Trainium2 kernel written in Bass

```python
import math
import jax, jax.numpy as jnp
from jax import lax
import numpy as np

D_MODEL = 2048
BATCH = 16
SEQ = 256
DEPTH = 2
DEC_BATCH = 8
DEC_SEQ = 4096
PAST_LEN = 256

GRID_W = 64
ROPE_BASE = 10000.0
EPS = 1e-6
Q_BLOCK = 128
N_MOD = 6

GROUP_W = D_MODEL // 4
DIFF_H = 4
DIFF_DH = GROUP_W // (2 * DIFF_H)
MLA_H = 4
MLA_NOPE = 64
MLA_ROPE = 32
MLA_V = GROUP_W // MLA_H
MLA_Q_LORA = 384
MLA_KV_LORA = 128
NA_H = 8
NA_DH = GROUP_W // NA_H
NA_ROWS = 8
NA_COLS = 16
NA_CB = GRID_W // NA_COLS
NA_BAND = 2 * NA_COLS
POOL_WINDOWS = (2, 4, 8, 16)
POOL_G = 4
POOL_C = GROUP_W // POOL_G
D_FF = 5632
CONV_W = 3

IN_SIZES = (DIFF_H * 2 * DIFF_DH, DIFF_H * 2 * DIFF_DH, DIFF_H * 2 * DIFF_DH,
            MLA_Q_LORA, MLA_KV_LORA, MLA_ROPE,
            GROUP_W, GROUP_W, GROUP_W,
            GROUP_W)
IN_W = sum(IN_SIZES)

kernel_name = 'hybrid_diffusion_prefix_step'

F32 = jnp.float32


def _rms(x, g):
    xf = x.astype(F32)
    y = xf * lax.rsqrt(jnp.mean(xf * xf, axis=-1, keepdims=True) + EPS)
    return (y * g.astype(F32)).astype(x.dtype)


def _modulate(x, g, shift, scale):
    return _rms(x, g) * (1 + scale[:, None]) + shift[:, None]


def _axial_rope(n, dim):
    t = jnp.arange(n)
    rows = (t // GRID_W).astype(F32)
    cols = (t % GRID_W).astype(F32)
    quarter = dim // 4
    inv = ROPE_BASE ** (-jnp.arange(quarter, dtype=F32) / quarter)
    ang = jnp.concatenate([rows[:, None] * inv, cols[:, None] * inv], axis=-1)
    return jnp.cos(ang), jnp.sin(ang)


def _rope(x, cos, sin):
    shape = (cos.shape[0],) + (1,) * (x.ndim - 3) + (cos.shape[1],)
    c = cos.reshape(shape)
    s = sin.reshape(shape)
    x1, x2 = jnp.split(x.astype(F32), 2, axis=-1)
    return jnp.concatenate([x1 * c - x2 * s, x1 * s + x2 * c], axis=-1).astype(x.dtype)


def _q_blocks(q):
    b, n = q.shape[:2]
    return jnp.moveaxis(q.reshape((b, n // Q_BLOCK, Q_BLOCK) + q.shape[2:]), 1, 0)


def _unblock(o):
    o = jnp.moveaxis(o, 0, 1)
    return o.reshape((o.shape[0], o.shape[1] * o.shape[2]) + o.shape[3:])


def _dense_attend(q, k, v):
    scale = q.shape[-1] ** -0.5

    def blk(qb):
        s = jnp.einsum('bqhd,bkhd->bhqk', qb, k).astype(F32) * scale
        p = jax.nn.softmax(s, axis=-1)
        return jnp.einsum('bhqk,bkhe->bqhe', p.astype(v.dtype), v)

    return _unblock(lax.map(blk, _q_blocks(q)))


def _diff_attend(q, k, v, lam):
    scale = q.shape[-1] ** -0.5

    def blk(qb):
        s = jnp.einsum('bqhjd,bkhjd->bhjqk', qb, k).astype(F32) * scale
        p = jax.nn.softmax(s, axis=-1)
        a = p[:, :, 0] - lam * p[:, :, 1]
        return jnp.einsum('bhqk,bkhe->bqhe', a.astype(v.dtype), v)

    return _unblock(lax.map(blk, _q_blocks(q)))


def _diff_qkv(zq, zk, zv, qn_g, kn_g):
    b, n = zq.shape[:2]
    q = _rms(zq.reshape(b, n, DIFF_H, 2, DIFF_DH), qn_g)
    k = _rms(zk.reshape(b, n, DIFF_H, 2, DIFF_DH), kn_g)
    v = zv.reshape(b, n, DIFF_H, 2 * DIFF_DH)
    return q, k, v


def _diff_lambda(lam_p, layer):
    lam_init = 0.8 - 0.6 * math.exp(-0.3 * layer)
    lf = lam_p.astype(F32)
    lam = jnp.exp(jnp.sum(lf[0] * lf[1])) - jnp.exp(jnp.sum(lf[2] * lf[3])) + lam_init
    return lam, lam_init


def _diff_out(o, sub_g, lam_init):
    o = _rms(o, sub_g) * (1 - lam_init)
    return o.reshape(o.shape[0], o.shape[1], DIFF_H * 2 * DIFF_DH)


def _mla_q(zcq, qa_g, w_uq, qn_g):
    b, n = zcq.shape[:2]
    q = (_rms(zcq, qa_g) @ w_uq).reshape(b, n, MLA_H, MLA_NOPE + MLA_ROPE)
    return _rms(q, qn_g)


def _mla_kv(ckv, kpe, w_ukv, kn_g):
    b, m = ckv.shape[:2]
    kv = (ckv @ w_ukv).reshape(b, m, MLA_H, MLA_NOPE + MLA_V)
    k_nope, v = kv[..., :MLA_NOPE], kv[..., MLA_NOPE:]
    k = jnp.concatenate([k_nope, jnp.broadcast_to(kpe[:, :, None, :], (b, m, MLA_H, MLA_ROPE))], axis=-1)
    return _rms(k, kn_g), v


def _rope_tail(x, cos, sin):
    return jnp.concatenate([x[..., :MLA_NOPE], _rope(x[..., MLA_NOPE:], cos, sin)], axis=-1)


def _na_qkv(zq, zk, zv, qn_g, kn_g):
    b, n = zq.shape[:2]
    sh = (b, n, NA_H, NA_DH)
    return _rms(zq.reshape(sh), qn_g), _rms(zk.reshape(sh), kn_g), zv.reshape(sh)


def _na_latent(q, k, v, k_ctx, v_ctx, bias_tab):
    b, n = q.shape[:2]
    rows = n // GRID_W
    kr = min(NA_ROWS, rows)
    scale = NA_DH ** -0.5
    grid = (b, rows, GRID_W, NA_H, NA_DH)
    qg, kg, vg = q.reshape(grid), k.reshape(grid), v.reshape(grid)
    row_start = jnp.clip(jnp.arange(rows) - kr // 2, 0, rows - kr)
    qcol = jnp.arange(NA_CB)[:, None] * NA_COLS + jnp.arange(NA_COLS)[None, :]
    band0 = jnp.clip(jnp.arange(NA_CB) * NA_COLS - NA_COLS // 2, 0, GRID_W - NA_BAND)
    kcol = band0[:, None] + jnp.arange(NA_BAND)[None, :]
    cstart = jnp.clip(qcol - NA_COLS // 2, 0, GRID_W - NA_COLS)
    in_win = (kcol[:, None, :] >= cstart[:, :, None]) & (kcol[:, None, :] < cstart[:, :, None] + NA_COLS)
    dc_idx = jnp.clip(kcol[:, None, :] - qcol[:, :, None], 1 - NA_COLS, NA_COLS - 1) + NA_COLS - 1
    n_loc = kr * NA_BAND

    def row_fn(args):
        r, q_row = args
        rs = row_start[r]
        k_rows = lax.dynamic_slice_in_dim(kg, rs, kr, axis=1)
        v_rows = lax.dynamic_slice_in_dim(vg, rs, kr, axis=1)
        k_band = k_rows[:, :, kcol]
        v_band = v_rows[:, :, kcol]
        qb = q_row.reshape(b, NA_CB, NA_COLS, NA_H, NA_DH)
        dr_idx = rs + jnp.arange(kr) - r + NA_ROWS - 1
        bias = bias_tab[:, dr_idx[None, None, :, None], dc_idx[:, :, None, :]].astype(F32)
        s_loc = jnp.einsum('bjchd,bijkhd->bhjcik', qb, k_band).astype(F32) * scale + bias
        s_loc = jnp.where(in_win[:, :, None, :], s_loc, -jnp.inf)
        s_ctx = jnp.einsum('bjchd,bmhd->bhjcm', qb, k_ctx).astype(F32) * scale
        s = jnp.concatenate([s_loc.reshape(b, NA_H, NA_CB, NA_COLS, n_loc), s_ctx], axis=-1)
        p = jax.nn.softmax(s, axis=-1).astype(v.dtype)
        p_loc = p[..., :n_loc].reshape(b, NA_H, NA_CB, NA_COLS, kr, NA_BAND)
        o = (jnp.einsum('bhjcik,bijkhd->bjchd', p_loc, v_band)
             + jnp.einsum('bhjcm,bmhd->bjchd', p[..., n_loc:], v_ctx))
        return o.reshape(b, GRID_W, NA_H, NA_DH)

    o = lax.map(row_fn, (jnp.arange(rows), jnp.moveaxis(qg, 1, 0)))
    return jnp.moveaxis(o, 0, 1).reshape(b, n, NA_H * NA_DH)


def _pool_mixer(u, w_pool, scale):
    b, n = u.shape[:2]
    uf = u.astype(F32).reshape(b, n, POOL_G, POOL_C)
    cs = jnp.concatenate([jnp.zeros((b, 1, POOL_G, POOL_C), F32), jnp.cumsum(uf, axis=1)], axis=1)
    win = jnp.array(POOL_WINDOWS, dtype=jnp.int32)[None, :]
    t = jnp.arange(n, dtype=jnp.int32)[:, None]
    lo = jnp.clip(t - win // 2, 0, n)
    hi = jnp.clip(t - win // 2 + win, 0, n)
    grp = jnp.arange(POOL_G)[None, :]
    mean = (cs[:, hi, grp] - cs[:, lo, grp]) / (hi - lo).astype(F32)[None, :, :, None]
    d = (mean - uf).astype(u.dtype)
    y = jnp.einsum('bngc,gce->bnge', d, w_pool).reshape(b, n, GROUP_W)
    return y * scale


def _conv_ffn(h, w_up, conv_w, conv_b, w_down):
    u = h @ w_up
    up = jnp.pad(u, ((0, 0), (1, 1), (0, 0)))
    u = up[:, :-2] * conv_w[0] + up[:, 1:-1] * conv_w[1] + up[:, 2:] * conv_w[2] + conv_b
    gate, val = jnp.split(u, 2, axis=-1)
    return (jax.nn.silu(gate) * val) @ w_down


def _layer(x, cvec, P, l, cache):
    b, n = x.shape[:2]
    sh1, sc1, g1, sh2, sc2, g2 = jnp.split(jax.nn.silu(cvec) @ P['ada_w'][l] + P['ada_b'][l], N_MOD, axis=-1)
    h = _modulate(x, P['norm1_g'][l], sh1, sc1)
    z = h @ P['w_in'][l]
    splits = [int(s) for s in np.cumsum(IN_SIZES)[:-1]]
    dq, dk, dv, mcq, mckv, mkpe, nq, nk, nv, pu = jnp.split(z, splits, axis=-1)
    qa, ka, va = _diff_qkv(dq, dk, dv, P['diff_qn_g'][l], P['diff_kn_g'][l])
    lam, lam_init = _diff_lambda(P['diff_lam'][l], l)
    ckv = _rms(mckv, P['mla_kva_g'][l])
    qb = _mla_q(mcq, P['mla_qa_g'][l], P['mla_w_uq'][l], P['mla_qn_g'][l])
    kb, vb = _mla_kv(ckv, mkpe, P['mla_w_ukv'][l], P['mla_kn_g'][l])
    qc, kc, vc = _na_qkv(nq, nk, nv, P['na_qn_g'][l], P['na_kn_g'][l])
    if cache is None:
        oa = _diff_attend(qa, ka, va, lam)
        ob = _dense_attend(qb, kb, vb)
        oc = _dense_attend(qc, kc, vc).reshape(b, n, GROUP_W)
        state = (ka, va, ckv, mkpe, kc, vc)
    else:
        a_k, a_v, b_ckv, b_kpe, c_k, c_v = cache
        cos_a, sin_a = _axial_rope(n, DIFF_DH)
        cos_b, sin_b = _axial_rope(n, MLA_ROPE)
        oa = _diff_attend(_rope(qa, cos_a, sin_a),
                          jnp.concatenate([a_k, _rope(ka, cos_a, sin_a)], axis=1),
                          jnp.concatenate([a_v, va], axis=1), lam)
        kb_ctx, vb_ctx = _mla_kv(b_ckv, b_kpe, P['mla_w_ukv'][l], P['mla_kn_g'][l])
        ob = _dense_attend(_rope_tail(qb, cos_b, sin_b),
                           jnp.concatenate([kb_ctx, _rope_tail(kb, cos_b, sin_b)], axis=1),
                           jnp.concatenate([vb_ctx, vb], axis=1))
        oc = _na_latent(qc, kc, vc, c_k, c_v, P['na_bias'][l])
        state = ()
    oa = _diff_out(oa, P['diff_sub_g'][l], lam_init)
    ob = ob.reshape(b, n, GROUP_W)
    od = _pool_mixer(pu, P['pool_w'][l], P['pool_scale'][l])
    mix = jnp.concatenate([oa, ob, oc, od], axis=-1) @ P['w_out'][l]
    x = x + g1[:, None] * mix
    h = _modulate(x, P['norm2_g'][l], sh2, sc2)
    x = x + g2[:, None] * _conv_ffn(h, P['w_up'][l], P['conv_w'][l], P['conv_b'][l], P['w_down'][l])
    return x, state


def setup_inputs(seed: int = 0) -> dict:
    key = jax.random.key(seed)
    ks = iter(jax.random.split(key, 40))

    def nrm(shape, s):
        return jax.random.normal(next(ks), shape, F32) * s

    def gain(shape):
        return 1.0 + nrm(shape, 0.05)

    D = D_MODEL
    return {
        'x_prompt': nrm((BATCH, SEQ, D), 1.0),
        'x_sample': nrm((DEC_BATCH, DEC_SEQ, D), 1.0),
        'cache_diff_k': nrm((DEC_BATCH, DEPTH, PAST_LEN, DIFF_H, 2, DIFF_DH), 1.0),
        'cache_diff_v': nrm((DEC_BATCH, DEPTH, PAST_LEN, DIFF_H, 2 * DIFF_DH), 1.0),
        'cache_mla_ckv': nrm((DEC_BATCH, DEPTH, PAST_LEN, MLA_KV_LORA), 1.0),
        'cache_mla_kpe': nrm((DEC_BATCH, DEPTH, PAST_LEN, MLA_ROPE), 1.0),
        'cache_na_k': nrm((DEC_BATCH, DEPTH, PAST_LEN, NA_H, NA_DH), 1.0),
        'cache_na_v': nrm((DEC_BATCH, DEPTH, PAST_LEN, NA_H, NA_DH), 1.0),
        'c': nrm((DEC_BATCH, D), 1.0),
        'c_ctx': nrm((D,), 1.0),
        'norm1_g': gain((DEPTH, D)),
        'norm2_g': gain((DEPTH, D)),
        'ada_w': nrm((DEPTH, D, N_MOD * D), 0.5 * D ** -0.5),
        'ada_b': nrm((DEPTH, N_MOD * D), 0.02),
        'w_in': nrm((DEPTH, D, IN_W), D ** -0.5),
        'diff_qn_g': gain((DEPTH, DIFF_DH)),
        'diff_kn_g': gain((DEPTH, DIFF_DH)),
        'diff_lam': nrm((DEPTH, 4, DIFF_DH), 0.1),
        'diff_sub_g': gain((DEPTH, 2 * DIFF_DH)),
        'mla_qa_g': gain((DEPTH, MLA_Q_LORA)),
        'mla_kva_g': gain((DEPTH, MLA_KV_LORA)),
        'mla_w_uq': nrm((DEPTH, MLA_Q_LORA, MLA_H * (MLA_NOPE + MLA_ROPE)), MLA_Q_LORA ** -0.5),
        'mla_w_ukv': nrm((DEPTH, MLA_KV_LORA, MLA_H * (MLA_NOPE + MLA_V)), MLA_KV_LORA ** -0.5),
        'mla_qn_g': gain((DEPTH, MLA_NOPE + MLA_ROPE)),
        'mla_kn_g': gain((DEPTH, MLA_NOPE + MLA_ROPE)),
        'na_qn_g': gain((DEPTH, NA_DH)),
        'na_kn_g': gain((DEPTH, NA_DH)),
        'na_bias': nrm((DEPTH, NA_H, 2 * NA_ROWS - 1, 2 * NA_COLS - 1), 0.1),
        'pool_w': nrm((DEPTH, POOL_G, POOL_C, POOL_C), POOL_C ** -0.5),
        'pool_scale': gain((DEPTH, GROUP_W)),
        'w_out': nrm((DEPTH, D, D), D ** -0.5),
        'w_up': nrm((DEPTH, D, 2 * D_FF), D ** -0.5),
        'conv_w': nrm((DEPTH, CONV_W, 2 * D_FF), CONV_W ** -0.5),
        'conv_b': nrm((DEPTH, 2 * D_FF), 0.02),
        'w_down': nrm((DEPTH, D_FF, D), D_FF ** -0.5),
    }


def reference(x_prompt, x_sample, cache_diff_k, cache_diff_v, cache_mla_ckv, cache_mla_kpe, cache_na_k,
              cache_na_v, c, c_ctx, norm1_g, norm2_g, ada_w, ada_b, w_in, diff_qn_g, diff_kn_g, diff_lam,
              diff_sub_g, mla_qa_g, mla_kva_g, mla_w_uq, mla_w_ukv, mla_qn_g, mla_kn_g, na_qn_g, na_kn_g,
              na_bias, pool_w, pool_scale, w_out, w_up, conv_w, conv_b, w_down):
    P = dict(norm1_g=norm1_g, norm2_g=norm2_g, ada_w=ada_w, ada_b=ada_b, w_in=w_in,
             diff_qn_g=diff_qn_g, diff_kn_g=diff_kn_g, diff_lam=diff_lam, diff_sub_g=diff_sub_g,
             mla_qa_g=mla_qa_g, mla_kva_g=mla_kva_g, mla_w_uq=mla_w_uq, mla_w_ukv=mla_w_ukv,
             mla_qn_g=mla_qn_g, mla_kn_g=mla_kn_g, na_qn_g=na_qn_g, na_kn_g=na_kn_g, na_bias=na_bias,
             pool_w=pool_w, pool_scale=pool_scale, w_out=w_out, w_up=w_up, conv_w=conv_w,
             conv_b=conv_b, w_down=w_down)
    xp = x_prompt
    states = []
    for l in range(DEPTH):
        xp, st = _layer(xp, c_ctx[None, :], P, l, None)
        states.append(st)
    xs = x_sample
    for l in range(DEPTH):
        cache_l = (cache_diff_k[:, l], cache_diff_v[:, l], cache_mla_ckv[:, l], cache_mla_kpe[:, l],
                   cache_na_k[:, l], cache_na_v[:, l])
        xs, _ = _layer(xs, c, P, l, cache_l)
    new_diff_k = jnp.stack([s[0] for s in states], axis=1)
    new_diff_v = jnp.stack([s[1] for s in states], axis=1)
    new_mla_ckv = jnp.stack([s[2] for s in states], axis=1)
    new_mla_kpe = jnp.stack([s[3] for s in states], axis=1)
    new_na_k = jnp.stack([s[4] for s in states], axis=1)
    new_na_v = jnp.stack([s[5] for s in states], axis=1)
    return (xp, xs, new_diff_k, new_diff_v, new_mla_ckv, new_mla_kpe, new_na_k, new_na_v)
```

```python
import math
import os as _os
from contextlib import ExitStack
import numpy as np
import concourse.bass as bass
import concourse.mybir as mybir
from concourse.bass_utils import run_bass_kernel_spmd

F32, BF16 = mybir.dt.float32, mybir.dt.bfloat16
AF = mybir.ActivationFunctionType
ALU = mybir.AluOpType

D = 2048
FC = 16
L = 2
NP = 512
NS = 4096
NT = NP + NS
NCH = NT // 512
PAST = 256
NKEY = 256 + 256 + PAST + NS
KB = [0, 256, 512]
SEQ_T0 = [0, 256, 512]
SEQ_N = [256, 256, 4096]
SEQ_CTX = [0, 0, 256]
IN_W = 4128
DFF = 5632
EPS = 1e-6
COMPUTE = ['pe', 'act', 'dve', 'pool']
STORE_Q = 'pool'


class Prog:
    def __init__(s, nc, es):
        s.nc = nc
        s.ops = {e: [] for e in ['pe', 'act', 'dve', 'pool', 'sp']}
        s.cnt = {e: 0 for e in COMPUTE}
        s.sem = {}
        for e in COMPUTE:
            s.sem['c_' + e] = es.enter_context(nc.semaphore('c_' + e))
        s.dq = {'sp': ['d_sp%d' % i for i in range(20)], 'pool': ['d_pl%d' % i for i in range(12)]}
        s.duse = {}
        for q in s.dq:
            for n in s.dq[q]:
                s.sem[n] = es.enter_context(nc.semaphore(n))
                s.duse[n] = 0
        s.drr = {'sp': 0, 'pool': 0}
        s.waited = {e: {} for e in s.ops}
        s.lastw = {}
        s.readers = {}
        s.nins = 0

    def _deps(s, eng, reads, writes):
        need = {}

        def add(tok):
            if tok is None:
                return
            n, v = tok
            if eng == 'pe' and n == 'c_pe':
                return
            if need.get(n, 0) < v:
                need[n] = v
        for r in reads:
            add(s.lastw.get(r))
        for r in writes:
            add(s.lastw.get(r))
            for n, v in s.readers.get(r, {}).items():
                add((n, v))
        out = []
        for n, v in need.items():
            if s.waited[eng].get(n, 0) < v:
                s.waited[eng][n] = v
                out.append((n, v))
        return out

    def _commit(s, tok, reads, writes):
        for r in reads:
            d = s.readers.setdefault(r, {})
            if d.get(tok[0], 0) < tok[1]:
                d[tok[0]] = tok[1]
        for r in writes:
            s.lastw[r] = tok
            s.readers[r] = {}

    ENGMAP = {'pe': 'tensor', 'act': 'scalar', 'dve': 'vector', 'pool': 'gpsimd', 'sp': 'sync'}

    def _issue(s, e, fn, waits, inc):
        eng = getattr(s.nc, s.ENGMAP[e])
        if fn is None:
            for n, v in waits:
                eng.wait_ge(s.sem[n], v)
            return
        if e == 'pe':
            pre, emb = waits, None
        else:
            pre, emb = waits[1:], (waits[0] if waits else None)
        for n, v in pre:
            eng.wait_ge(s.sem[n], v)
        ins = fn(eng)
        if emb is not None:
            ins.wait_op(s.sem[emb[0]], emb[1], "sem-ge")
        if inc is not None:
            ins.then_inc(s.sem[inc[0]], inc[1])
        s.nins += 1

    def op(s, eng, fn, r=(), w=()):
        waits = s._deps(eng, r, w)
        s.cnt[eng] += 1
        tok = ('c_' + eng, s.cnt[eng])
        s._issue(eng, fn, waits, (tok[0], 1))
        s._commit(tok, r, w)

    def dma(s, q, out, in_, r=(), w=(), slow=False):
        if q == 'pool':
            q = STORE_Q
        waits = s._deps(q, r, w)
        names = s.dq[q]
        n = names[s.drr[q] % len(names)]
        s.drr[q] += 1
        if s.duse[n] > 0 and s.waited[q].get(n, 0) < 16 * s.duse[n]:
            s.waited[q][n] = 16 * s.duse[n]
            waits.append((n, 16 * s.duse[n]))
        s.duse[n] += 1
        tok = (n, 16 * s.duse[n])
        if slow:
            s._issue(q, lambda e: e.dma_start(out=out, in_=in_, allow_slow_non_contiguous=True), waits, (n, 16))
        else:
            s._issue(q, lambda e: e.dma_start(out=out, in_=in_), waits, (n, 16))
        s._commit(tok, r, w)

    def barrier(s):
        toks = [('c_' + e, s.cnt[e]) for e in COMPUTE if s.cnt[e] > 0]
        toks += [(n, 16 * u) for n, u in s.duse.items() if u > 0]
        for e in s.ops:
            waits = [(n, v) for n, v in toks if s.waited[e].get(n, 0) < v]
            for n, v in waits:
                s.waited[e][n] = v
            if waits:
                s._issue(e, None, waits, None)
        s.lastw.clear()
        s.readers.clear()

    def flush(s):
        pass


def _host_consts():
    c = {}
    cst = np.zeros((128, 832), np.float32)
    cst[:, 0:128] = np.eye(128)
    cst[:, 128:256] = 1.0
    for b in range(2):
        cst[b * 64:(b + 1) * 64, 256 + b * 64:256 + (b + 1) * 64] = 1.0
    R = np.zeros((128, 128), np.float32)
    for b in range(2):
        for m in range(32):
            R[b * 64 + m + 32, b * 64 + m] = -1.0
            R[b * 64 + m, b * 64 + m + 32] = 1.0
    cst[:, 384:512] = R
    Rm = np.zeros((128, 128), np.float32)
    for m in range(16):
        Rm[64 + m + 16, 64 + m] = -1.0
        Rm[64 + m, 64 + m + 16] = 1.0
    cst[:, 512:640] = Rm
    sh = np.zeros((128, 128), np.float32)
    for i in range(32):
        sh[i, 64 + i] = 1.0
    cst[:, 640:768] = sh
    for i in range(64):
        cst[i, 768 + 63 - i] = 1.0
    c['cst'] = cst
    t = np.arange(NS)
    rows = (t // 64).astype(np.float32)
    cols = (t % 64).astype(np.float32)

    def ang(dim):
        q = dim // 4
        inv = (10000.0 ** (-np.arange(q, dtype=np.float32) / q)).astype(np.float32)
        return np.concatenate([rows[:, None] * inv, cols[:, None] * inv], axis=-1).astype(np.float32)
    a64 = ang(64)
    a32 = ang(32)
    rope = np.zeros((4, 128, NS), np.float32)
    for p in range(128):
        d = p % 64
        rope[0, p] = np.cos(a64[:, d % 32])
        rope[1, p] = np.sin(a64[:, d % 32])
    rope[2, :64] = 1.0
    for i in range(32):
        rope[2, 64 + i] = np.cos(a32[:, i % 16])
        rope[3, 64 + i] = np.sin(a32[:, i % 16])
    c['rope'] = rope
    invc = np.zeros((4, NT), np.float32)
    for g, w in enumerate((2, 4, 8, 16)):
        for s in range(3):
            n = SEQ_N[s]
            tt = np.arange(n)
            lo = np.clip(tt - w // 2, 0, n)
            hi = np.clip(tt - w // 2 + w, 0, n)
            invc[g, SEQ_T0[s]:SEQ_T0[s] + n] = 1.0 / (hi - lo)
    c['invc'] = np.ascontiguousarray(np.broadcast_to(invc[None], (128, 4, NT))).astype(np.float32)
    nam = np.zeros((3, 8, 128, 512), np.float32)
    for ty, (r0, kr0) in enumerate(((0, 0), (8, 4), (56, 48))):
        for ch in range(8):
            for krl in range(2):
                kr = kr0 + 2 * ch + krl
                for qrl in range(8):
                    qr = r0 + qrl
                    rs = min(max(qr - 4, 0), 56)
                    if not (rs <= kr < rs + 8):
                        continue
                    qc = np.arange(64)
                    cs = np.clip(qc - 8, 0, 48)
                    kc = np.arange(64)
                    ok = (kc[:, None] >= cs[None, :]) & (kc[:, None] < cs[None, :] + 16)
                    nam[ty, ch, krl * 64:(krl + 1) * 64, qrl * 64:(qrl + 1) * 64] = ok
    c['nam'] = nam
    return c


NA_TYPES = ((0, 0), (8, 4), (56, 48))


def na_block_info(qb):
    r0 = qb * 8
    kr0 = min(max(r0 - 4, 0), 48)
    ty = 0 if qb == 0 else (2 if qb == 7 else 1)
    return r0, kr0, ty


def build_program(stop_after=None, dbg=False):
    nc = bass.Bass("TRN2", target_bir_lowering=False)
    es = ExitStack()

    def din(name, shape, dt=F32):
        return nc.dram_tensor(name, list(shape), dt, kind="ExternalInput").ap()

    def dout(name, shape):
        return nc.dram_tensor(name, list(shape), F32, kind="ExternalOutput").ap()

    def dscr(name, shape, dt, out=False):
        return nc.dram_tensor(name, list(shape), dt, kind="ExternalOutput" if (out and dbg) else "Internal").ap()

    I = {}
    for name, shape in [
        ('xp', (NP, D)), ('xs', (NS, D)), ('cdk', (L, PAST, 512)), ('cdv', (L, PAST, 512)),
        ('cckv', (L, PAST, 128)), ('ckpe', (L, PAST, 32)), ('cnk', (L, PAST, 512)), ('cnv', (L, PAST, 512)),
        ('cvec', (2, D)), ('norm1_g', (L, D)), ('norm2_g', (L, D)), ('ada_w', (L, D, 6 * D)), ('ada_b', (L, 6 * D)),
        ('w_in', (L, D, IN_W)), ('diff_qn_g', (L, 64)), ('diff_kn_g', (L, 64)), ('diff_lam', (L, 4, 64)),
        ('diff_sub_g', (L, 128)), ('mla_qa_g', (L, 384)), ('mla_kva_g', (L, 128)), ('mla_w_uq', (L, 384, 384)),
        ('mla_w_ukv', (L, 128, 768)), ('mla_qn_g', (L, 96)), ('mla_kn_g', (L, 96)), ('na_qn_g', (L, 64)),
        ('na_kn_g', (L, 64)), ('na_bias', (L, 8, 15, 31)), ('pool_w', (L, 4, 128, 128)), ('pool_scale', (L, 512)),
        ('w_out', (L, D, D)), ('w_up', (L, D, 2 * DFF)), ('conv_w', (L, 3, 2 * DFF)), ('conv_b', (L, 2 * DFF)),
        ('w_down', (L, DFF, D)),
        ('cst', (128, 832)), ('rope', (4, 128, NS)), ('invc', (128, 4, NT)), ('nam', (3, 8, 128, 512)),
    ]:
        I[name] = din(name, shape)
    O = {}
    for name, shape in [('yp', (NP, D)), ('ys', (NS, D)), ('o_dk', (2 * L * 256, 512)), ('o_dv', (2 * L * 256, 512)),
                        ('o_ckv', (2 * L * 256, 128)), ('o_kpe', (2 * L * 256, 32)), ('o_nk', (2 * L * 256, 512)),
                        ('o_nv', (2 * L * 256, 512))]:
        O[name] = dout(name, shape)

    S = {}
    S['xTa'] = dscr('xTa', (FC, 128, NT), F32, True)
    S['xTb'] = dscr('xTb', (FC, 128, NT), F32, True)
    S['x1T'] = dscr('x1T', (FC, 128, NT), F32, True)
    S['mixT'] = dscr('mixT', (FC, 128, NT), BF16, True)
    S['puT'] = dscr('puT', (4, 128, NT), F32, True)
    for m in 'dmn':
        S['QT' + m] = dscr('QT' + m, (128, 4, NT), BF16, True)
        S['KT' + m] = dscr('KT' + m, (128, 4, NKEY), BF16, True)
        S['V' + m] = dscr('V' + m, (NKEY // 128, 128, 512), BF16, True)
    S['WinG'] = dscr('WinG', (L, 8, 128, FC, 512), BF16)
    S['Wout'] = dscr('Wout', (L, 4, 128, FC, 512), BF16)
    S['Wup'] = dscr('Wup', (L, 22, 128, FC, 512), BF16)
    S['Wdn'] = dscr('Wdn', (L, 4, 4, 128, 11, 512), BF16)
    S['biasP'] = dscr('biasP', (8, 15, 127), F32)
    S['Emask'] = dscr('Emask', (3, 8, 128, 8, 512), BF16, True)

    P = Prog(nc, es)
    sb = {}

    uid = [0]

    def salloc(name, shape, dt, stack):
        uid[0] += 1
        t = stack.enter_context(nc.sbuf_tensor('sb%d_%s' % (uid[0], name), list(shape), dt))
        sb[name] = t
        return t

    ps = es.enter_context(nc.psum_tensor("ps", [128, 8, 512], F32))

    def PSB(b):
        return 'ps%d' % b

    cst = salloc('cst', (128, 832), F32, es)
    cstb = salloc('cstb', (128, 832), BF16, es)
    epsT = salloc('epsT', (128, 1), F32, es)
    PRM = salloc('PRM', (128, L * 2 * 6 * FC), F32, es)
    n1g = salloc('n1g', (128, L * FC), F32, es)
    n2g = salloc('n2g', (128, L * FC), F32, es)
    GV = salloc('GV', (128, L * 16), F32, es)
    CW = salloc('CW', (128, L * 4 * 88), F32, es)
    identf = cst[:, 0:128]
    identb = cstb[:, 0:128]
    onesb = cstb[:, 128:256]
    blk64b = cstb[:, 256:384]
    Rd = cst[:, 384:512]
    Rm = cst[:, 512:640]
    shiftb = cstb[:, 640:768]

    def prm(l, v, m):
        o = ((l * 2 + v) * 6 + m) * FC
        return PRM[:, o:o + FC]

    def gv(l, j, n=128):
        return GV[0:n, l * 16 + j:l * 16 + j + 1]

    def cw(l, k, t):
        o = (l * 4 + k) * 88 + t
        return CW[:, o:o + 1]

    P.dma('sp', cst[:], I['cst'][:, :], w=['cst'])
    P.op('dve', lambda e: e.tensor_copy(out=cstb[:], in_=cst[:]), r=['cst'], w=['cstb'])
    P.op('dve', lambda e: e.memset(epsT[:], EPS), w=['epsT'])
    for l in range(L):
        P.dma('sp', n1g[:, l * FC:(l + 1) * FC], I['norm1_g'][l].rearrange("(c p) -> p c", p=128), w=['n1g'], slow=True)
        P.dma('sp', n2g[:, l * FC:(l + 1) * FC], I['norm2_g'][l].rearrange("(c p) -> p c", p=128), w=['n2g'], slow=True)
        for j, (nm, n) in enumerate([('diff_qn_g', 64), ('diff_kn_g', 64), ('diff_sub_g', 128), ('mla_kva_g', 128),
                                     ('mla_qn_g', 96), ('mla_kn_g', 96), ('na_qn_g', 64), ('na_kn_g', 64)]):
            src = I[nm][l].rearrange("(p o) -> p o", o=1)
            P.dma('sp', GV[0:n, l * 16 + j:l * 16 + j + 1], src, w=['GV'], slow=True)
            if n == 64:
                P.dma('sp', GV[64:128, l * 16 + j:l * 16 + j + 1], src, w=['GV'], slow=True)
        P.dma('sp', GV[:, l * 16 + 8:l * 16 + 11], I['mla_qa_g'][l].rearrange("(c p) -> p c", p=128), w=['GV'], slow=True)
        P.dma('sp', GV[:, l * 16 + 11:l * 16 + 15], I['pool_scale'][l].rearrange("(c p) -> p c", p=128), w=['GV'], slow=True)
        P.op('dve', lambda e, l=l: e.tensor_scalar(out=GV[:, l * 16 + 2:l * 16 + 3], in0=GV[:, l * 16 + 2:l * 16 + 3],
                                                   scalar1=1.0 - (0.8 - 0.6 * math.exp(-0.3 * l)), scalar2=None,
                                                   op0=ALU.mult), r=['GV'], w=['GV'])
        for k in range(3):
            P.dma('sp', CW[:, (l * 4 + k) * 88:(l * 4 + k + 1) * 88],
                  I['conv_w'][l, k].rearrange("(c p) -> p c", p=128), w=['CW'], slow=True)
        P.dma('sp', CW[:, (l * 4 + 3) * 88:(l * 4 + 4) * 88], I['conv_b'][l].rearrange("(c p) -> p c", p=128),
              w=['CW'], slow=True)

    with ExitStack() as st:
        lamt = salloc('lamt', (128, L * 256), F32, st)
        lamw = salloc('lamw', (128, 8), F32, st)
        for l in range(L):
            P.dma('sp', lamt[:, l * 256:(l + 1) * 256],
                  I['diff_lam'][l].rearrange("a b -> (a b)").partition_broadcast(128), w=['lamt'])
        for l in range(L):
            lam_init = 0.8 - 0.6 * math.exp(-0.3 * l)
            o = l * 256
            P.op('dve', lambda e, o=o: e.tensor_tensor(out=lamt[:, o:o + 64], in0=lamt[:, o:o + 64],
                                                       in1=lamt[:, o + 64:o + 128], op=ALU.mult),
                 r=['lamt'], w=['lamt'])
            P.op('dve', lambda e, o=o: e.tensor_tensor(out=lamt[:, o + 128:o + 192], in0=lamt[:, o + 128:o + 192],
                                                       in1=lamt[:, o + 192:o + 256], op=ALU.mult),
                 r=['lamt'], w=['lamt'])
            P.op('dve', lambda e, o=o: e.reduce_sum(out=lamw[:, 0:1], in_=lamt[:, o:o + 64],
                                                    axis=mybir.AxisListType.X), r=['lamt'], w=['lamw'])
            P.op('dve', lambda e, o=o: e.reduce_sum(out=lamw[:, 1:2], in_=lamt[:, o + 128:o + 192],
                                                    axis=mybir.AxisListType.X), r=['lamt'], w=['lamw'])
            P.op('act', lambda e: e.activation(out=lamw[:, 2:4], in_=lamw[:, 0:2], func=AF.Exp),
                 r=['lamw'], w=['lamw'])
            P.op('dve', lambda e, l=l, li=lam_init: e.scalar_tensor_tensor(
                out=GV[:, l * 16 + 15:l * 16 + 16], in0=lamw[:, 3:4], scalar=-li, in1=lamw[:, 2:3],
                op0=ALU.add, op1=ALU.subtract), r=['lamw'], w=['GV'])
        P.barrier()
        P.flush()

    with ExitStack() as st:
        cT = salloc('cT', (128, FC, 2), F32, st)
        adb = salloc('adb', (128, L * 96), F32, st)
        aw = [salloc('aw%d' % i, (128, FC, 512), F32, st) for i in range(2)]
        for v in range(2):
            P.dma('sp', cT[:, :, v], I['cvec'][v].rearrange("(c p) -> p c", p=128), w=['cT'], slow=True)
        P.op('act', lambda e: e.activation(out=cT[:], in_=cT[:], func=AF.Silu), r=['cT'], w=['cT'])
        for l in range(L):
            P.dma('sp', adb[:, l * 96:(l + 1) * 96], I['ada_b'][l].rearrange("(c p) -> p c", p=128), w=['adb'], slow=True)
        gi = 0
        for l in range(L):
            awv = I['ada_w'][l].rearrange("(c p) n -> p c n", p=128)
            for g in range(24):
                slot = gi % 2
                gi += 1
                P.dma('sp', aw[slot][:], awv[:, :, g * 512:(g + 1) * 512], w=['aw%d' % slot])
                bank = g % 2
                for j in range(4):
                    for fc in range(FC):
                        P.op('pe', lambda e, slot=slot, j=j, fc=fc, bank=bank: e.matmul(
                            ps[:, bank, j * 2:j * 2 + 2], lhsT=aw[slot][:, fc, j * 128:(j + 1) * 128],
                            rhs=cT[:, fc, :], start=(fc == 0), stop=(fc == FC - 1)),
                            r=['aw%d' % slot, 'cT'], w=[PSB(bank)])
                for j in range(4):
                    ct = g * 4 + j
                    m, fc = ct // 16, ct % 16
                    for v in range(2):
                        o = ((l * 2 + v) * 6 + m) * FC + fc
                        P.op('dve', lambda e, o=o, j=j, v=v, bank=bank, l=l, ct=ct: e.tensor_tensor(
                            out=PRM[:, o:o + 1], in0=ps[:, bank, j * 2 + v:j * 2 + v + 1],
                            in1=adb[:, l * 96 + ct:l * 96 + ct + 1], op=ALU.add),
                            r=[PSB(bank), 'adb'], w=['PRM'])
        for l in range(L):
            for v in range(2):
                for (m, ng) in ((1, n1g), (4, n2g)):
                    P.op('dve', lambda e, l=l, v=v, m=m, ng=ng: e.scalar_tensor_tensor(
                        out=prm(l, v, m), in0=prm(l, v, m), scalar=1.0, in1=ng[:, l * FC:(l + 1) * FC],
                        op0=ALU.add, op1=ALU.mult), r=['PRM', 'n1g', 'n2g'], w=['PRM'])
        P.barrier()
        P.flush()

    with ExitStack() as st:
        stg = [salloc('stg%d' % i, (128, FC * 512), F32, st) for i in range(2)]
        stb = [salloc('stb%d' % i, (128, FC * 512), BF16, st) for i in range(2)]
        jobs = []
        WIN_G = [0, 512, 1024, 1536, 2080, 2592, 3104, 3616]
        for l in range(L):
            wv = I['w_in'][l].rearrange("(c p) n -> p c n", p=128)
            for g in range(8):
                jobs.append((wv[:, :, WIN_G[g]:WIN_G[g] + 512], S['WinG'][l, g], FC))
            wv = I['w_out'][l].rearrange("(c p) n -> p c n", p=128)
            for g in range(4):
                jobs.append((wv[:, :, g * 512:(g + 1) * 512], S['Wout'][l, g], FC))
            wv = I['w_up'][l].rearrange("(c p) n -> p c n", p=128)
            for g in range(22):
                jobs.append((wv[:, :, g * 512:(g + 1) * 512], S['Wup'][l, g], FC))
            wv = I['w_down'][l].rearrange("(c p) n -> p c n", p=128)
            for dg in range(4):
                for fq in range(4):
                    jobs.append((wv[:, fq * 11:(fq + 1) * 11, dg * 512:(dg + 1) * 512], S['Wdn'][l, dg, fq], 11))
        cengs = ['dve', 'act']
        def cload(i):
            src, dst, nk = jobs[i]
            slot = i % 2
            a = stg[slot][:, 0:nk * 512].rearrange("p (c n) -> p c n", n=512)
            P.dma('sp', a, src, w=['stg%d' % slot])
        cload(0)
        for i, (src, dst, nk) in enumerate(jobs):
            slot = i % 2
            b = stb[slot][:, 0:nk * 512]
            if i + 1 < len(jobs):
                cload(i + 1)
            ce = cengs[i % 2]
            if ce == 'act':
                P.op('act', lambda e, slot=slot, nk=nk: e.copy(out=stb[slot][:, 0:nk * 512],
                                                                in_=stg[slot][:, 0:nk * 512]),
                     r=['stg%d' % slot], w=['stb%d' % slot])
            else:
                P.op(ce, lambda e, slot=slot, nk=nk: e.tensor_copy(out=stb[slot][:, 0:nk * 512],
                                                                    in_=stg[slot][:, 0:nk * 512]),
                     r=['stg%d' % slot], w=['stb%d' % slot])
            P.dma('pool', dst.rearrange("p c n -> p (c n)"), b, r=['stb%d' % slot], w=['Wscr'])
        P.barrier()
        P.flush()

    with ExitStack() as st:
        xtok = [salloc('xtok%d' % i, (128, D), F32, st) for i in range(2)]
        xTt = [salloc('xTt%d' % i, (128, FC, 128), F32, st) for i in range(2)]
        def t0load(tt):
            slot = tt % 2
            t0 = tt * 128
            src = I['xp'][t0:t0 + 128, :] if t0 < NP else I['xs'][t0 - NP:t0 - NP + 128, :]
            P.dma('sp', xtok[slot][:], src, w=['xtok%d' % slot])
        t0load(0)
        for tt in range(NT // 128):
            slot = tt % 2
            t0 = tt * 128
            if tt + 1 < NT // 128:
                t0load(tt + 1)
            for q in range(4):
                bank = (tt * 4 + q) % 8
                for j in range(4):
                    fc = q * 4 + j
                    P.op('pe', lambda e, slot=slot, fc=fc, j=j, bank=bank: e.transpose(
                        out=ps[:, bank, j * 128:(j + 1) * 128], in_=xtok[slot][:, fc * 128:(fc + 1) * 128],
                        identity=identf), r=['xtok%d' % slot, 'cst'], w=[PSB(bank)])
                eng = 'act' if q % 2 else 'dve'
                if eng == 'act':
                    P.op('act', lambda e, slot=slot, q=q, bank=bank: e.copy(
                        out=xTt[slot][:, q * 4:(q + 1) * 4, :].rearrange("p c t -> p (c t)"), in_=ps[:, bank, :]),
                        r=[PSB(bank)], w=['xTt%d' % slot])
                else:
                    P.op('dve', lambda e, slot=slot, q=q, bank=bank: e.tensor_copy(
                        out=xTt[slot][:, q * 4:(q + 1) * 4, :].rearrange("p c t -> p (c t)"), in_=ps[:, bank, :]),
                        r=[PSB(bank)], w=['xTt%d' % slot])
            P.dma('pool', S['xTa'][:, :, t0:t0 + 128].rearrange("c p t -> p c t"), xTt[slot][:],
                  r=['xTt%d' % slot], w=['xTa'])
        P.barrier()
        P.flush()

    if stop_after == 'T0':
        return finish(nc, es, P)

    rr = {'b': 0, 'pool': list(range(8))}

    def nb():
        rr['b'] += 1
        return rr['pool'][rr['b'] % len(rr['pool'])]

    cnt_eng = {'i': 0}

    def copy_op(out, in_, r, w, eng=None):
        if eng is None:
            cnt_eng['i'] += 1
            eng = 'act' if cnt_eng['i'] % 2 else 'dve'
        if eng == 'act':
            P.op('act', lambda e: e.copy(out=out, in_=in_), r=r, w=w)
        else:
            P.op(eng, lambda e: e.tensor_copy(out=out, in_=in_), r=r, w=w)

    def chunk_info(c):
        if c == 0:
            return 0, [(0, 0, 256, 0), (1, 256, 256, 0)]
        return 1, [(2, 0, 512, (c - 1) * 512)]

    def rstd_from_ps(bank, M, N, nfeat, rt, rs, rtn, rsn):
        P.op('act', lambda e: e.activation(out=rt[0:M, 0:N], in_=ps[0:M, bank, 0:N], func=AF.Sqrt,
                                           bias=epsT[0:M, :], scale=1.0 / nfeat), r=[PSB(bank), 'epsT'], w=[rtn])
        P.op('dve', lambda e: e.reciprocal(out=rs[0:M, 0:N], in_=rt[0:M, 0:N]), r=[rtn], w=[rsn])

    def make_hT(T, xT, xTn, hT, hTn, l, v, which, N=512, ncols_off=0):
        o = ncols_off
        gg = prm(l, v, 1 if which == 1 else 4)
        shv = prm(l, v, 0 if which == 1 else 3)
        bank = nb()
        for fc in range(FC):
            sl = fc % 2
            P.op('act', lambda e, fc=fc, sl=sl: e.activation(out=T['sq'][sl][:, 0:N], in_=xT[:, fc, o:o + N],
                                                            func=AF.Square), r=[xTn], w=['sq%d' % sl])
            P.op('pe', lambda e, fc=fc, sl=sl: e.matmul(ps[:, bank, 0:N], lhsT=onesb, rhs=T['sq'][sl][:, 0:N],
                                                        start=(fc == 0), stop=(fc == FC - 1)),
                 r=['sq%d' % sl, 'cstb'], w=[PSB(bank)])
        rstd_from_ps(bank, 128, N, D, T['rt'], T['rsx'], 'rt', 'rsx')
        for fc in range(FC):
            sl = fc % 2
            P.op('dve', lambda e, fc=fc, sl=sl: e.scalar_tensor_tensor(
                out=T['tmp'][sl][:, 0:N], in0=xT[:, fc, o:o + N], scalar=gg[:, fc:fc + 1], in1=T['rsx'][:, 0:N],
                op0=ALU.mult, op1=ALU.mult), r=[xTn, 'rsx', 'PRM'], w=['tmp%d' % sl])
            P.op('act', lambda e, fc=fc, sl=sl: e.activation(out=hT[:, fc, o:o + N], in_=T['tmp'][sl][:, 0:N],
                                                            func=AF.Identity, bias=shv[:, fc:fc + 1], scale=1.0),
                 r=['tmp%d' % sl, 'PRM'], w=[hTn])

    def alloc_common(st):
        T = {}
        T['sq'] = [salloc('sq%d' % i, (128, 512), BF16, st) for i in range(2)]
        T['rt'] = salloc('rt', (128, 512), F32, st)
        T['rsx'] = salloc('rsx', (128, 512), F32, st)
        T['rs'] = salloc('rs', (128, 512), F32, st)
        T['tmp'] = [salloc('tmp%d' % i, (128, 512), F32, st) for i in range(2)]
        T['qn'] = [salloc('qn%d' % i, (128, 512), F32, st) for i in range(4)]
        T['t1'] = salloc('t1', (128, 512), F32, st)
        T['t2'] = salloc('t2', (128, 512), F32, st)
        T['ob'] = [salloc('ob%d' % i, (128, 512), BF16, st) for i in range(4)]
        T['obi'] = 0
        T['qni'] = 0
        return T

    def head_norm(T, src, srcn, M, N, ones_l, nfeat, gcol, rope=None, f32_out=None, f32n=None, defer=False):
        sl = T['qni'] % 2
        T['qni'] += 1
        P.op('act', lambda e: e.activation(out=T['sq'][sl][0:M, 0:N], in_=src, func=AF.Square),
             r=[srcn], w=['sq%d' % sl])
        import os as _os
        hs = int(_os.environ.get('HN_STOP', '9'))
        if hs <= 1:
            return T['ob'][0], 'ob0'
        bank = nb()
        P.op('pe', lambda e: e.matmul(ps[0:M, bank, 0:N], lhsT=ones_l, rhs=T['sq'][sl][0:M, 0:N],
                                      start=True, stop=True), r=['sq%d' % sl, 'cstb'], w=[PSB(bank)])
        if hs <= 2:
            return T['ob'][0], 'ob0'
        rstd_from_ps(bank, M, N, nfeat, T['rt'], T['rs'], 'rt', 'rs')
        if hs <= 3:
            return T['ob'][0], 'ob0'
        qi = T.get('qn4', 0) % 4
        T['qn4'] = T.get('qn4', 0) + 1
        qn = T['qn'][qi]
        qnn = 'qn%d' % qi
        if rope is None and f32_out is None:
            oi = T['obi'] % 4
            T['obi'] += 1
            ob = T['ob'][oi]
            obn = 'ob%d' % oi
            P.op('dve', lambda e: e.scalar_tensor_tensor(out=ob[0:M, 0:N], in0=src, scalar=gcol,
                                                         in1=T['rs'][0:M, 0:N], op0=ALU.mult, op1=ALU.mult),
                 r=[srcn, 'rs', 'GV'], w=[obn])
            return ob, obn
        P.op('dve', lambda e: e.scalar_tensor_tensor(out=qn[0:M, 0:N], in0=src, scalar=gcol, in1=T['rs'][0:M, 0:N],
                                                     op0=ALU.mult, op1=ALU.mult), r=[srcn, 'rs', 'GV'], w=[qnn])
        if hs <= 4:
            return T['ob'][0], 'ob0'
        if f32_out is not None:
            copy_op(f32_out, qn[0:M, 0:N], r=[qnn], w=[f32n])
        oi = T['obi'] % 4
        T['obi'] += 1
        ob = T['ob'][oi]
        obn = 'ob%d' % oi
        if rope is None:
            copy_op(ob[0:M, 0:N], qn[0:M, 0:N], r=[qnn], w=[obn])
            if hs <= 5:
                return T['ob'][0], 'ob0'
        elif defer:
            def part_b():
                Rl, Ct, St, ropen = rope
                b2 = nb()
                P.op('pe', lambda e: e.matmul(ps[0:M, b2, 0:N], lhsT=Rl, rhs=qn[0:M, 0:N], start=True, stop=True),
                     r=[qnn, 'cst'], w=[PSB(b2)])
                P.op('dve', lambda e: e.tensor_tensor(out=T['t1'][0:M, 0:N], in0=qn[0:M, 0:N], in1=Ct, op=ALU.mult),
                     r=[qnn, ropen], w=['t1'])
                P.op('dve', lambda e: e.tensor_tensor(out=T['t2'][0:M, 0:N], in0=ps[0:M, b2, 0:N], in1=St,
                                                      op=ALU.mult), r=[PSB(b2), ropen], w=['t2'])
                P.op('dve', lambda e: e.tensor_tensor(out=ob[0:M, 0:N], in0=T['t1'][0:M, 0:N], in1=T['t2'][0:M, 0:N],
                                                      op=ALU.add), r=['t1', 't2'], w=[obn])
                return ob, obn
            return part_b
        else:
            Rl, Ct, St, ropen = rope
            rm = int(_os.environ.get('ROPE_MODE', '3'))
            if rm == 0:
                copy_op(ob[0:M, 0:N], qn[0:M, 0:N], r=[qnn], w=[obn])
                return ob, obn
            b2 = nb()
            P.op('pe', lambda e: e.matmul(ps[0:M, b2, 0:N], lhsT=Rl, rhs=qn[0:M, 0:N], start=True, stop=True),
                 r=[qnn, 'cst'], w=[PSB(b2)])
            if rm == 1:
                copy_op(ob[0:M, 0:N], ps[0:M, b2, 0:N], r=[PSB(b2)], w=[obn])
                return ob, obn
            P.op('dve', lambda e: e.tensor_tensor(out=T['t1'][0:M, 0:N], in0=qn[0:M, 0:N], in1=Ct, op=ALU.mult),
                 r=[qnn, ropen], w=['t1'])
            P.op('dve', lambda e: e.tensor_tensor(out=T['t2'][0:M, 0:N], in0=ps[0:M, b2, 0:N], in1=St, op=ALU.mult),
                 r=[PSB(b2), ropen], w=['t2'])
            P.op('dve', lambda e: e.tensor_tensor(out=ob[0:M, 0:N], in0=T['t1'][0:M, 0:N], in1=T['t2'][0:M, 0:N],
                                                  op=ALU.add), r=['t1', 't2'], w=[obn])
        return ob, obn

    def transpose_out(T, src_fn, srcn, ncol, dst_fn, N):
        for tt in range(N // 128):
            bank = nb()
            off = 0
            for i in range(ncol):
                a, M = src_fn(i, tt)
                P.op('pe', lambda e, a=a, M=M, off=off: e.transpose(out=ps[:, bank, off:off + M], in_=a,
                                                                    identity=identf[0:M, 0:M]),
                     r=[srcn, 'cst'], w=[PSB(bank)])
                off += M
            sl = T['sti'] % 2
            T['sti'] += 1
            copy_op(T['st'][sl][:, 0:off], ps[:, bank, 0:off], r=[PSB(bank)], w=['st%d' % sl])
            P.dma('pool', dst_fn(tt), T['st'][sl][:, 0:off], r=['st%d' % sl], w=['stateout'])

    def mla_kv(T, l, ckvT, ckvn, kpeb, kpen, N, key0, rope, ropen=None):
        for h in range(4):
            bank = nb()
            P.op('pe', lambda e, h=h: e.matmul(ps[0:96, bank, 0:N], lhsT=T['wukvK'][:, h, :], rhs=ckvT,
                                               start=True, stop=False), r=['wukv', ckvn], w=[PSB(bank)])
            P.op('pe', lambda e: e.matmul(ps[0:96, bank, 0:N], lhsT=shiftb[0:32, 0:96], rhs=kpeb,
                                          start=False, stop=True), r=['cstb', kpen], w=[PSB(bank)])
            rp = None
            if rope is not None:
                rp = (Rm[0:96, 0:96], rope[0], rope[1], ropen)
            ob, obn = head_norm(T, ps[0:96, bank, 0:N], PSB(bank), 96, N, onesb[0:96, 0:96], 96.0, gv(l, 5, 96),
                                rope=rp)
            P.dma('pool', S['KTm'][0:96, h, key0:key0 + N], ob[0:96, 0:N], r=[obn], w=['KTm'])
        for tt in range(N // 128):
            bank = nb()
            P.op('pe', lambda e, tt=tt: e.matmul(ps[:, bank, :], lhsT=ckvT[:, tt * 128:(tt + 1) * 128],
                                                 rhs=T['wukvV'][:], start=True, stop=True),
                 r=['wukv', ckvn], w=[PSB(bank)])
            oi = T['obi'] % 4
            T['obi'] += 1
            copy_op(T['ob'][oi][:], ps[:, bank, :], r=[PSB(bank)], w=['ob%d' % oi])
            P.dma('pool', S['Vm'][(key0 + tt * 128) // 128], T['ob'][oi][:], r=['ob%d' % oi], w=['Vm'])

    def load_small_weights(T, l, st):
        T['wuq'] = salloc('wuq', (128, 3, 384), BF16, st)
        T['wukvK'] = salloc('wukvK', (128, 4, 96), BF16, st)
        T['wukvV'] = salloc('wukvV', (128, 512), BF16, st)
        T['wkpe'] = salloc('wkpe', (128, FC, 32), BF16, st)
        with ExitStack() as s2:
            a = salloc('swA', (128, 3, 384), F32, s2)
            b = salloc('swB', (128, 768), F32, s2)
            cc = salloc('swC', (128, FC, 32), F32, s2)
            P.dma('sp', a[:], I['mla_w_uq'][l].rearrange("(c p) n -> p c n", p=128), w=['swA'])
            P.dma('sp', b[:], I['mla_w_ukv'][l], w=['swB'])
            P.dma('sp', cc[:], I['w_in'][l].rearrange("(c p) n -> p c n", p=128)[:, :, 2048:2080], w=['swC'])
            P.op('dve', lambda e: e.tensor_copy(out=T['wuq'][:], in_=a[:]), r=['swA'], w=['wuq'])
            P.op('dve', lambda e: e.memset(T['wukvK'][:], 0.0), w=['wukv'])
            bv = b[:].rearrange("p (h x) -> p h x", x=192)
            P.op('dve', lambda e: e.tensor_copy(out=T['wukvK'][:, :, 0:64], in_=bv[:, :, 0:64]),
                 r=['swB'], w=['wukv'])
            P.op('dve', lambda e: e.tensor_copy(out=T['wukvV'][:].rearrange("p (h x) -> p h x", x=128),
                                                in_=bv[:, :, 64:192]), r=['swB'], w=['wukv'])
            P.op('dve', lambda e: e.tensor_copy(out=T['wkpe'][:], in_=cc[:]), r=['swC'], w=['wkpe'])
            P.barrier()
            P.flush()

    def phase_A(l, chunks=range(NCH)):
        with ExitStack() as st:
            T = alloc_common(st)
            load_small_weights(T, l, st)
            xT = salloc('xT', (128, FC, 512), F32, st)
            hT = salloc('hT', (128, FC, 512), BF16, st)
            wb = [salloc('wb%d' % i, (128, FC, 512), BF16, st) for i in range(3)]
            T['st'] = [salloc('st%d' % i, (128, 512), F32, st) for i in range(2)]
            T['sti'] = 0
            ropeT = salloc('ropeT', (128, 4, 512), F32, st)
            kst = salloc('kst', (128, 4, 512), F32, st)
            cqn = salloc('cqn', (128, 3, 512), BF16, st)
            ckvb = salloc('ckvb', (128, 512), BF16, st)
            ckvf = salloc('ckvf', (128, 512), F32, st)
            kpef = salloc('kpef', (32, 512), F32, st)
            kpeb = salloc('kpeb', (32, 512), BF16, st)
            puf = [salloc('puf%d' % i, (128, 512), F32, st) for i in range(2)]
            xsrc = 'xTa' if l % 2 == 0 else 'xTb'
            ctok = [salloc('ctok%d' % i, (128, 512), F32, st) for i in range(2)]
            ci = 0
            for (src, KT, Vn_) in ((I['cdk'], 'KTd', None), (I['cnk'], 'KTn', None)):
                for tt in range(2):
                    sl = ci % 2
                    ci += 1
                    P.dma('sp', ctok[sl][:], src[l, tt * 128:(tt + 1) * 128, :], w=['ctok%d' % sl])
                    for h in range(4):
                        bank = nb()
                        P.op('pe', lambda e, sl=sl, h=h, bank=bank: e.transpose(
                            out=ps[:, bank, 0:128], in_=ctok[sl][:, h * 128:(h + 1) * 128], identity=identf),
                            r=['ctok%d' % sl, 'cst'], w=[PSB(bank)])
                        oi = T['obi'] % 4
                        T['obi'] += 1
                        copy_op(T['ob'][oi][:, 0:128], ps[:, bank, 0:128], r=[PSB(bank)], w=['ob%d' % oi])
                        P.dma('pool', S[KT][:, h, 512 + tt * 128:512 + (tt + 1) * 128], T['ob'][oi][:, 0:128],
                              r=['ob%d' % oi], w=[KT])
            for (src, Vn_) in ((I['cdv'], 'Vd'), (I['cnv'], 'Vn')):
                for tt in range(2):
                    sl = ci % 2
                    ci += 1
                    P.dma('sp', ctok[sl][:], src[l, tt * 128:(tt + 1) * 128, :], w=['ctok%d' % sl])
                    oi = T['obi'] % 4
                    T['obi'] += 1
                    copy_op(T['ob'][oi][:], ctok[sl][:], r=['ctok%d' % sl], w=['ob%d' % oi])
                    P.dma('pool', S[Vn_][4 + tt], T['ob'][oi][:], r=['ob%d' % oi], w=[Vn_])
            for tt in range(2):
                sl = ci % 2
                ci += 1
                P.dma('sp', ctok[sl][:, 0:128], I['cckv'][l, tt * 128:(tt + 1) * 128, :], w=['ctok%d' % sl])
                P.dma('sp', ctok[sl][:, 128:160], I['ckpe'][l, tt * 128:(tt + 1) * 128, :], w=['ctok%d' % sl])
                bank = nb()
                P.op('pe', lambda e, sl=sl, bank=bank: e.transpose(out=ps[:, bank, 0:128], in_=ctok[sl][:, 0:128],
                                                                   identity=identf),
                     r=['ctok%d' % sl, 'cst'], w=[PSB(bank)])
                copy_op(ckvb[:, tt * 128:(tt + 1) * 128], ps[:, bank, 0:128], r=[PSB(bank)], w=['ckvb'])
                bank = nb()
                P.op('pe', lambda e, sl=sl, bank=bank: e.transpose(out=ps[0:32, bank, 0:128],
                                                                   in_=ctok[sl][:, 128:160], identity=identf),
                     r=['ctok%d' % sl, 'cst'], w=[PSB(bank)])
                copy_op(kpeb[:, tt * 128:(tt + 1) * 128], ps[0:32, bank, 0:128], r=[PSB(bank)], w=['kpeb'])
            mla_kv(T, l, ckvb[:, 0:256], 'ckvb', kpeb[:, 0:256], 'kpeb', 256, 512, None)

            stream = [(c, g) for c in chunks for g in range(8)]
            if _os.environ.get('A_PARTS') == '1':
                stream = []
            if _os.environ.get('A_GROUPS'):
                stream = [(c, g) for c in chunks for g in range(8) if str(g) in _os.environ['A_GROUPS']]
            loaded = {}

            def wload(i):
                c, g = stream[i]
                sl = i % 3
                P.dma('sp', wb[sl][:], S['WinG'][l, g], r=['Wscr'], w=['wb%d' % sl])
                loaded[i] = sl
            for i in range(min(2, len(stream))):
                wload(i)
            for i, (c, g) in enumerate(stream):
                if i + 2 < len(stream):
                    wload(i + 2)
                sl = loaded[i]
                W = wb[sl]
                Wn = 'wb%d' % sl
                v, segs = chunk_info(c)
                tok0 = c * 512
                key0 = tok0 + (256 if c > 0 else 0)
                is_s = c > 0
                if i == 0 or stream[i - 1][0] != c:
                    P.dma('sp', xT[:], S[xsrc][:, :, tok0:tok0 + 512].rearrange("c p t -> p c t"),
                          r=[xsrc], w=['xT'])
                    if is_s:
                        t0l = (c - 1) * 512
                        P.dma('sp', ropeT[:], I['rope'][:, :, t0l:t0l + 512].rearrange("k p t -> p k t"),
                              w=['ropeT'])
                    make_hT(T, xT, 'xT', hT, 'hT', l, v, 1)

                def fm_tile(j, M=128, W=W, Wn=Wn):
                    bank = nb()
                    for fc in range(FC):
                        P.op('pe', lambda e, fc=fc: e.matmul(ps[0:M, bank, :], lhsT=W[:, fc, j * 128:j * 128 + M],
                                                             rhs=hT[:, fc, :], start=(fc == 0), stop=(fc == FC - 1)),
                             r=[Wn, 'hT'], w=[PSB(bank)])
                    return bank

                def tm_group(Vn_, onm, W=W, Wn=Wn):
                    for tt in range(4):
                        bank = nb()
                        for fc in range(FC):
                            P.op('pe', lambda e, fc=fc, tt=tt: e.matmul(
                                ps[:, bank, :], lhsT=hT[:, fc, tt * 128:(tt + 1) * 128], rhs=W[:, fc, :],
                                start=(fc == 0), stop=(fc == FC - 1)), r=[Wn, 'hT'], w=[PSB(bank)])
                        oi = T['obi'] % 4
                        T['obi'] += 1
                        copy_op(T['ob'][oi][:], ps[:, bank, :], r=[PSB(bank)], w=['ob%d' % oi], eng='dve')
                        P.dma('pool', S[Vn_][(key0 + tt * 128) // 128], T['ob'][oi][:], r=['ob%d' % oi], w=[Vn_])
                        if (not is_s) and _os.environ.get('NO_VOUT') != '1':
                            sl2 = T['sti'] % 2
                            T['sti'] += 1
                            copy_op(T['st'][sl2][:], ps[:, bank, :], r=[PSB(bank)], w=['st%d' % sl2], eng='dve')
                            b_, t_ = tt // 2, (tt % 2) * 128
                            if _os.environ.get('VOUT_MODE') == 'scratch':
                                P.dma('pool', S['puT'][0, :, 0:512], T['st'][sl2][:], r=['st%d' % sl2], w=['stateout'])
                            elif _os.environ.get('VOUT_MODE') != 'copyonly':
                                P.dma('pool', O[onm][(b_ * L + l) * 256 + t_:(b_ * L + l) * 256 + t_ + 128, :], T['st'][sl2][:], r=['st%d' % sl2],
                                      w=['stateout'])

                if g in (0, 1, 4, 5):
                    isq = g in (0, 4)
                    isd = g in (0, 1)
                    gcol = gv(l, {0: 0, 1: 1, 4: 6, 5: 7}[g])
                    dst = {0: 'QTd', 1: 'KTd', 4: 'QTn', 5: 'KTn'}[g]
                    gbanks = [fm_tile(j) for j in range(4)]
                    res_ = []
                    for j in range(4):
                        bank = gbanks[j]
                        rp = None
                        if isd and is_s:
                            rp = (Rd, ropeT[:, 0, :], ropeT[:, 1, :], 'ropeT')
                        f32o = None
                        if (not is_s) and (not isq):
                            f32o = kst[:, j, :]
                        res_.append(head_norm(T, ps[:, bank, :], PSB(bank), 128, 512, blk64b, 64.0, gcol, rope=rp,
                                              f32_out=f32o, f32n='kst', defer=True))
                    for j in range(4):
                        rj = res_[j]
                        ob, obn = rj() if callable(rj) else rj
                        if isq:
                            P.dma('pool', S[dst][:, j, tok0:tok0 + 512], ob[:], r=[obn], w=[dst])
                        else:
                            P.dma('pool', S[dst][:, j, key0:key0 + 512], ob[:], r=[obn], w=[dst])
                    if (not is_s) and (not isq):
                        onm = 'o_dk' if isd else 'o_nk'
                        transpose_out(T, lambda i, tt: (kst[:, i, tt * 128:(tt + 1) * 128], 128), 'kst', 4,
                                      lambda tt: O[onm][((tt // 2) * L + l) * 256 + (tt % 2) * 128:((tt // 2) * L + l) * 256 + (tt % 2) * 128 + 128, :], 512)
                elif g == 2:
                    tm_group('Vd', 'o_dv')
                elif g == 6:
                    tm_group('Vn', 'o_nv')
                elif g == 7:
                    for j in range(4):
                        bank = fm_tile(j)
                        sl2 = j % 2
                        copy_op(puf[sl2][:], ps[:, bank, :], r=[PSB(bank)], w=['puf%d' % sl2])
                        P.dma('pool', S['puT'][j, :, tok0:tok0 + 512], puf[sl2][:], r=['puf%d' % sl2], w=['puT'])
                elif g == 3:
                    banks = [fm_tile(j) for j in range(3)]
                    sb_ = nb()
                    for j in range(3):
                        sl2 = j % 2
                        P.op('act', lambda e, j=j, sl2=sl2: e.activation(out=T['sq'][sl2][:], in_=ps[:, banks[j], :],
                                                                         func=AF.Square),
                             r=[PSB(banks[j])], w=['sq%d' % sl2])
                        P.op('pe', lambda e, j=j, sl2=sl2: e.matmul(ps[:, sb_, :], lhsT=onesb, rhs=T['sq'][sl2][:],
                                                                    start=(j == 0), stop=(j == 2)),
                             r=['sq%d' % sl2, 'cstb'], w=[PSB(sb_)])
                    rstd_from_ps(sb_, 128, 512, 384.0, T['rt'], T['rs'], 'rt', 'rs')
                    for j in range(3):
                        P.op('dve', lambda e, j=j: e.scalar_tensor_tensor(
                            out=cqn[:, j, :], in0=ps[:, banks[j], :], scalar=GV[:, l * 16 + 8 + j:l * 16 + 9 + j],
                            in1=T['rs'][:], op0=ALU.mult, op1=ALU.mult), r=[PSB(banks[j]), 'rs', 'GV'], w=['cqn'])
                    for h in range(4):
                        bank = nb()
                        for j in range(3):
                            P.op('pe', lambda e, h=h, j=j: e.matmul(ps[0:96, bank, :],
                                                                    lhsT=T['wuq'][:, j, h * 96:(h + 1) * 96],
                                                                    rhs=cqn[:, j, :], start=(j == 0), stop=(j == 2)),
                                 r=['wuq', 'cqn'], w=[PSB(bank)])
                        rp = (Rm[0:96, 0:96], ropeT[0:96, 2, :], ropeT[0:96, 3, :], 'ropeT') if is_s else None
                        ob, obn = head_norm(T, ps[0:96, bank, :], PSB(bank), 96, 512, onesb[0:96, 0:96], 96.0,
                                            gv(l, 4, 96), rope=rp)
                        P.dma('pool', S['QTm'][0:96, h, tok0:tok0 + 512], ob[0:96, :], r=[obn], w=['QTm'])
                    bank = fm_tile(3)
                    sl2 = T['qni'] % 2
                    T['qni'] += 1
                    P.op('act', lambda e: e.activation(out=T['sq'][sl2][:], in_=ps[:, bank, :], func=AF.Square),
                         r=[PSB(bank)], w=['sq%d' % sl2])
                    b2 = nb()
                    P.op('pe', lambda e: e.matmul(ps[:, b2, :], lhsT=onesb, rhs=T['sq'][sl2][:], start=True, stop=True),
                         r=['sq%d' % sl2, 'cstb'], w=[PSB(b2)])
                    rstd_from_ps(b2, 128, 512, 128.0, T['rt'], T['rs'], 'rt', 'rs')
                    P.op('dve', lambda e: e.scalar_tensor_tensor(out=ckvf[:], in0=ps[:, bank, :], scalar=gv(l, 3),
                                                                 in1=T['rs'][:], op0=ALU.mult, op1=ALU.mult),
                         r=[PSB(bank), 'rs', 'GV'], w=['ckvf'])
                    copy_op(ckvb[:], ckvf[:], r=['ckvf'], w=['ckvb'])
                    bank = nb()
                    for fc in range(FC):
                        P.op('pe', lambda e, fc=fc: e.matmul(ps[0:32, bank, :], lhsT=T['wkpe'][:, fc, :],
                                                             rhs=hT[:, fc, :], start=(fc == 0), stop=(fc == FC - 1)),
                             r=['wkpe', 'hT'], w=[PSB(bank)])
                    copy_op(kpef[:], ps[0:32, bank, :], r=[PSB(bank)], w=['kpef'], eng='dve')
                    copy_op(kpeb[:], ps[0:32, bank, :], r=[PSB(bank)], w=['kpeb'], eng='dve')
                    if not is_s:
                        transpose_out(T, lambda i, tt: (ckvf[:, tt * 128:(tt + 1) * 128], 128), 'ckvf', 1,
                                      lambda tt: O['o_ckv'][((tt // 2) * L + l) * 256 + (tt % 2) * 128:((tt // 2) * L + l) * 256 + (tt % 2) * 128 + 128, :], 512)
                        transpose_out(T, lambda i, tt: (kpef[:, tt * 128:(tt + 1) * 128], 32), 'kpef', 1,
                                      lambda tt: O['o_kpe'][((tt // 2) * L + l) * 256 + (tt % 2) * 128:((tt // 2) * L + l) * 256 + (tt % 2) * 128 + 128, :], 512)
                    mla_kv(T, l, ckvb[:], 'ckvb', kpeb[:], 'kpeb', 512, key0,
                           (ropeT[0:96, 2, :], ropeT[0:96, 3, :]) if is_s else None, 'ropeT')
            P.barrier()
            P.flush()

    def phase_E(l):
        with ExitStack() as st:
            zt = salloc('zt', (15, 127), F32, st)
            Hk = [salloc('Hk%d' % i, (64, 15, 64), F32, st) for i in range(2)]
            ETr = salloc('ETr', (128, 8, 31 * 64), F32, st)
            mk = [salloc('mk%d' % i, (128, 512), F32, st) for i in range(2)]
            eb = [salloc('eb%d' % i, (128, 512), BF16, st) for i in range(2)]
            Jx = cst[0:64, 768:832]
            P.op('dve', lambda e: e.memset(zt[:], 0.0), w=['zt'])
            P.op('dve', lambda e: e.memset(ETr[:], 0.0), w=['ETr'])
            for h in range(8):
                P.dma('sp', S['biasP'][h], zt[:], r=['zt'], w=['biasP'])
                P.dma('sp', S['biasP'][h, :, 48:79], I['na_bias'][l, h], r=[], w=['biasP'])
            bp = S['biasP']
            for h in range(8):
                sl = h % 2
                src = bass.AP(tensor=bp.tensor, offset=bp.offset + h * 15 * 127, ap=[[1, 64], [127, 15], [1, 64]])
                P.dma('sp', Hk[sl][:], src, r=['biasP'], w=['Hk%d' % sl])
                b0 = 1 + 2 * sl
                for half in range(2):
                    for i2 in range(8 + half, 23 + half):
                        dr = 15 - i2 + half
                        r_ = dr + 7
                        bank = b0 + (i2 - 8) // 8
                        col = ((i2 - 8) % 8) * 64
                        P.op('pe', lambda e, half=half, r_=r_, bank=bank, col=col, sl=sl: e.matmul(
                            ps[half * 64:(half + 1) * 64, bank, col:col + 64], lhsT=Hk[sl][:, r_, :], rhs=Jx,
                            start=True, stop=True), r=['Hk%d' % sl, 'cst'], w=[PSB(bank)])
                for half in range(2):
                    p0 = half * 64
                    lo1, hi1 = 8 + half, 16
                    lo2, hi2 = 16, 23 + half
                    P.op('act', lambda e, p0=p0, lo1=lo1, hi1=hi1, b0=b0, h=h: e.activation(
                        out=ETr[p0:p0 + 64, h, lo1 * 64:hi1 * 64], in_=ps[p0:p0 + 64, b0, (lo1 - 8) * 64:(hi1 - 8) * 64],
                        func=AF.Exp), r=[PSB(b0)], w=['ETr'])
                    P.op('act', lambda e, p0=p0, lo2=lo2, hi2=hi2, b0=b0, h=h: e.activation(
                        out=ETr[p0:p0 + 64, h, lo2 * 64:hi2 * 64],
                        in_=ps[p0:p0 + 64, b0 + 1, (lo2 - 16) * 64:(hi2 - 16) * 64],
                        func=AF.Exp), r=[PSB(b0 + 1)], w=['ETr'])
            k = 0
            for ty, (r0, kr0) in enumerate(NA_TYPES):
                off = kr0 - r0
                for ch in range(8):
                    ms = (ty * 8 + ch) % 2
                    P.dma('sp', mk[ms][:], I['nam'][ty, ch], w=['mk%d' % ms])
                    i0 = 15 - (2 * ch + off)
                    for h in range(8):
                        sl = k % 2
                        k += 1
                        eng = 'dve'
                        P.op(eng, lambda e, sl=sl, ms=ms, h=h, i0=i0: e.tensor_tensor(
                            out=eb[sl][:], in0=ETr[:, h, i0 * 64:(i0 + 8) * 64], in1=mk[ms][:], op=ALU.mult),
                            r=['ETr', 'mk%d' % ms], w=['eb%d' % sl])
                        P.dma('pool', S['Emask'][ty, h, :, ch, :], eb[sl][:], r=['eb%d' % sl], w=['Emask'])
            P.barrier()

    def phase_B(l, seqs=(0, 1, 2)):
        with ExitStack() as st:
            T = alloc_common(st)
            KT = salloc('KT', (128, 4, PAST + NS), BF16, st)
            Vt = salloc('Vt', (128, (PAST + NS) // 128, 512), BF16, st)
            QT = [salloc('QT%d' % i, (128, 4, 512), BF16, st) for i in range(2)]
            pb = [salloc('pb%d' % i, (128, 512), BF16, st) for i in range(6)]
            rinv = [salloc('rinv%d' % i, (128, 512), F32, st) for i in range(2)]
            of = [salloc('of%d' % i, (128, 512), F32, st) for i in range(2)]
            Et = [salloc('Et%d' % i, (128, 8, 512), BF16, st) for i in range(2)]
            af = [salloc('af%d' % i, (128, 512), F32, st) for i in range(2)]
            onesf = cst[:, 128:256]
            rr['pool'] = [3]
            cn = {'s': 0, 'p': 0, 'q': 0, 'e': 0}

            LA = 3
            pend = []

            def emit_S(u):
                if u.get('pre') is not None:
                    u['pre']()
                cn['s'] += 1
                bs = cn['s'] % 4
                nq = u['nq']
                P.op('pe', lambda e: e.matmul(ps[:, bs, 0:nq], lhsT=u['kt'], rhs=u['q'], start=True, stop=True),
                     r=['KT', u['qn']], w=[PSB(bs)])
                cn['p'] += 1
                pi = cn['p'] % 6
                u['pi'] = pi
                P.op('act', lambda e: e.activation(out=pb[pi][:, 0:nq], in_=ps[:, bs, 0:nq], func=AF.Exp,
                                                   scale=u['scale']), r=[PSB(bs)], w=['pb%d' % pi])
                if u.get('emul') is not None:
                    eap, en = u['emul']
                    P.op('dve', lambda e: e.tensor_tensor(out=pb[pi][:, 0:nq], in0=pb[pi][:, 0:nq], in1=eap,
                                                          op=ALU.mult), r=['pb%d' % pi, en], w=['pb%d' % pi])

            def emit_PV(u):
                pi = u['pi']
                nq = u['nq']
                P.op('pe', lambda e: e.matmul(u['o'], lhsT=u['v'], rhs=pb[pi][:, 0:nq], start=u['first'],
                                              stop=u['last']), r=['Vt', 'pb%d' % pi], w=[PSB(u['ob'])])
                ab = u['accb']
                if u['first']:
                    P.op('dve', lambda e: e.tensor_copy(out=ps[:, ab, 0:nq], in_=pb[pi][:, 0:nq]),
                         r=['pb%d' % pi], w=[PSB(ab)])
                else:
                    P.op('dve', lambda e: e.tensor_tensor(out=ps[:, ab, 0:nq], in0=ps[:, ab, 0:nq],
                                                          in1=pb[pi][:, 0:nq], op=ALU.add),
                         r=['pb%d' % pi, PSB(ab)], w=[PSB(ab)])
                if u.get('post') is not None:
                    u['post']()

            def push(u):
                emit_S(u)
                pend.append(u)
                if len(pend) > LA:
                    emit_PV(pend.pop(0))

            def drain():
                while pend:
                    emit_PV(pend.pop(0))

            def mk(kt, q, v, nq, scale, o, s_, ones_, first, last, ob_, sb_, qn, emul=None, accb=None):
                return dict(kt=kt, q=q, v=v, nq=nq, scale=scale, o=o, s=s_, ones=ones_, first=first, last=last,
                            ob=ob_, sb=sb_, qn=qn, emul=emul, pre=None, post=None,
                            accb=(sb_ if accb is None else accb))

            def fin_sum(accb, nq, outs):
                fi = cn.get('af', 0) % 2
                cn['af'] = cn.get('af', 0) + 1
                P.op('dve', lambda e: e.tensor_copy(out=af[fi][:, 0:nq], in_=ps[:, accb, 0:nq]),
                     r=[PSB(accb)], w=['af%d' % fi])
                for (o_ap, l_ap, bk) in outs:
                    P.op('pe', lambda e, o_ap=o_ap, l_ap=l_ap: e.matmul(o_ap, lhsT=l_ap, rhs=af[fi][:, 0:nq],
                                                                        start=True, stop=True),
                         r=['af%d' % fi, 'cst'], w=[PSB(bk)])

            for m in _os.environ.get('B_MIX', 'dmn'):
                for s_ in seqs:
                    n = SEQ_N[s_]
                    nk = SEQ_CTX[s_] + n
                    nkc = nk // 128
                    kb = KB[s_]
                    Mk = 96 if m == 'm' else 128
                    drain()
                    P.dma('sp', KT[0:Mk, :, 0:nk], S['KT' + m][0:Mk, :, kb:kb + nk], r=['KT' + m], w=['KT'])
                    P.dma('sp', Vt[:, 0:nkc, :], S['V' + m][kb // 128:kb // 128 + nkc].rearrange("c p e -> p c e"),
                          r=['V' + m], w=['Vt'])
                    nq = min(512, n)
                    for qb in range(n // nq):
                        tok0 = SEQ_T0[s_] + qb * nq
                        cn['q'] += 1
                        qs = cn['q'] % 2
                        Q = QT[qs]
                        qnm = 'QT%d' % qs

                        def load_q(Q=Q, qs=qs, tok0=tok0, Mk=Mk, nq=nq, m=m):
                            P.dma('sp', Q[0:Mk, :, 0:nq], S['QT' + m][0:Mk, :, tok0:tok0 + nq], r=['QT' + m],
                                  w=['QT%d' % qs])
                        first_unit = True
                        if m == 'd':
                            sc = 64.0 ** -0.5
                            for h in range(4):
                                us = []
                                for kc in range(nkc):
                                    for j in range(2):
                                        us.append(mk(KT[j * 64:(j + 1) * 64, h, kc * 128:(kc + 1) * 128],
                                                     Q[j * 64:(j + 1) * 64, h, 0:nq], Vt[:, kc, h * 128:(h + 1) * 128],
                                                     nq, sc, ps[:, 4 + j, 0:nq], ps[:, 6 + j, 0:nq], onesb, kc == 0,
                                                     kc == nkc - 1, 4 + j, 6 + j, qnm))

                                def epi(h=h, tok0=tok0, nq=nq):
                                    for j in range(2):
                                        fin_sum(6 + j, nq, [(ps[:, 6 + j, 0:nq], onesf, 6 + j)])
                                    for j in range(2):
                                        P.op('dve', lambda e, j=j: e.reciprocal(out=rinv[j][:, 0:nq],
                                                                                in_=ps[:, 6 + j, 0:nq]),
                                             r=[PSB(6 + j)], w=['rinv%d' % j])
                                        P.op('dve', lambda e, j=j: e.tensor_tensor(
                                            out=of[j][:, 0:nq], in0=ps[:, 4 + j, 0:nq], in1=rinv[j][:, 0:nq],
                                            op=ALU.mult), r=[PSB(4 + j), 'rinv%d' % j], w=['of%d' % j])
                                    P.op('dve', lambda e: e.scalar_tensor_tensor(
                                        out=of[0][:, 0:nq], in0=of[1][:, 0:nq], scalar=gv(l, 15), in1=of[0][:, 0:nq],
                                        op0=ALU.mult, op1=ALU.add), r=['of0', 'of1', 'GV'], w=['of0'])
                                    ob, obn = head_norm(T, of[0][:, 0:nq], 'of0', 128, nq, onesb, 128.0, gv(l, 2))
                                    P.dma('pool', S['mixT'][h, :, tok0:tok0 + nq], ob[:, 0:nq], r=[obn], w=['mixT'])
                                us[-1]['post'] = epi
                                if first_unit:
                                    us[0]['pre'] = load_q
                                    first_unit = False
                                for u in us:
                                    push(u)
                        elif m == 'm':
                            sc = 96.0 ** -0.5
                            for h in range(4):
                                ob_, sb_ = 4 + h % 2, 6 + h % 2
                                us = []
                                for kc in range(nkc):
                                    us.append(mk(KT[0:96, h, kc * 128:(kc + 1) * 128], Q[0:96, h, 0:nq],
                                                 Vt[:, kc, h * 128:(h + 1) * 128], nq, sc, ps[:, ob_, 0:nq],
                                                 ps[:, sb_, 0:nq], onesb, kc == 0, kc == nkc - 1, ob_, sb_, qnm))

                                def epi(h=h, tok0=tok0, nq=nq, ob_=ob_, sb_=sb_):
                                    fin_sum(sb_, nq, [(ps[:, sb_, 0:nq], onesf, sb_)])
                                    P.op('dve', lambda e: e.reciprocal(out=rinv[0][:, 0:nq], in_=ps[:, sb_, 0:nq]),
                                         r=[PSB(sb_)], w=['rinv0'])
                                    oi = T['obi'] % 4
                                    T['obi'] += 1
                                    P.op('dve', lambda e: e.tensor_tensor(out=T['ob'][oi][:, 0:nq], in0=ps[:, ob_, 0:nq],
                                                                          in1=rinv[0][:, 0:nq], op=ALU.mult),
                                         r=[PSB(ob_), 'rinv0'], w=['ob%d' % oi])
                                    P.dma('pool', S['mixT'][4 + h, :, tok0:tok0 + nq], T['ob'][oi][:, 0:nq],
                                          r=['ob%d' % oi], w=['mixT'])
                                us[-1]['post'] = epi
                                if first_unit:
                                    us[0]['pre'] = load_q
                                    first_unit = False
                                for u in us:
                                    push(u)
                        else:
                            sc = 64.0 ** -0.5
                            if s_ == 2:
                                r0, kr0, ty = na_block_info(qb)
                                kcs = [(0, None), (1, None)] + [(2 + kr0 // 2 + ch, ch) for ch in range(8)]
                            else:
                                ty = 0
                                kcs = [(kc, None) for kc in range(nkc)]
                            for i in range(4):
                                ob_, sb_ = 4 + i % 2, 6
                                for hh in range(2):
                                    h = 2 * i + hh
                                    po = hh * 64
                                    us = []
                                    es_ = None
                                    if s_ == 2:
                                        cn['e'] += 1
                                        es_ = cn['e'] % 2
                                    for ki, (kc, ch) in enumerate(kcs):
                                        em = None
                                        if ch is not None:
                                            em = (Et[es_][:, ch, 0:nq], 'Et%d' % es_)
                                        us.append(mk(KT[po:po + 64, i, kc * 128:(kc + 1) * 128], Q[po:po + 64, i, 0:nq],
                                                     Vt[:, kc, h * 64:(h + 1) * 64], nq, sc, ps[po:po + 64, ob_, 0:nq],
                                                     ps[po:po + 64, sb_, 0:nq], onesb[:, 0:64], ki == 0,
                                                     ki == len(kcs) - 1, ob_, sb_, qnm, emul=em, accb=6 + hh))
                                    pres = []
                                    if first_unit:
                                        pres.append(load_q)
                                        first_unit = False
                                    if s_ == 2:
                                        def load_e(es_=es_, ty=ty, h=h):
                                            P.dma('sp', Et[es_][:], S['Emask'][ty, h], r=['Emask'], w=['Et%d' % es_])
                                        pres.append(load_e)
                                    if pres:
                                        us[0]['pre'] = (lambda pres=pres: [f() for f in pres])
                                    if hh == 1:
                                        def epi(i=i, tok0=tok0, nq=nq, ob_=ob_, sb_=sb_):
                                            fin_sum(6, nq, [(ps[0:64, 3, 0:nq], onesf[:, 0:64], 3)])
                                            fin_sum(7, nq, [(ps[64:128, 3, 0:nq], onesf[:, 0:64], 3)])
                                            sb_ = 3
                                            P.op('dve', lambda e: e.reciprocal(out=rinv[0][:, 0:nq],
                                                                               in_=ps[:, sb_, 0:nq]),
                                                 r=[PSB(sb_)], w=['rinv0'])
                                            oi = T['obi'] % 4
                                            T['obi'] += 1
                                            P.op('dve', lambda e: e.tensor_tensor(
                                                out=T['ob'][oi][:, 0:nq], in0=ps[:, ob_, 0:nq], in1=rinv[0][:, 0:nq],
                                                op=ALU.mult), r=[PSB(ob_), 'rinv0'], w=['ob%d' % oi])
                                            P.dma('pool', S['mixT'][8 + i, :, tok0:tok0 + nq], T['ob'][oi][:, 0:nq],
                                                  r=['ob%d' % oi], w=['mixT'])
                                        us[-1]['post'] = epi
                                    for u in us:
                                        push(u)
            drain()
            rr['pool'] = list(range(8))
            P.barrier()

    def phase_Ap(l, chunks=range(NCH)):
        with ExitStack() as st:
            pus = [salloc('pus%d' % i, (128, 528), F32, st) for i in range(2)]
            A = salloc('pA', (128, 528), F32, st)
            B = salloc('pB', (128, 528), F32, st)
            ivc = [salloc('ivc%d' % i, (128, 512), F32, st) for i in range(2)]
            db = [salloc('pdb%d' % i, (128, 512), BF16, st) for i in range(2)]
            yb = [salloc('pyb%d' % i, (128, 512), BF16, st) for i in range(2)]
            pwf = salloc('pwf', (128, 4, 128), F32, st)
            pw = salloc('pw', (128, 4, 128), BF16, st)
            P.dma('sp', pwf[:], I['pool_w'][l].rearrange("g c e -> c g e"), w=['pwf'])
            P.op('dve', lambda e: e.tensor_copy(out=pw[:], in_=pwf[:]), r=['pwf'], w=['pw'])
            k = 0
            for c in chunks:
                v, segs = chunk_info(c)
                for (s_, col0, n, tl0) in segs:
                    tok0 = c * 512 + col0
                    hasl = tl0 > 0
                    hasr = tl0 + n < SEQ_N[s_]
                    for g, w_ in enumerate((2, 4, 8, 16)):
                        sl = k % 2
                        k += 1
                        u = pus[sl]
                        un = 'pus%d' % sl
                        lo = 8 if not hasl else 0
                        hi = n + 8 if not hasr else n + 16
                        if not hasl:
                            P.op('dve', lambda e, u=u: e.memset(u[:, 0:8], 0.0), w=[un])
                        if not hasr:
                            P.op('dve', lambda e, u=u: e.memset(u[:, n + 8:n + 16], 0.0), w=[un])
                        P.dma('sp', u[:, lo:hi], S['puT'][g, :, tok0 - 8 + lo:tok0 - 8 + hi], r=['puT'], w=[un])
                        P.dma('sp', ivc[sl][:, 0:n], I['invc'][:, g, tok0:tok0 + n], w=['ivc%d' % sl])
                        P.op('dve', lambda e, u=u: e.tensor_tensor(out=A[:, 0:n + 15], in0=u[:, 0:n + 15],
                                                                   in1=u[:, 1:n + 16], op=ALU.add), r=[un], w=['pA'])
                        cur, curn, o = A, 'pA', 7
                        if w_ >= 4:
                            P.op('dve', lambda e: e.tensor_tensor(out=B[:, 0:n + 13], in0=A[:, 0:n + 13],
                                                                  in1=A[:, 2:n + 15], op=ALU.add), r=['pA'], w=['pB'])
                            cur, curn, o = B, 'pB', 6
                        if w_ >= 8:
                            P.op('dve', lambda e: e.tensor_tensor(out=A[:, 0:n + 9], in0=B[:, 0:n + 9],
                                                                  in1=B[:, 4:n + 13], op=ALU.add), r=['pB'], w=['pA'])
                            cur, curn, o = A, 'pA', 4
                        if w_ >= 16:
                            P.op('dve', lambda e: e.tensor_tensor(out=B[:, 0:n + 1], in0=A[:, 0:n + 1],
                                                                  in1=A[:, 8:n + 9], op=ALU.add), r=['pA'], w=['pB'])
                            cur, curn, o = B, 'pB', 0
                        o = 8 - w_ // 2
                        P.op('dve', lambda e, cur=cur, o=o, sl=sl: e.tensor_tensor(
                            out=cur[:, o:o + n], in0=cur[:, o:o + n], in1=ivc[sl][:, 0:n], op=ALU.mult),
                            r=[curn, 'ivc%d' % sl], w=[curn])
                        P.op('dve', lambda e, cur=cur, o=o, sl=sl, u=u: e.tensor_tensor(
                            out=db[sl][:, 0:n], in0=cur[:, o:o + n], in1=u[:, 8:8 + n], op=ALU.subtract),
                            r=[curn, un], w=['pdb%d' % sl])
                        bank = nb()
                        P.op('pe', lambda e, g=g, sl=sl, bank=bank: e.matmul(ps[:, bank, 0:n], lhsT=pw[:, g, :],
                                                                             rhs=db[sl][:, 0:n], start=True, stop=True),
                             r=['pw', 'pdb%d' % sl], w=[PSB(bank)])
                        P.op('act', lambda e, g=g, sl=sl, bank=bank: e.activation(
                            out=yb[sl][:, 0:n], in_=ps[:, bank, 0:n], func=AF.Copy, scale=gv(l, 11 + g)),
                            r=[PSB(bank), 'GV'], w=['pyb%d' % sl])
                        P.dma('pool', S['mixT'][12 + g, :, tok0:tok0 + n], yb[sl][:, 0:n], r=['pyb%d' % sl], w=['mixT'])
            P.barrier()

    def phase_C1(l, chunks=range(NCH)):
        with ExitStack() as st:
            xT = [salloc('xT%d' % i, (128, FC, 512), F32, st) for i in range(2)]
            mx = [salloc('mx%d' % i, (128, FC, 512), BF16, st) for i in range(2)]
            wb = [salloc('wb%d' % i, (128, FC, 512), BF16, st) for i in range(3)]
            xsrc = 'xTa' if l % 2 == 0 else 'xTb'
            stream = [(c, g) for c in chunks for g in range(4)]
            loaded = {}

            def wload(i):
                c, g = stream[i]
                sl = i % 3
                P.dma('sp', wb[sl][:], S['Wout'][l, g], r=['Wscr'], w=['wb%d' % sl])
                loaded[i] = sl
            for i in range(min(2, len(stream))):
                wload(i)
            for i, (c, g) in enumerate(stream):
                if i + 2 < len(stream):
                    wload(i + 2)
                sl = loaded[i]
                v, segs = chunk_info(c)
                tok0 = c * 512
                cs = (i // 4) % 2
                if g == 0:
                    P.dma('sp', xT[cs][:], S[xsrc][:, :, tok0:tok0 + 512].rearrange("c p t -> p c t"), r=[xsrc],
                          w=['xT%d' % cs])
                    P.dma('sp', mx[cs][:], S['mixT'][:, :, tok0:tok0 + 512].rearrange("c p t -> p c t"), r=['mixT'],
                          w=['mx%d' % cs])
                for j in range(4):
                    dc = g * 4 + j
                    bank = nb()
                    for fc in range(FC):
                        P.op('pe', lambda e, fc=fc, j=j, sl=sl, cs=cs, bank=bank: e.matmul(
                            ps[:, bank, :], lhsT=wb[sl][:, fc, j * 128:(j + 1) * 128], rhs=mx[cs][:, fc, :],
                            start=(fc == 0), stop=(fc == FC - 1)), r=['wb%d' % sl, 'mx%d' % cs], w=[PSB(bank)])
                    g1 = prm(l, v, 2)
                    P.op('dve', lambda e, dc=dc, cs=cs, bank=bank, g1=g1: e.scalar_tensor_tensor(
                        out=xT[cs][:, dc, :], in0=ps[:, bank, :], scalar=g1[:, dc:dc + 1], in1=xT[cs][:, dc, :],
                        op0=ALU.mult, op1=ALU.add), r=[PSB(bank), 'xT%d' % cs, 'PRM'], w=['xT%d' % cs])
                if g == 3:
                    P.dma('pool', S['x1T'][:, :, tok0:tok0 + 512].rearrange("c p t -> p c t"), xT[cs][:],
                          r=['xT%d' % cs], w=['x1T'])
            P.barrier()

    def phase_C2(l, chunks=range(NCH)):
        with ExitStack() as st:
            T = {}
            T['sq'] = [salloc('sq%d' % i, (128, 512), BF16, st) for i in range(2)]
            T['rt'] = salloc('rt', (128, 512), F32, st)
            T['rsx'] = salloc('rsx', (128, 512), F32, st)
            T['tmp'] = [salloc('tmp%d' % i, (128, 512), F32, st) for i in range(2)]
            xT = salloc('xT', (128, FC, 512), F32, st)
            xh = salloc('xh', (128, FC, 2), F32, st)
            hT = salloc('hT', (128, FC, 512), BF16, st)
            hTh = salloc('hTh', (128, FC, 2), BF16, st)
            actT = salloc('actT', (128, 44, 512), BF16, st)
            wb = [salloc('wb%d' % i, (128, FC, 512), BF16, st) for i in range(3)]
            Ab = [salloc('Ab%d' % i, (128, 512), F32, st) for i in range(2)]
            sg = [salloc('sg%d' % i, (128, 512), F32, st) for i in range(4)]
            xdst = 'xTb' if l % 2 == 0 else 'xTa'
            items = []
            for k in range(11):
                items += [('u', k, 0), ('u', k, 1)]
            for dg in range(4):
                for fq in range(4):
                    items.append(('d', dg, fq))
            stream = [(c,) + it for c in chunks for it in items]
            loaded = {}

            def wload(i):
                c, kind, a, b = stream[i]
                sl = i % 3
                if kind == 'u':
                    P.dma('sp', wb[sl][:], S['Wup'][l, a + 11 * b], r=['Wscr'], w=['wb%d' % sl])
                else:
                    P.dma('sp', wb[sl][:, 0:11, :], S['Wdn'][l, a, b], r=['Wscr'], w=['wb%d' % sl])
                loaded[i] = sl
            for i in range(min(2, len(stream))):
                wload(i)
            ai = 0
            dbanks = None
            for i, (c, kind, a, b) in enumerate(stream):
                if i + 2 < len(stream):
                    wload(i + 2)
                sl = loaded[i]
                W = wb[sl]
                Wn = 'wb%d' % sl
                v, segs = chunk_info(c)
                tok0 = c * 512
                s_ = segs[0][0]
                hasl = c > 1
                hasr = (c > 0) and (c < NCH - 1)
                if kind == 'u' and a == 0 and b == 0:
                    P.dma('sp', xT[:], S['x1T'][:, :, tok0:tok0 + 512].rearrange("c p t -> p c t"), r=['x1T'], w=['xT'])
                    make_hT(T, xT, 'xT', hT, 'hT', l, v, 2)
                    if c > 0:
                        P.op('dve', lambda e: e.memset(xh[:], 0.0), w=['xh'])
                        if hasl:
                            P.dma('sp', xh[:, :, 0:1], S['x1T'][:, :, tok0 - 1:tok0].rearrange("c p t -> p c t"),
                                  r=['x1T'], w=['xh'], slow=True)
                        if hasr:
                            P.dma('sp', xh[:, :, 1:2], S['x1T'][:, :, tok0 + 512:tok0 + 513].rearrange("c p t -> p c t"),
                                  r=['x1T'], w=['xh'], slow=True)
                        make_hT(T, xh, 'xh', hTh, 'hTh', l, v, 2, N=2)
                if kind == 'u':
                    hb = 6 + (i % 2)
                    for j in range(4):
                        ti = a * 4 + j
                        ct = ti + 44 * b
                        bank = rr['pool'][0]
                        rr['b'] += 1
                        bank = rr['b'] % 6
                        for fc in range(FC):
                            P.op('pe', lambda e, fc=fc, j=j: e.matmul(ps[:, bank, :], lhsT=W[:, fc, j * 128:(j + 1) * 128],
                                                                      rhs=hT[:, fc, :], start=(fc == 0),
                                                                      stop=(fc == FC - 1)), r=[Wn, 'hT'], w=[PSB(bank)])
                        if c > 0:
                            for fc in range(FC):
                                P.op('pe', lambda e, fc=fc, j=j: e.matmul(
                                    ps[:, hb, j * 2:j * 2 + 2], lhsT=W[:, fc, j * 128:(j + 1) * 128], rhs=hTh[:, fc, :],
                                    start=(fc == 0), stop=(fc == FC - 1)), r=[Wn, 'hTh'], w=[PSB(hb)])
                        ai += 1
                        A = Ab[ai % 2]
                        An = 'Ab%d' % (ai % 2)
                        P.op('act', lambda e, ct=ct, A=A: e.activation(out=A[:], in_=ps[:, bank, :], func=AF.Identity,
                                                                       bias=cw(l, 3, ct), scale=cw(l, 1, ct)),
                             r=[PSB(bank), 'CW'], w=[An])
                        for (sq_, col0, n, tl0) in segs:
                            P.op('dve', lambda e, ct=ct, A=A, col0=col0, n=n: e.scalar_tensor_tensor(
                                out=A[:, col0 + 1:col0 + n], in0=ps[:, bank, col0:col0 + n - 1], scalar=cw(l, 0, ct),
                                in1=A[:, col0 + 1:col0 + n], op0=ALU.mult, op1=ALU.add),
                                r=[PSB(bank), An, 'CW'], w=[An])
                            P.op('dve', lambda e, ct=ct, A=A, col0=col0, n=n: e.scalar_tensor_tensor(
                                out=A[:, col0:col0 + n - 1], in0=ps[:, bank, col0 + 1:col0 + n], scalar=cw(l, 2, ct),
                                in1=A[:, col0:col0 + n - 1], op0=ALU.mult, op1=ALU.add),
                                r=[PSB(bank), An, 'CW'], w=[An])
                        if hasl:
                            P.op('dve', lambda e, ct=ct, A=A, j=j: e.scalar_tensor_tensor(
                                out=A[:, 0:1], in0=ps[:, hb, j * 2:j * 2 + 1], scalar=cw(l, 0, ct), in1=A[:, 0:1],
                                op0=ALU.mult, op1=ALU.add), r=[PSB(hb), An, 'CW'], w=[An])
                        if hasr:
                            P.op('dve', lambda e, ct=ct, A=A, j=j: e.scalar_tensor_tensor(
                                out=A[:, 511:512], in0=ps[:, hb, j * 2 + 1:j * 2 + 2], scalar=cw(l, 2, ct),
                                in1=A[:, 511:512], op0=ALU.mult, op1=ALU.add), r=[PSB(hb), An, 'CW'], w=[An])
                        if b == 0:
                            P.op('act', lambda e, A=A, j=j: e.activation(out=sg[j][:], in_=A[:], func=AF.Silu),
                                 r=[An], w=['sg%d' % j])
                        else:
                            P.op('dve', lambda e, A=A, j=j, ti=ti: e.tensor_tensor(out=actT[:, ti, :], in0=A[:],
                                                                                   in1=sg[j][:], op=ALU.mult),
                                 r=[An, 'sg%d' % j], w=['actT'])
                else:
                    dg, fq = a, b
                    base = 0 if dg % 2 == 0 else 4
                    for j in range(4):
                        bank = base + j
                        for kk in range(11):
                            P.op('pe', lambda e, kk=kk, j=j, bank=bank: e.matmul(
                                ps[:, bank, :], lhsT=W[:, kk, j * 128:(j + 1) * 128], rhs=actT[:, fq * 11 + kk, :],
                                start=(fq == 0 and kk == 0), stop=(fq == 3 and kk == 10)),
                                r=[Wn, 'actT'], w=[PSB(bank)])
                    if fq == 3:
                        g2 = prm(l, v, 5)
                        for j in range(4):
                            dc = dg * 4 + j
                            bank = base + j
                            P.op('dve', lambda e, dc=dc, bank=bank, g2=g2: e.scalar_tensor_tensor(
                                out=xT[:, dc, :], in0=ps[:, bank, :], scalar=g2[:, dc:dc + 1], in1=xT[:, dc, :],
                                op0=ALU.mult, op1=ALU.add), r=[PSB(bank), 'xT', 'PRM'], w=['xT'])
                        if dg == 3:
                            P.dma('pool', S[xdst][:, :, tok0:tok0 + 512].rearrange("c p t -> p c t"), xT[:],
                                  r=['xT'], w=[xdst])
            P.barrier()

    def phase_T1(src, tiles=range(NT // 128)):
        with ExitStack() as st:
            xtok = [salloc('xtok%d' % i, (128, D), F32, st) for i in range(2)]
            xTt = [salloc('xTt%d' % i, (128, FC, 128), F32, st) for i in range(2)]
            tiles = list(tiles)

            def t1load(k):
                P.dma('sp', xTt[k % 2][:], S[src][:, :, tiles[k] * 128:tiles[k] * 128 + 128].rearrange("c p t -> p c t"),
                      r=[src], w=['xTt%d' % (k % 2)])
            t1load(0)
            for k, tt in enumerate(tiles):
                slot = k % 2
                t0 = tt * 128
                if k + 1 < len(tiles):
                    t1load(k + 1)
                for q in range(4):
                    bank = nb()
                    for j in range(4):
                        fc = q * 4 + j
                        P.op('pe', lambda e, slot=slot, fc=fc, j=j, bank=bank: e.transpose(
                            out=ps[:, bank, j * 128:(j + 1) * 128], in_=xTt[slot][:, fc, :], identity=identf),
                            r=['xTt%d' % slot, 'cst'], w=[PSB(bank)])
                    copy_op(xtok[slot][:, q * 512:(q + 1) * 512], ps[:, bank, :], r=[PSB(bank)], w=['xtok%d' % slot])
                dst = O['yp'][t0:t0 + 128, :] if t0 < NP else O['ys'][t0 - NP:t0 - NP + 128, :]
                P.dma('pool', dst, xtok[slot][:], r=['xtok%d' % slot], w=['yout'])
            P.barrier()

    dbg_chunks = range(NCH)
    if stop_after in ('pB', 'pAp', 'pC1', 'pC2'):
        phase_A(0, chunks=[0])
        phase_B(0, seqs=(0, 1))
        if stop_after != 'pB':
            phase_Ap(0, chunks=[0])
        if stop_after in ('pC1', 'pC2'):
            phase_C1(0, chunks=[0])
        if stop_after == 'pC2':
            phase_C2(0, chunks=[0])
        return finish(nc, es, P)
    if stop_after in ('sA', 'sB', 'sAp', 'sC1', 'sC2'):
        phase_A(0, chunks=[1, 2])
        if stop_after != 'sA':
            phase_E(0)
            phase_B(0, seqs=(2,))
        if stop_after in ('sAp', 'sC1', 'sC2'):
            phase_Ap(0, chunks=[1, 2])
        if stop_after in ('sC1', 'sC2'):
            phase_C1(0, chunks=[1, 2])
        if stop_after == 'sC2':
            phase_C2(0, chunks=[1, 2])
        return finish(nc, es, P)
    if stop_after == 'E':
        phase_E(0)
        return finish(nc, es, P)
    if stop_after == 'prompt':
        for l in range(L):
            phase_A(l, chunks=[0])
            phase_B(l, seqs=(0, 1))
            phase_Ap(l, chunks=[0])
            phase_C1(l, chunks=[0])
            phase_C2(l, chunks=[0])
        phase_T1('xTa', tiles=range(4))
        return finish(nc, es, P)
    if stop_after is None:
        for l in range(L):
            phase_A(l)
            phase_E(l)
            phase_B(l)
            phase_Ap(l)
            phase_C1(l)
            phase_C2(l)
        phase_T1('xTa')
        return finish(nc, es, P)
    dbg_chunks = range(NCH)
    if stop_after == 'A0c0':
        phase_A(0, chunks=[0])
        return finish(nc, es, P)
    if stop_after == 'A0c01':
        phase_A(0, chunks=[0, 1])
        return finish(nc, es, P)
    raise NotImplementedError


def finish(nc, es, P):
    P.barrier()
    P.flush()
    es.close()
    return nc


_CONSTS = None


def make_in_maps(inp):
    global _CONSTS
    if _CONSTS is None:
        _CONSTS = _host_consts()
    f = lambda a: np.ascontiguousarray(np.asarray(a, dtype=np.float32))
    shared = {k: f(inp[k]) for k in ['norm1_g', 'norm2_g', 'ada_w', 'ada_b', 'w_in', 'diff_qn_g', 'diff_kn_g',
                                     'diff_lam', 'diff_sub_g', 'mla_qa_g', 'mla_kva_g', 'mla_w_uq', 'mla_w_ukv',
                                     'mla_qn_g', 'mla_kn_g', 'na_qn_g', 'na_kn_g', 'na_bias', 'pool_w', 'pool_scale',
                                     'w_out', 'w_up', 'conv_w', 'conv_b', 'w_down']}
    shared.update(_CONSTS)
    maps = []
    for c in range(8):
        m = dict(shared)
        m['xp'] = f(inp['x_prompt'][2 * c:2 * c + 2]).reshape(NP, D)
        m['xs'] = f(inp['x_sample'][c])
        m['cdk'] = f(inp['cache_diff_k'][c]).reshape(L, PAST, 512)
        m['cdv'] = f(inp['cache_diff_v'][c]).reshape(L, PAST, 512)
        m['cckv'] = f(inp['cache_mla_ckv'][c])
        m['ckpe'] = f(inp['cache_mla_kpe'][c])
        m['cnk'] = f(inp['cache_na_k'][c]).reshape(L, PAST, 512)
        m['cnv'] = f(inp['cache_na_v'][c]).reshape(L, PAST, 512)
        m['cvec'] = np.ascontiguousarray(np.stack([f(inp['c_ctx']), f(inp['c'][c])], 0))
        maps.append(m)
    return maps


_NC = None


def kernel(**inp):
    global _NC
    if _NC is None:
        _NC = build_program()
    maps = make_in_maps(inp)
    res = run_bass_kernel_spmd(_NC, maps, core_ids=list(range(8)))
    R = res.results
    yp = np.concatenate([r['yp'].reshape(2, 256, D) for r in R], 0)
    ys = np.stack([r['ys'] for r in R], 0)
    dk = np.concatenate([r['o_dk'].reshape(2, L, 256, 4, 2, 64) for r in R], 0)
    dv = np.concatenate([r['o_dv'].reshape(2, L, 256, 4, 128) for r in R], 0)
    ckv = np.concatenate([r['o_ckv'].reshape(2, L, 256, 128) for r in R], 0)
    kpe = np.concatenate([r['o_kpe'].reshape(2, L, 256, 32) for r in R], 0)
    nk = np.concatenate([r['o_nk'].reshape(2, L, 256, 8, 64) for r in R], 0)
    nv = np.concatenate([r['o_nv'].reshape(2, L, 256, 8, 64) for r in R], 0)
    return tuple(np.ascontiguousarray(a.astype(np.float32)) for a in (yp, ys, dk, dv, ckv, kpe, nk, nv))
```

```python
import math
import os as _os
from contextlib import ExitStack
import numpy as np
import concourse.bass as bass
import concourse.mybir as mybir
from concourse.bass_utils import run_bass_kernel_spmd

F32, BF16 = mybir.dt.float32, mybir.dt.bfloat16
AF = mybir.ActivationFunctionType
ALU = mybir.AluOpType

D = 2048
FC = 16
L = 2
NP = 512
NS = 4096
NT = NP + NS
NCH = NT // 512
PAST = 256
NKEY = 256 + 256 + PAST + NS
KB = [0, 256, 512]
SEQ_T0 = [0, 256, 512]
SEQ_N = [256, 256, 4096]
SEQ_CTX = [0, 0, 256]
IN_W = 4128
DFF = 5632
EPS = 1e-6
COMPUTE = ['pe', 'act', 'dve', 'pool']
STORE_Q = 'pool'


class Prog:
    def __init__(s, nc, es):
        s.nc = nc
        s.ops = {e: [] for e in ['pe', 'act', 'dve', 'pool', 'sp']}
        s.cnt = {e: 0 for e in COMPUTE}
        s.sem = {}
        for e in COMPUTE:
            s.sem['c_' + e] = es.enter_context(nc.semaphore('c_' + e))
        s.dq = {'sp': ['d_sp%d' % i for i in range(20)], 'pool': ['d_pl%d' % i for i in range(12)]}
        s.duse = {}
        for q in s.dq:
            for n in s.dq[q]:
                s.sem[n] = es.enter_context(nc.semaphore(n))
                s.duse[n] = 0
        s.drr = {'sp': 0, 'pool': 0}
        s.waited = {e: {} for e in s.ops}
        s.lastw = {}
        s.readers = {}
        s.nins = 0

    def _deps(s, eng, reads, writes):
        need = {}

        def add(tok):
            if tok is None:
                return
            n, v = tok
            if eng == 'pe' and n == 'c_pe':
                return
            if need.get(n, 0) < v:
                need[n] = v
        for r in reads:
            add(s.lastw.get(r))
        for r in writes:
            add(s.lastw.get(r))
            for n, v in s.readers.get(r, {}).items():
                add((n, v))
        out = []
        for n, v in need.items():
            if s.waited[eng].get(n, 0) < v:
                s.waited[eng][n] = v
                out.append((n, v))
        return out

    def _commit(s, tok, reads, writes):
        for r in reads:
            d = s.readers.setdefault(r, {})
            if d.get(tok[0], 0) < tok[1]:
                d[tok[0]] = tok[1]
        for r in writes:
            s.lastw[r] = tok
            s.readers[r] = {}

    ENGMAP = {'pe': 'tensor', 'act': 'scalar', 'dve': 'vector', 'pool': 'gpsimd', 'sp': 'sync'}

    def _issue(s, e, fn, waits, inc):
        eng = getattr(s.nc, s.ENGMAP[e])
        if fn is None:
            for n, v in waits:
                eng.wait_ge(s.sem[n], v)
            return
        if e == 'pe':
            pre, emb = waits, None
        else:
            pre, emb = waits[1:], (waits[0] if waits else None)
        for n, v in pre:
            eng.wait_ge(s.sem[n], v)
        ins = fn(eng)
        if emb is not None:
            ins.wait_op(s.sem[emb[0]], emb[1], "sem-ge")
        if inc is not None:
            ins.then_inc(s.sem[inc[0]], inc[1])
        s.nins += 1

    def op(s, eng, fn, r=(), w=()):
        waits = s._deps(eng, r, w)
        s.cnt[eng] += 1
        tok = ('c_' + eng, s.cnt[eng])
        s._issue(eng, fn, waits, (tok[0], 1))
        s._commit(tok, r, w)

    def dma(s, q, out, in_, r=(), w=(), slow=False):
        if q == 'pool':
            q = STORE_Q
        waits = s._deps(q, r, w)
        names = s.dq[q]
        n = names[s.drr[q] % len(names)]
        s.drr[q] += 1
        if s.duse[n] > 0 and s.waited[q].get(n, 0) < 16 * s.duse[n]:
            s.waited[q][n] = 16 * s.duse[n]
            waits.append((n, 16 * s.duse[n]))
        s.duse[n] += 1
        tok = (n, 16 * s.duse[n])
        if slow:
            s._issue(q, lambda e: e.dma_start(out=out, in_=in_, allow_slow_non_contiguous=True), waits, (n, 16))
        else:
            s._issue(q, lambda e: e.dma_start(out=out, in_=in_), waits, (n, 16))
        s._commit(tok, r, w)

    def barrier(s):
        toks = [('c_' + e, s.cnt[e]) for e in COMPUTE if s.cnt[e] > 0]
        toks += [(n, 16 * u) for n, u in s.duse.items() if u > 0]
        for e in s.ops:
            waits = [(n, v) for n, v in toks if s.waited[e].get(n, 0) < v]
            for n, v in waits:
                s.waited[e][n] = v
            if waits:
                s._issue(e, None, waits, None)
        s.lastw.clear()
        s.readers.clear()

    def flush(s):
        pass


def _host_consts():
    c = {}
    cst = np.zeros((128, 832), np.float32)
    cst[:, 0:128] = np.eye(128)
    cst[:, 128:256] = 1.0
    for b in range(2):
        cst[b * 64:(b + 1) * 64, 256 + b * 64:256 + (b + 1) * 64] = 1.0
    R = np.zeros((128, 128), np.float32)
    for b in range(2):
        for m in range(32):
            R[b * 64 + m + 32, b * 64 + m] = -1.0
            R[b * 64 + m, b * 64 + m + 32] = 1.0
    cst[:, 384:512] = R
    Rm = np.zeros((128, 128), np.float32)
    for m in range(16):
        Rm[64 + m + 16, 64 + m] = -1.0
        Rm[64 + m, 64 + m + 16] = 1.0
    cst[:, 512:640] = Rm
    sh = np.zeros((128, 128), np.float32)
    for i in range(32):
        sh[i, 64 + i] = 1.0
    cst[:, 640:768] = sh
    for i in range(64):
        cst[i, 768 + 63 - i] = 1.0
    c['cst'] = cst
    t = np.arange(NS)
    rows = (t // 64).astype(np.float32)
    cols = (t % 64).astype(np.float32)

    def ang(dim):
        q = dim // 4
        inv = (10000.0 ** (-np.arange(q, dtype=np.float32) / q)).astype(np.float32)
        return np.concatenate([rows[:, None] * inv, cols[:, None] * inv], axis=-1).astype(np.float32)
    a64 = ang(64)
    a32 = ang(32)
    rope = np.zeros((4, 128, NS), np.float32)
    for p in range(128):
        d = p % 64
        rope[0, p] = np.cos(a64[:, d % 32])
        rope[1, p] = np.sin(a64[:, d % 32])
    rope[2, :64] = 1.0
    for i in range(32):
        rope[2, 64 + i] = np.cos(a32[:, i % 16])
        rope[3, 64 + i] = np.sin(a32[:, i % 16])
    c['rope'] = rope
    invc = np.zeros((4, NT), np.float32)
    for g, w in enumerate((2, 4, 8, 16)):
        for s in range(3):
            n = SEQ_N[s]
            tt = np.arange(n)
            lo = np.clip(tt - w // 2, 0, n)
            hi = np.clip(tt - w // 2 + w, 0, n)
            invc[g, SEQ_T0[s]:SEQ_T0[s] + n] = 1.0 / (hi - lo)
    c['invc'] = np.ascontiguousarray(np.broadcast_to(invc[None], (128, 4, NT))).astype(np.float32)
    nam = np.zeros((3, 8, 128, 512), np.float32)
    for ty, (r0, kr0) in enumerate(((0, 0), (8, 4), (56, 48))):
        for ch in range(8):
            for krl in range(2):
                kr = kr0 + 2 * ch + krl
                for qrl in range(8):
                    qr = r0 + qrl
                    rs = min(max(qr - 4, 0), 56)
                    if not (rs <= kr < rs + 8):
                        continue
                    qc = np.arange(64)
                    cs = np.clip(qc - 8, 0, 48)
                    kc = np.arange(64)
                    ok = (kc[:, None] >= cs[None, :]) & (kc[:, None] < cs[None, :] + 16)
                    nam[ty, ch, krl * 64:(krl + 1) * 64, qrl * 64:(qrl + 1) * 64] = ok
    c['nam'] = nam
    return c


NA_TYPES = ((0, 0), (8, 4), (56, 48))


def na_block_info(qb):
    r0 = qb * 8
    kr0 = min(max(r0 - 4, 0), 48)
    ty = 0 if qb == 0 else (2 if qb == 7 else 1)
    return r0, kr0, ty


def build_program(stop_after=None, dbg=False):
    nc = bass.Bass("TRN2", target_bir_lowering=False)
    es = ExitStack()

    def din(name, shape, dt=F32):
        return nc.dram_tensor(name, list(shape), dt, kind="ExternalInput").ap()

    def dout(name, shape):
        return nc.dram_tensor(name, list(shape), F32, kind="ExternalOutput").ap()

    def dscr(name, shape, dt, out=False):
        return nc.dram_tensor(name, list(shape), dt, kind="ExternalOutput" if (out and dbg) else "Internal").ap()

    I = {}
    for name, shape in [
        ('xp', (NP, D)), ('xs', (NS, D)), ('cdk', (L, PAST, 512)), ('cdv', (L, PAST, 512)),
        ('cckv', (L, PAST, 128)), ('ckpe', (L, PAST, 32)), ('cnk', (L, PAST, 512)), ('cnv', (L, PAST, 512)),
        ('cvec', (2, D)), ('norm1_g', (L, D)), ('norm2_g', (L, D)), ('ada_w', (L, D, 6 * D)), ('ada_b', (L, 6 * D)),
        ('w_in', (L, D, IN_W)), ('diff_qn_g', (L, 64)), ('diff_kn_g', (L, 64)), ('diff_lam', (L, 4, 64)),
        ('diff_sub_g', (L, 128)), ('mla_qa_g', (L, 384)), ('mla_kva_g', (L, 128)), ('mla_w_uq', (L, 384, 384)),
        ('mla_w_ukv', (L, 128, 768)), ('mla_qn_g', (L, 96)), ('mla_kn_g', (L, 96)), ('na_qn_g', (L, 64)),
        ('na_kn_g', (L, 64)), ('na_bias', (L, 8, 15, 31)), ('pool_w', (L, 4, 128, 128)), ('pool_scale', (L, 512)),
        ('w_out', (L, D, D)), ('w_up', (L, D, 2 * DFF)), ('conv_w', (L, 3, 2 * DFF)), ('conv_b', (L, 2 * DFF)),
        ('w_down', (L, DFF, D)),
        ('cst', (128, 832)), ('rope', (4, 128, NS)), ('invc', (128, 4, NT)), ('nam', (3, 8, 128, 512)),
    ]:
        I[name] = din(name, shape)
    O = {}
    for name, shape in [('yp', (NP, D)), ('ys', (NS, D)), ('o_dk', (2 * L * 256, 512)), ('o_dv', (2 * L * 256, 512)),
                        ('o_ckv', (2 * L * 256, 128)), ('o_kpe', (2 * L * 256, 32)), ('o_nk', (2 * L * 256, 512)),
                        ('o_nv', (2 * L * 256, 512))]:
        O[name] = dout(name, shape)

    S = {}
    S['xTa'] = dscr('xTa', (FC, 128, NT), F32, True)
    S['xTb'] = dscr('xTb', (FC, 128, NT), F32, True)
    S['x1T'] = dscr('x1T', (FC, 128, NT), F32, True)
    S['mixT'] = dscr('mixT', (FC, 128, NT), BF16, True)
    S['puT'] = dscr('puT', (4, 128, NT), F32, True)
    for m in 'dmn':
        S['QT' + m] = dscr('QT' + m, (128, 4, NT), BF16, True)
        S['KT' + m] = dscr('KT' + m, (128, 4, NKEY), BF16, True)
        S['V' + m] = dscr('V' + m, (NKEY // 128, 128, 512), BF16, True)
    S['WinG'] = dscr('WinG', (L, 8, 128, FC, 512), BF16)
    S['Wout'] = dscr('Wout', (L, 4, 128, FC, 512), BF16)
    S['Wup'] = dscr('Wup', (L, 22, 128, FC, 512), BF16)
    S['Wdn'] = dscr('Wdn', (L, 4, 4, 128, 11, 512), BF16)
    S['biasP'] = dscr('biasP', (8, 15, 127), F32)
    S['Emask'] = dscr('Emask', (3, 8, 128, 8, 512), BF16, True)

    P = Prog(nc, es)
    sb = {}

    uid = [0]

    def salloc(name, shape, dt, stack):
        uid[0] += 1
        t = stack.enter_context(nc.sbuf_tensor('sb%d_%s' % (uid[0], name), list(shape), dt))
        sb[name] = t
        return t

    ps = es.enter_context(nc.psum_tensor("ps", [128, 8, 512], F32))

    def PSB(b):
        return 'ps%d' % b

    cst = salloc('cst', (128, 832), F32, es)
    cstb = salloc('cstb', (128, 832), BF16, es)
    epsT = salloc('epsT', (128, 1), F32, es)
    PRM = salloc('PRM', (128, L * 2 * 6 * FC), F32, es)
    n1g = salloc('n1g', (128, L * FC), F32, es)
    n2g = salloc('n2g', (128, L * FC), F32, es)
    GV = salloc('GV', (128, L * 16), F32, es)
    CW = salloc('CW', (128, L * 4 * 88), F32, es)
    identf = cst[:, 0:128]
    identb = cstb[:, 0:128]
    onesb = cstb[:, 128:256]
    blk64b = cstb[:, 256:384]
    Rd = cst[:, 384:512]
    Rm = cst[:, 512:640]
    shiftb = cstb[:, 640:768]

    def prm(l, v, m):
        o = ((l * 2 + v) * 6 + m) * FC
        return PRM[:, o:o + FC]

    def gv(l, j, n=128):
        return GV[0:n, l * 16 + j:l * 16 + j + 1]

    def cw(l, k, t):
        o = (l * 4 + k) * 88 + t
        return CW[:, o:o + 1]

    P.dma('sp', cst[:], I['cst'][:, :], w=['cst'])
    P.op('dve', lambda e: e.tensor_copy(out=cstb[:], in_=cst[:]), r=['cst'], w=['cstb'])
    P.op('dve', lambda e: e.memset(epsT[:], EPS), w=['epsT'])
    for l in range(L):
        P.dma('sp', n1g[:, l * FC:(l + 1) * FC], I['norm1_g'][l].rearrange("(c p) -> p c", p=128), w=['n1g'], slow=True)
        P.dma('sp', n2g[:, l * FC:(l + 1) * FC], I['norm2_g'][l].rearrange("(c p) -> p c", p=128), w=['n2g'], slow=True)
        for j, (nm, n) in enumerate([('diff_qn_g', 64), ('diff_kn_g', 64), ('diff_sub_g', 128), ('mla_kva_g', 128),
                                     ('mla_qn_g', 96), ('mla_kn_g', 96), ('na_qn_g', 64), ('na_kn_g', 64)]):
            src = I[nm][l].rearrange("(p o) -> p o", o=1)
            P.dma('sp', GV[0:n, l * 16 + j:l * 16 + j + 1], src, w=['GV'], slow=True)
            if n == 64:
                P.dma('sp', GV[64:128, l * 16 + j:l * 16 + j + 1], src, w=['GV'], slow=True)
        P.dma('sp', GV[:, l * 16 + 8:l * 16 + 11], I['mla_qa_g'][l].rearrange("(c p) -> p c", p=128), w=['GV'], slow=True)
        P.dma('sp', GV[:, l * 16 + 11:l * 16 + 15], I['pool_scale'][l].rearrange("(c p) -> p c", p=128), w=['GV'], slow=True)
        P.op('dve', lambda e, l=l: e.tensor_scalar(out=GV[:, l * 16 + 2:l * 16 + 3], in0=GV[:, l * 16 + 2:l * 16 + 3],
                                                   scalar1=1.0 - (0.8 - 0.6 * math.exp(-0.3 * l)), scalar2=None,
                                                   op0=ALU.mult), r=['GV'], w=['GV'])
        for k in range(3):
            P.dma('sp', CW[:, (l * 4 + k) * 88:(l * 4 + k + 1) * 88],
                  I['conv_w'][l, k].rearrange("(c p) -> p c", p=128), w=['CW'], slow=True)
        P.dma('sp', CW[:, (l * 4 + 3) * 88:(l * 4 + 4) * 88], I['conv_b'][l].rearrange("(c p) -> p c", p=128),
              w=['CW'], slow=True)

    with ExitStack() as st:
        lamt = salloc('lamt', (128, L * 256), F32, st)
        lamw = salloc('lamw', (128, 8), F32, st)
        for l in range(L):
            P.dma('sp', lamt[:, l * 256:(l + 1) * 256],
                  I['diff_lam'][l].rearrange("a b -> (a b)").partition_broadcast(128), w=['lamt'])
        for l in range(L):
            lam_init = 0.8 - 0.6 * math.exp(-0.3 * l)
            o = l * 256
            P.op('dve', lambda e, o=o: e.tensor_tensor(out=lamt[:, o:o + 64], in0=lamt[:, o:o + 64],
                                                       in1=lamt[:, o + 64:o + 128], op=ALU.mult),
                 r=['lamt'], w=['lamt'])
            P.op('dve', lambda e, o=o: e.tensor_tensor(out=lamt[:, o + 128:o + 192], in0=lamt[:, o + 128:o + 192],
                                                       in1=lamt[:, o + 192:o + 256], op=ALU.mult),
                 r=['lamt'], w=['lamt'])
            P.op('dve', lambda e, o=o: e.reduce_sum(out=lamw[:, 0:1], in_=lamt[:, o:o + 64],
                                                    axis=mybir.AxisListType.X), r=['lamt'], w=['lamw'])
            P.op('dve', lambda e, o=o: e.reduce_sum(out=lamw[:, 1:2], in_=lamt[:, o + 128:o + 192],
                                                    axis=mybir.AxisListType.X), r=['lamt'], w=['lamw'])
            P.op('act', lambda e: e.activation(out=lamw[:, 2:4], in_=lamw[:, 0:2], func=AF.Exp),
                 r=['lamw'], w=['lamw'])
            P.op('dve', lambda e, l=l, li=lam_init: e.scalar_tensor_tensor(
                out=GV[:, l * 16 + 15:l * 16 + 16], in0=lamw[:, 3:4], scalar=-li, in1=lamw[:, 2:3],
                op0=ALU.add, op1=ALU.subtract), r=['lamw'], w=['GV'])
        P.barrier()
        P.flush()

    with ExitStack() as st:
        cT = salloc('cT', (128, FC, 2), F32, st)
        adb = salloc('adb', (128, L * 96), F32, st)
        aw = [salloc('aw%d' % i, (128, FC, 512), F32, st) for i in range(2)]
        for v in range(2):
            P.dma('sp', cT[:, :, v], I['cvec'][v].rearrange("(c p) -> p c", p=128), w=['cT'], slow=True)
        P.op('act', lambda e: e.activation(out=cT[:], in_=cT[:], func=AF.Silu), r=['cT'], w=['cT'])
        for l in range(L):
            P.dma('sp', adb[:, l * 96:(l + 1) * 96], I['ada_b'][l].rearrange("(c p) -> p c", p=128), w=['adb'], slow=True)
        gi = 0
        for l in range(L):
            awv = I['ada_w'][l].rearrange("(c p) n -> p c n", p=128)
            for g in range(24):
                slot = gi % 2
                gi += 1
                P.dma('sp', aw[slot][:], awv[:, :, g * 512:(g + 1) * 512], w=['aw%d' % slot])
                bank = g % 2
                for j in range(4):
                    for fc in range(FC):
                        P.op('pe', lambda e, slot=slot, j=j, fc=fc, bank=bank: e.matmul(
                            ps[:, bank, j * 2:j * 2 + 2], lhsT=aw[slot][:, fc, j * 128:(j + 1) * 128],
                            rhs=cT[:, fc, :], start=(fc == 0), stop=(fc == FC - 1)),
                            r=['aw%d' % slot, 'cT'], w=[PSB(bank)])
                for j in range(4):
                    ct = g * 4 + j
                    m, fc = ct // 16, ct % 16
                    for v in range(2):
                        o = ((l * 2 + v) * 6 + m) * FC + fc
                        P.op('dve', lambda e, o=o, j=j, v=v, bank=bank, l=l, ct=ct: e.tensor_tensor(
                            out=PRM[:, o:o + 1], in0=ps[:, bank, j * 2 + v:j * 2 + v + 1],
                            in1=adb[:, l * 96 + ct:l * 96 + ct + 1], op=ALU.add),
                            r=[PSB(bank), 'adb'], w=['PRM'])
        for l in range(L):
            for v in range(2):
                for (m, ng) in ((1, n1g), (4, n2g)):
                    P.op('dve', lambda e, l=l, v=v, m=m, ng=ng: e.scalar_tensor_tensor(
                        out=prm(l, v, m), in0=prm(l, v, m), scalar=1.0, in1=ng[:, l * FC:(l + 1) * FC],
                        op0=ALU.add, op1=ALU.mult), r=['PRM', 'n1g', 'n2g'], w=['PRM'])
        P.barrier()
        P.flush()

    with ExitStack() as st:
        stg = [salloc('stg%d' % i, (128, FC * 512), F32, st) for i in range(2)]
        stb = [salloc('stb%d' % i, (128, FC * 512), BF16, st) for i in range(2)]
        jobs = []
        WIN_G = [0, 512, 1024, 1536, 2080, 2592, 3104, 3616]
        for l in range(L):
            wv = I['w_in'][l].rearrange("(c p) n -> p c n", p=128)
            for g in range(8):
                jobs.append((wv[:, :, WIN_G[g]:WIN_G[g] + 512], S['WinG'][l, g], FC))
            wv = I['w_out'][l].rearrange("(c p) n -> p c n", p=128)
            for g in range(4):
                jobs.append((wv[:, :, g * 512:(g + 1) * 512], S['Wout'][l, g], FC))
            wv = I['w_up'][l].rearrange("(c p) n -> p c n", p=128)
            for g in range(22):
                jobs.append((wv[:, :, g * 512:(g + 1) * 512], S['Wup'][l, g], FC))
            wv = I['w_down'][l].rearrange("(c p) n -> p c n", p=128)
            for dg in range(4):
                for fq in range(4):
                    jobs.append((wv[:, fq * 11:(fq + 1) * 11, dg * 512:(dg + 1) * 512], S['Wdn'][l, dg, fq], 11))
        cengs = ['dve', 'act']
        def cload(i):
            src, dst, nk = jobs[i]
            slot = i % 2
            a = stg[slot][:, 0:nk * 512].rearrange("p (c n) -> p c n", n=512)
            P.dma('sp', a, src, w=['stg%d' % slot])
        cload(0)
        for i, (src, dst, nk) in enumerate(jobs):
            slot = i % 2
            b = stb[slot][:, 0:nk * 512]
            if i + 1 < len(jobs):
                cload(i + 1)
            ce = cengs[i % 2]
            if ce == 'act':
                P.op('act', lambda e, slot=slot, nk=nk: e.copy(out=stb[slot][:, 0:nk * 512],
                                                                in_=stg[slot][:, 0:nk * 512]),
                     r=['stg%d' % slot], w=['stb%d' % slot])
            else:
                P.op(ce, lambda e, slot=slot, nk=nk: e.tensor_copy(out=stb[slot][:, 0:nk * 512],
                                                                    in_=stg[slot][:, 0:nk * 512]),
                     r=['stg%d' % slot], w=['stb%d' % slot])
            P.dma('pool', dst.rearrange("p c n -> p (c n)"), b, r=['stb%d' % slot], w=['Wscr'])
        P.barrier()
        P.flush()

    with ExitStack() as st:
        xtok = [salloc('xtok%d' % i, (128, D), F32, st) for i in range(2)]
        xTt = [salloc('xTt%d' % i, (128, FC, 128), F32, st) for i in range(2)]
        def t0load(tt):
            slot = tt % 2
            t0 = tt * 128
            src = I['xp'][t0:t0 + 128, :] if t0 < NP else I['xs'][t0 - NP:t0 - NP + 128, :]
            P.dma('sp', xtok[slot][:], src, w=['xtok%d' % slot])
        t0load(0)
        for tt in range(NT // 128):
            slot = tt % 2
            t0 = tt * 128
            if tt + 1 < NT // 128:
                t0load(tt + 1)
            for q in range(4):
                bank = (tt * 4 + q) % 8
                for j in range(4):
                    fc = q * 4 + j
                    P.op('pe', lambda e, slot=slot, fc=fc, j=j, bank=bank: e.transpose(
                        out=ps[:, bank, j * 128:(j + 1) * 128], in_=xtok[slot][:, fc * 128:(fc + 1) * 128],
                        identity=identf), r=['xtok%d' % slot, 'cst'], w=[PSB(bank)])
                eng = 'act' if q % 2 else 'dve'
                if eng == 'act':
                    P.op('act', lambda e, slot=slot, q=q, bank=bank: e.copy(
                        out=xTt[slot][:, q * 4:(q + 1) * 4, :].rearrange("p c t -> p (c t)"), in_=ps[:, bank, :]),
                        r=[PSB(bank)], w=['xTt%d' % slot])
                else:
                    P.op('dve', lambda e, slot=slot, q=q, bank=bank: e.tensor_copy(
                        out=xTt[slot][:, q * 4:(q + 1) * 4, :].rearrange("p c t -> p (c t)"), in_=ps[:, bank, :]),
                        r=[PSB(bank)], w=['xTt%d' % slot])
            P.dma('pool', S['xTa'][:, :, t0:t0 + 128].rearrange("c p t -> p c t"), xTt[slot][:],
                  r=['xTt%d' % slot], w=['xTa'])
        P.barrier()
        P.flush()

    if stop_after == 'T0':
        return finish(nc, es, P)

    rr = {'b': 0, 'pool': list(range(8))}

    def nb():
        rr['b'] += 1
        return rr['pool'][rr['b'] % len(rr['pool'])]

    cnt_eng = {'i': 0}

    def copy_op(out, in_, r, w, eng=None):
        if eng is None:
            cnt_eng['i'] += 1
            eng = 'act' if cnt_eng['i'] % 2 else 'dve'
        if eng == 'act':
            P.op('act', lambda e: e.copy(out=out, in_=in_), r=r, w=w)
        else:
            P.op(eng, lambda e: e.tensor_copy(out=out, in_=in_), r=r, w=w)

    def chunk_info(c):
        if c == 0:
            return 0, [(0, 0, 256, 0), (1, 256, 256, 0)]
        return 1, [(2, 0, 512, (c - 1) * 512)]

    def rstd_from_ps(bank, M, N, nfeat, rt, rs, rtn, rsn):
        P.op('act', lambda e: e.activation(out=rt[0:M, 0:N], in_=ps[0:M, bank, 0:N], func=AF.Ln,
                                           bias=epsT[0:M, :], scale=1.0 / nfeat), r=[PSB(bank), 'epsT'], w=[rtn])
        P.op('act', lambda e: e.activation(out=rs[0:M, 0:N], in_=rt[0:M, 0:N], func=AF.Exp, scale=-0.5),
             r=[rtn], w=[rsn])

    def make_hT(T, xT, xTn, hT, hTn, l, v, which, N=512, ncols_off=0):
        o = ncols_off
        gg = prm(l, v, 1 if which == 1 else 4)
        shv = prm(l, v, 0 if which == 1 else 3)
        bank = nb()
        for fc in range(FC):
            sl = fc % 2
            P.op('act', lambda e, fc=fc, sl=sl: e.activation(out=T['sq'][sl][:, 0:N], in_=xT[:, fc, o:o + N],
                                                            func=AF.Square), r=[xTn], w=['sq%d' % sl])
            P.op('pe', lambda e, fc=fc, sl=sl: e.matmul(ps[:, bank, 0:N], lhsT=onesb, rhs=T['sq'][sl][:, 0:N],
                                                        start=(fc == 0), stop=(fc == FC - 1)),
                 r=['sq%d' % sl, 'cstb'], w=[PSB(bank)])
        rstd_from_ps(bank, 128, N, D, T['rt'], T['rsx'], 'rt', 'rsx')
        for fc in range(FC):
            sl = fc % 2
            P.op('dve', lambda e, fc=fc, sl=sl: e.scalar_tensor_tensor(
                out=T['tmp'][sl][:, 0:N], in0=xT[:, fc, o:o + N], scalar=gg[:, fc:fc + 1], in1=T['rsx'][:, 0:N],
                op0=ALU.mult, op1=ALU.mult), r=[xTn, 'rsx', 'PRM'], w=['tmp%d' % sl])
            P.op('act', lambda e, fc=fc, sl=sl: e.activation(out=hT[:, fc, o:o + N], in_=T['tmp'][sl][:, 0:N],
                                                            func=AF.Identity, bias=shv[:, fc:fc + 1], scale=1.0),
                 r=['tmp%d' % sl, 'PRM'], w=[hTn])

    def alloc_common(st):
        T = {}
        T['sq'] = [salloc('sq%d' % i, (128, 512), BF16, st) for i in range(2)]
        T['rt'] = salloc('rt', (128, 512), F32, st)
        T['rsx'] = salloc('rsx', (128, 512), F32, st)
        T['rs'] = salloc('rs', (128, 512), F32, st)
        T['tmp'] = [salloc('tmp%d' % i, (128, 512), F32, st) for i in range(2)]
        T['qn'] = [salloc('qn%d' % i, (128, 512), F32, st) for i in range(4)]
        T['t1'] = salloc('t1', (128, 512), F32, st)
        T['t2'] = salloc('t2', (128, 512), F32, st)
        T['ob'] = [salloc('ob%d' % i, (128, 512), BF16, st) for i in range(4)]
        T['obi'] = 0
        T['qni'] = 0
        return T

    def head_norm(T, src, srcn, M, N, ones_l, nfeat, gcol, rope=None, f32_out=None, f32n=None, defer=False):
        sl = T['qni'] % 2
        T['qni'] += 1
        P.op('act', lambda e: e.activation(out=T['sq'][sl][0:M, 0:N], in_=src, func=AF.Square),
             r=[srcn], w=['sq%d' % sl])
        import os as _os
        hs = int(_os.environ.get('HN_STOP', '9'))
        if hs <= 1:
            return T['ob'][0], 'ob0'
        bank = nb()
        P.op('pe', lambda e: e.matmul(ps[0:M, bank, 0:N], lhsT=ones_l, rhs=T['sq'][sl][0:M, 0:N],
                                      start=True, stop=True), r=['sq%d' % sl, 'cstb'], w=[PSB(bank)])
        if hs <= 2:
            return T['ob'][0], 'ob0'
        rstd_from_ps(bank, M, N, nfeat, T['rt'], T['rs'], 'rt', 'rs')
        if hs <= 3:
            return T['ob'][0], 'ob0'
        qi = T.get('qn4', 0) % 4
        T['qn4'] = T.get('qn4', 0) + 1
        qn = T['qn'][qi]
        qnn = 'qn%d' % qi
        if rope is None and f32_out is None:
            oi = T['obi'] % 4
            T['obi'] += 1
            ob = T['ob'][oi]
            obn = 'ob%d' % oi
            P.op('dve', lambda e: e.scalar_tensor_tensor(out=ob[0:M, 0:N], in0=src, scalar=gcol,
                                                         in1=T['rs'][0:M, 0:N], op0=ALU.mult, op1=ALU.mult),
                 r=[srcn, 'rs', 'GV'], w=[obn])
            return ob, obn
        P.op('dve', lambda e: e.scalar_tensor_tensor(out=qn[0:M, 0:N], in0=src, scalar=gcol, in1=T['rs'][0:M, 0:N],
                                                     op0=ALU.mult, op1=ALU.mult), r=[srcn, 'rs', 'GV'], w=[qnn])
        if hs <= 4:
            return T['ob'][0], 'ob0'
        if f32_out is not None:
            copy_op(f32_out, qn[0:M, 0:N], r=[qnn], w=[f32n])
        oi = T['obi'] % 4
        T['obi'] += 1
        ob = T['ob'][oi]
        obn = 'ob%d' % oi
        if rope is None:
            copy_op(ob[0:M, 0:N], qn[0:M, 0:N], r=[qnn], w=[obn])
            if hs <= 5:
                return T['ob'][0], 'ob0'
        elif defer:
            def part_b():
                Rl, Ct, St, ropen = rope
                b2 = nb()
                P.op('pe', lambda e: e.matmul(ps[0:M, b2, 0:N], lhsT=Rl, rhs=qn[0:M, 0:N], start=True, stop=True),
                     r=[qnn, 'cst'], w=[PSB(b2)])
                P.op('dve', lambda e: e.tensor_tensor(out=T['t1'][0:M, 0:N], in0=qn[0:M, 0:N], in1=Ct, op=ALU.mult),
                     r=[qnn, ropen], w=['t1'])
                P.op('dve', lambda e: e.tensor_tensor(out=T['t2'][0:M, 0:N], in0=ps[0:M, b2, 0:N], in1=St,
                                                      op=ALU.mult), r=[PSB(b2), ropen], w=['t2'])
                P.op('dve', lambda e: e.tensor_tensor(out=ob[0:M, 0:N], in0=T['t1'][0:M, 0:N], in1=T['t2'][0:M, 0:N],
                                                      op=ALU.add), r=['t1', 't2'], w=[obn])
                return ob, obn
            return part_b
        else:
            Rl, Ct, St, ropen = rope
            rm = int(_os.environ.get('ROPE_MODE', '3'))
            if rm == 0:
                copy_op(ob[0:M, 0:N], qn[0:M, 0:N], r=[qnn], w=[obn])
                return ob, obn
            b2 = nb()
            P.op('pe', lambda e: e.matmul(ps[0:M, b2, 0:N], lhsT=Rl, rhs=qn[0:M, 0:N], start=True, stop=True),
                 r=[qnn, 'cst'], w=[PSB(b2)])
            if rm == 1:
                copy_op(ob[0:M, 0:N], ps[0:M, b2, 0:N], r=[PSB(b2)], w=[obn])
                return ob, obn
            P.op('dve', lambda e: e.tensor_tensor(out=T['t1'][0:M, 0:N], in0=qn[0:M, 0:N], in1=Ct, op=ALU.mult),
                 r=[qnn, ropen], w=['t1'])
            P.op('dve', lambda e: e.tensor_tensor(out=T['t2'][0:M, 0:N], in0=ps[0:M, b2, 0:N], in1=St, op=ALU.mult),
                 r=[PSB(b2), ropen], w=['t2'])
            P.op('dve', lambda e: e.tensor_tensor(out=ob[0:M, 0:N], in0=T['t1'][0:M, 0:N], in1=T['t2'][0:M, 0:N],
                                                  op=ALU.add), r=['t1', 't2'], w=[obn])
        return ob, obn

    def transpose_out(T, src_fn, srcn, ncol, dst_fn, N):
        for tt in range(N // 128):
            bank = nb()
            off = 0
            for i in range(ncol):
                a, M = src_fn(i, tt)
                P.op('pe', lambda e, a=a, M=M, off=off: e.transpose(out=ps[:, bank, off:off + M], in_=a,
                                                                    identity=identf[0:M, 0:M]),
                     r=[srcn, 'cst'], w=[PSB(bank)])
                off += M
            sl = T['sti'] % 2
            T['sti'] += 1
            copy_op(T['st'][sl][:, 0:off], ps[:, bank, 0:off], r=[PSB(bank)], w=['st%d' % sl])
            P.dma('pool', dst_fn(tt), T['st'][sl][:, 0:off], r=['st%d' % sl], w=['stateout'])

    def mla_kv(T, l, ckvT, ckvn, kpeb, kpen, N, key0, rope, ropen=None):
        for h in range(4):
            bank = nb()
            P.op('pe', lambda e, h=h: e.matmul(ps[0:96, bank, 0:N], lhsT=T['wukvK'][:, h, :], rhs=ckvT,
                                               start=True, stop=False), r=['wukv', ckvn], w=[PSB(bank)])
            P.op('pe', lambda e: e.matmul(ps[0:96, bank, 0:N], lhsT=shiftb[0:32, 0:96], rhs=kpeb,
                                          start=False, stop=True), r=['cstb', kpen], w=[PSB(bank)])
            rp = None
            if rope is not None:
                rp = (Rm[0:96, 0:96], rope[0], rope[1], ropen)
            ob, obn = head_norm(T, ps[0:96, bank, 0:N], PSB(bank), 96, N, onesb[0:96, 0:96], 96.0, gv(l, 5, 96),
                                rope=rp)
            P.dma('pool', S['KTm'][0:96, h, key0:key0 + N], ob[0:96, 0:N], r=[obn], w=['KTm'])
        for tt in range(N // 128):
            bank = nb()
            P.op('pe', lambda e, tt=tt: e.matmul(ps[:, bank, :], lhsT=ckvT[:, tt * 128:(tt + 1) * 128],
                                                 rhs=T['wukvV'][:], start=True, stop=True),
                 r=['wukv', ckvn], w=[PSB(bank)])
            oi = T['obi'] % 4
            T['obi'] += 1
            copy_op(T['ob'][oi][:], ps[:, bank, :], r=[PSB(bank)], w=['ob%d' % oi])
            P.dma('pool', S['Vm'][(key0 + tt * 128) // 128], T['ob'][oi][:], r=['ob%d' % oi], w=['Vm'])

    def load_small_weights(T, l, st):
        T['wuq'] = salloc('wuq', (128, 3, 384), BF16, st)
        T['wukvK'] = salloc('wukvK', (128, 4, 96), BF16, st)
        T['wukvV'] = salloc('wukvV', (128, 512), BF16, st)
        T['wkpe'] = salloc('wkpe', (128, FC, 32), BF16, st)
        with ExitStack() as s2:
            a = salloc('swA', (128, 3, 384), F32, s2)
            b = salloc('swB', (128, 768), F32, s2)
            cc = salloc('swC', (128, FC, 32), F32, s2)
            P.dma('sp', a[:], I['mla_w_uq'][l].rearrange("(c p) n -> p c n", p=128), w=['swA'])
            P.dma('sp', b[:], I['mla_w_ukv'][l], w=['swB'])
            P.dma('sp', cc[:], I['w_in'][l].rearrange("(c p) n -> p c n", p=128)[:, :, 2048:2080], w=['swC'])
            P.op('dve', lambda e: e.tensor_copy(out=T['wuq'][:], in_=a[:]), r=['swA'], w=['wuq'])
            P.op('dve', lambda e: e.memset(T['wukvK'][:], 0.0), w=['wukv'])
            bv = b[:].rearrange("p (h x) -> p h x", x=192)
            P.op('dve', lambda e: e.tensor_copy(out=T['wukvK'][:, :, 0:64], in_=bv[:, :, 0:64]),
                 r=['swB'], w=['wukv'])
            P.op('dve', lambda e: e.tensor_copy(out=T['wukvV'][:].rearrange("p (h x) -> p h x", x=128),
                                                in_=bv[:, :, 64:192]), r=['swB'], w=['wukv'])
            P.op('dve', lambda e: e.tensor_copy(out=T['wkpe'][:], in_=cc[:]), r=['swC'], w=['wkpe'])
            P.barrier()
            P.flush()

    def phase_A(l, chunks=range(NCH)):
        with ExitStack() as st:
            T = alloc_common(st)
            load_small_weights(T, l, st)
            xT = salloc('xT', (128, FC, 512), F32, st)
            hT = salloc('hT', (128, FC, 512), BF16, st)
            wb = [salloc('wb%d' % i, (128, FC, 512), BF16, st) for i in range(3)]
            T['st'] = [salloc('st%d' % i, (128, 512), F32, st) for i in range(2)]
            T['sti'] = 0
            ropeT = salloc('ropeT', (128, 4, 512), F32, st)
            kst = salloc('kst', (128, 4, 512), F32, st)
            cqn = salloc('cqn', (128, 3, 512), BF16, st)
            ckvb = salloc('ckvb', (128, 512), BF16, st)
            ckvf = salloc('ckvf', (128, 512), F32, st)
            kpef = salloc('kpef', (32, 512), F32, st)
            kpeb = salloc('kpeb', (32, 512), BF16, st)
            puf = [salloc('puf%d' % i, (128, 512), F32, st) for i in range(2)]
            xsrc = 'xTa' if l % 2 == 0 else 'xTb'
            ctok = [salloc('ctok%d' % i, (128, 512), F32, st) for i in range(2)]
            ci = 0
            for (src, KT, Vn_) in ((I['cdk'], 'KTd', None), (I['cnk'], 'KTn', None)):
                for tt in range(2):
                    sl = ci % 2
                    ci += 1
                    P.dma('sp', ctok[sl][:], src[l, tt * 128:(tt + 1) * 128, :], w=['ctok%d' % sl])
                    for h in range(4):
                        bank = nb()
                        P.op('pe', lambda e, sl=sl, h=h, bank=bank: e.transpose(
                            out=ps[:, bank, 0:128], in_=ctok[sl][:, h * 128:(h + 1) * 128], identity=identf),
                            r=['ctok%d' % sl, 'cst'], w=[PSB(bank)])
                        oi = T['obi'] % 4
                        T['obi'] += 1
                        copy_op(T['ob'][oi][:, 0:128], ps[:, bank, 0:128], r=[PSB(bank)], w=['ob%d' % oi])
                        P.dma('pool', S[KT][:, h, 512 + tt * 128:512 + (tt + 1) * 128], T['ob'][oi][:, 0:128],
                              r=['ob%d' % oi], w=[KT])
            for (src, Vn_) in ((I['cdv'], 'Vd'), (I['cnv'], 'Vn')):
                for tt in range(2):
                    sl = ci % 2
                    ci += 1
                    P.dma('sp', ctok[sl][:], src[l, tt * 128:(tt + 1) * 128, :], w=['ctok%d' % sl])
                    oi = T['obi'] % 4
                    T['obi'] += 1
                    copy_op(T['ob'][oi][:], ctok[sl][:], r=['ctok%d' % sl], w=['ob%d' % oi])
                    P.dma('pool', S[Vn_][4 + tt], T['ob'][oi][:], r=['ob%d' % oi], w=[Vn_])
            for tt in range(2):
                sl = ci % 2
                ci += 1
                P.dma('sp', ctok[sl][:, 0:128], I['cckv'][l, tt * 128:(tt + 1) * 128, :], w=['ctok%d' % sl])
                P.dma('sp', ctok[sl][:, 128:160], I['ckpe'][l, tt * 128:(tt + 1) * 128, :], w=['ctok%d' % sl])
                bank = nb()
                P.op('pe', lambda e, sl=sl, bank=bank: e.transpose(out=ps[:, bank, 0:128], in_=ctok[sl][:, 0:128],
                                                                   identity=identf),
                     r=['ctok%d' % sl, 'cst'], w=[PSB(bank)])
                copy_op(ckvb[:, tt * 128:(tt + 1) * 128], ps[:, bank, 0:128], r=[PSB(bank)], w=['ckvb'])
                bank = nb()
                P.op('pe', lambda e, sl=sl, bank=bank: e.transpose(out=ps[0:32, bank, 0:128],
                                                                   in_=ctok[sl][:, 128:160], identity=identf),
                     r=['ctok%d' % sl, 'cst'], w=[PSB(bank)])
                copy_op(kpeb[:, tt * 128:(tt + 1) * 128], ps[0:32, bank, 0:128], r=[PSB(bank)], w=['kpeb'])
            mla_kv(T, l, ckvb[:, 0:256], 'ckvb', kpeb[:, 0:256], 'kpeb', 256, 512, None)

            stream = [(c, g) for c in chunks for g in range(8)]
            if _os.environ.get('A_PARTS') == '1':
                stream = []
            if _os.environ.get('A_GROUPS'):
                stream = [(c, g) for c in chunks for g in range(8) if str(g) in _os.environ['A_GROUPS']]
            loaded = {}

            def wload(i):
                c, g = stream[i]
                sl = i % 3
                P.dma('sp', wb[sl][:], S['WinG'][l, g], r=['Wscr'], w=['wb%d' % sl])
                loaded[i] = sl
            for i in range(min(2, len(stream))):
                wload(i)
            for i, (c, g) in enumerate(stream):
                if i + 2 < len(stream):
                    wload(i + 2)
                sl = loaded[i]
                W = wb[sl]
                Wn = 'wb%d' % sl
                v, segs = chunk_info(c)
                tok0 = c * 512
                key0 = tok0 + (256 if c > 0 else 0)
                is_s = c > 0
                if i == 0 or stream[i - 1][0] != c:
                    P.dma('sp', xT[:], S[xsrc][:, :, tok0:tok0 + 512].rearrange("c p t -> p c t"),
                          r=[xsrc], w=['xT'])
                    if is_s:
                        t0l = (c - 1) * 512
                        P.dma('sp', ropeT[:], I['rope'][:, :, t0l:t0l + 512].rearrange("k p t -> p k t"),
                              w=['ropeT'])
                    make_hT(T, xT, 'xT', hT, 'hT', l, v, 1)

                def fm_tile(j, M=128, W=W, Wn=Wn):
                    bank = nb()
                    for fc in range(FC):
                        P.op('pe', lambda e, fc=fc: e.matmul(ps[0:M, bank, :], lhsT=W[:, fc, j * 128:j * 128 + M],
                                                             rhs=hT[:, fc, :], start=(fc == 0), stop=(fc == FC - 1)),
                             r=[Wn, 'hT'], w=[PSB(bank)])
                    return bank

                def tm_group(Vn_, onm, W=W, Wn=Wn):
                    for tt in range(4):
                        bank = nb()
                        for fc in range(FC):
                            P.op('pe', lambda e, fc=fc, tt=tt: e.matmul(
                                ps[:, bank, :], lhsT=hT[:, fc, tt * 128:(tt + 1) * 128], rhs=W[:, fc, :],
                                start=(fc == 0), stop=(fc == FC - 1)), r=[Wn, 'hT'], w=[PSB(bank)])
                        oi = T['obi'] % 4
                        T['obi'] += 1
                        copy_op(T['ob'][oi][:], ps[:, bank, :], r=[PSB(bank)], w=['ob%d' % oi], eng='dve')
                        P.dma('pool', S[Vn_][(key0 + tt * 128) // 128], T['ob'][oi][:], r=['ob%d' % oi], w=[Vn_])
                        if (not is_s) and _os.environ.get('NO_VOUT') != '1':
                            sl2 = T['sti'] % 2
                            T['sti'] += 1
                            copy_op(T['st'][sl2][:], ps[:, bank, :], r=[PSB(bank)], w=['st%d' % sl2], eng='dve')
                            b_, t_ = tt // 2, (tt % 2) * 128
                            if _os.environ.get('VOUT_MODE') == 'scratch':
                                P.dma('pool', S['puT'][0, :, 0:512], T['st'][sl2][:], r=['st%d' % sl2], w=['stateout'])
                            elif _os.environ.get('VOUT_MODE') != 'copyonly':
                                P.dma('pool', O[onm][(b_ * L + l) * 256 + t_:(b_ * L + l) * 256 + t_ + 128, :], T['st'][sl2][:], r=['st%d' % sl2],
                                      w=['stateout'])

                if g in (0, 1, 4, 5):
                    isq = g in (0, 4)
                    isd = g in (0, 1)
                    gcol = gv(l, {0: 0, 1: 1, 4: 6, 5: 7}[g])
                    dst = {0: 'QTd', 1: 'KTd', 4: 'QTn', 5: 'KTn'}[g]
                    gbanks = [fm_tile(j) for j in range(4)]
                    res_ = []
                    for j in range(4):
                        bank = gbanks[j]
                        rp = None
                        if isd and is_s:
                            rp = (Rd, ropeT[:, 0, :], ropeT[:, 1, :], 'ropeT')
                        f32o = None
                        if (not is_s) and (not isq):
                            f32o = kst[:, j, :]
                        res_.append(head_norm(T, ps[:, bank, :], PSB(bank), 128, 512, blk64b, 64.0, gcol, rope=rp,
                                              f32_out=f32o, f32n='kst', defer=True))
                    for j in range(4):
                        rj = res_[j]
                        ob, obn = rj() if callable(rj) else rj
                        if isq:
                            P.dma('pool', S[dst][:, j, tok0:tok0 + 512], ob[:], r=[obn], w=[dst])
                        else:
                            P.dma('pool', S[dst][:, j, key0:key0 + 512], ob[:], r=[obn], w=[dst])
                    if (not is_s) and (not isq):
                        onm = 'o_dk' if isd else 'o_nk'
                        transpose_out(T, lambda i, tt: (kst[:, i, tt * 128:(tt + 1) * 128], 128), 'kst', 4,
                                      lambda tt: O[onm][((tt // 2) * L + l) * 256 + (tt % 2) * 128:((tt // 2) * L + l) * 256 + (tt % 2) * 128 + 128, :], 512)
                elif g == 2:
                    tm_group('Vd', 'o_dv')
                elif g == 6:
                    tm_group('Vn', 'o_nv')
                elif g == 7:
                    for j in range(4):
                        bank = fm_tile(j)
                        sl2 = j % 2
                        copy_op(puf[sl2][:], ps[:, bank, :], r=[PSB(bank)], w=['puf%d' % sl2])
                        P.dma('pool', S['puT'][j, :, tok0:tok0 + 512], puf[sl2][:], r=['puf%d' % sl2], w=['puT'])
                elif g == 3:
                    banks = [fm_tile(j) for j in range(3)]
                    sb_ = nb()
                    for j in range(3):
                        sl2 = j % 2
                        P.op('act', lambda e, j=j, sl2=sl2: e.activation(out=T['sq'][sl2][:], in_=ps[:, banks[j], :],
                                                                         func=AF.Square),
                             r=[PSB(banks[j])], w=['sq%d' % sl2])
                        P.op('pe', lambda e, j=j, sl2=sl2: e.matmul(ps[:, sb_, :], lhsT=onesb, rhs=T['sq'][sl2][:],
                                                                    start=(j == 0), stop=(j == 2)),
                             r=['sq%d' % sl2, 'cstb'], w=[PSB(sb_)])
                    rstd_from_ps(sb_, 128, 512, 384.0, T['rt'], T['rs'], 'rt', 'rs')
                    for j in range(3):
                        P.op('dve', lambda e, j=j: e.scalar_tensor_tensor(
                            out=cqn[:, j, :], in0=ps[:, banks[j], :], scalar=GV[:, l * 16 + 8 + j:l * 16 + 9 + j],
                            in1=T['rs'][:], op0=ALU.mult, op1=ALU.mult), r=[PSB(banks[j]), 'rs', 'GV'], w=['cqn'])
                    for h in range(4):
                        bank = nb()
                        for j in range(3):
                            P.op('pe', lambda e, h=h, j=j: e.matmul(ps[0:96, bank, :],
                                                                    lhsT=T['wuq'][:, j, h * 96:(h + 1) * 96],
                                                                    rhs=cqn[:, j, :], start=(j == 0), stop=(j == 2)),
                                 r=['wuq', 'cqn'], w=[PSB(bank)])
                        rp = (Rm[0:96, 0:96], ropeT[0:96, 2, :], ropeT[0:96, 3, :], 'ropeT') if is_s else None
                        ob, obn = head_norm(T, ps[0:96, bank, :], PSB(bank), 96, 512, onesb[0:96, 0:96], 96.0,
                                            gv(l, 4, 96), rope=rp)
                        P.dma('pool', S['QTm'][0:96, h, tok0:tok0 + 512], ob[0:96, :], r=[obn], w=['QTm'])
                    bank = fm_tile(3)
                    sl2 = T['qni'] % 2
                    T['qni'] += 1
                    P.op('act', lambda e: e.activation(out=T['sq'][sl2][:], in_=ps[:, bank, :], func=AF.Square),
                         r=[PSB(bank)], w=['sq%d' % sl2])
                    b2 = nb()
                    P.op('pe', lambda e: e.matmul(ps[:, b2, :], lhsT=onesb, rhs=T['sq'][sl2][:], start=True, stop=True),
                         r=['sq%d' % sl2, 'cstb'], w=[PSB(b2)])
                    rstd_from_ps(b2, 128, 512, 128.0, T['rt'], T['rs'], 'rt', 'rs')
                    P.op('dve', lambda e: e.scalar_tensor_tensor(out=ckvf[:], in0=ps[:, bank, :], scalar=gv(l, 3),
                                                                 in1=T['rs'][:], op0=ALU.mult, op1=ALU.mult),
                         r=[PSB(bank), 'rs', 'GV'], w=['ckvf'])
                    copy_op(ckvb[:], ckvf[:], r=['ckvf'], w=['ckvb'])
                    bank = nb()
                    for fc in range(FC):
                        P.op('pe', lambda e, fc=fc: e.matmul(ps[0:32, bank, :], lhsT=T['wkpe'][:, fc, :],
                                                             rhs=hT[:, fc, :], start=(fc == 0), stop=(fc == FC - 1)),
                             r=['wkpe', 'hT'], w=[PSB(bank)])
                    copy_op(kpef[:], ps[0:32, bank, :], r=[PSB(bank)], w=['kpef'], eng='dve')
                    copy_op(kpeb[:], ps[0:32, bank, :], r=[PSB(bank)], w=['kpeb'], eng='dve')
                    if not is_s:
                        transpose_out(T, lambda i, tt: (ckvf[:, tt * 128:(tt + 1) * 128], 128), 'ckvf', 1,
                                      lambda tt: O['o_ckv'][((tt // 2) * L + l) * 256 + (tt % 2) * 128:((tt // 2) * L + l) * 256 + (tt % 2) * 128 + 128, :], 512)
                        transpose_out(T, lambda i, tt: (kpef[:, tt * 128:(tt + 1) * 128], 32), 'kpef', 1,
                                      lambda tt: O['o_kpe'][((tt // 2) * L + l) * 256 + (tt % 2) * 128:((tt // 2) * L + l) * 256 + (tt % 2) * 128 + 128, :], 512)
                    mla_kv(T, l, ckvb[:], 'ckvb', kpeb[:], 'kpeb', 512, key0,
                           (ropeT[0:96, 2, :], ropeT[0:96, 3, :]) if is_s else None, 'ropeT')
            P.barrier()
            P.flush()

    def phase_E(l):
        with ExitStack() as st:
            zt = salloc('zt', (15, 127), F32, st)
            Hk = [salloc('Hk%d' % i, (64, 15, 64), F32, st) for i in range(2)]
            ETr = salloc('ETr', (128, 8, 31 * 64), F32, st)
            mk = [salloc('mk%d' % i, (128, 512), F32, st) for i in range(2)]
            eb = [salloc('eb%d' % i, (128, 512), BF16, st) for i in range(2)]
            Jx = cst[0:64, 768:832]
            P.op('dve', lambda e: e.memset(zt[:], 0.0), w=['zt'])
            P.op('dve', lambda e: e.memset(ETr[:], 0.0), w=['ETr'])
            for h in range(8):
                P.dma('sp', S['biasP'][h], zt[:], r=['zt'], w=['biasP'])
                P.dma('sp', S['biasP'][h, :, 48:79], I['na_bias'][l, h], r=[], w=['biasP'])
            bp = S['biasP']
            for h in range(8):
                sl = h % 2
                src = bass.AP(tensor=bp.tensor, offset=bp.offset + h * 15 * 127, ap=[[1, 64], [127, 15], [1, 64]])
                P.dma('sp', Hk[sl][:], src, r=['biasP'], w=['Hk%d' % sl])
                b0 = 1 + 2 * sl
                for half in range(2):
                    for i2 in range(8 + half, 23 + half):
                        dr = 15 - i2 + half
                        r_ = dr + 7
                        bank = b0 + (i2 - 8) // 8
                        col = ((i2 - 8) % 8) * 64
                        P.op('pe', lambda e, half=half, r_=r_, bank=bank, col=col, sl=sl: e.matmul(
                            ps[half * 64:(half + 1) * 64, bank, col:col + 64], lhsT=Hk[sl][:, r_, :], rhs=Jx,
                            start=True, stop=True), r=['Hk%d' % sl, 'cst'], w=[PSB(bank)])
                for half in range(2):
                    p0 = half * 64
                    lo1, hi1 = 8 + half, 16
                    lo2, hi2 = 16, 23 + half
                    P.op('act', lambda e, p0=p0, lo1=lo1, hi1=hi1, b0=b0, h=h: e.activation(
                        out=ETr[p0:p0 + 64, h, lo1 * 64:hi1 * 64], in_=ps[p0:p0 + 64, b0, (lo1 - 8) * 64:(hi1 - 8) * 64],
                        func=AF.Exp), r=[PSB(b0)], w=['ETr'])
                    P.op('act', lambda e, p0=p0, lo2=lo2, hi2=hi2, b0=b0, h=h: e.activation(
                        out=ETr[p0:p0 + 64, h, lo2 * 64:hi2 * 64],
                        in_=ps[p0:p0 + 64, b0 + 1, (lo2 - 16) * 64:(hi2 - 16) * 64],
                        func=AF.Exp), r=[PSB(b0 + 1)], w=['ETr'])
            k = 0
            for ty, (r0, kr0) in enumerate(NA_TYPES):
                off = kr0 - r0
                for ch in range(8):
                    ms = (ty * 8 + ch) % 2
                    P.dma('sp', mk[ms][:], I['nam'][ty, ch], w=['mk%d' % ms])
                    i0 = 15 - (2 * ch + off)
                    for h in range(8):
                        sl = k % 2
                        k += 1
                        eng = 'dve'
                        P.op(eng, lambda e, sl=sl, ms=ms, h=h, i0=i0: e.tensor_tensor(
                            out=eb[sl][:], in0=ETr[:, h, i0 * 64:(i0 + 8) * 64], in1=mk[ms][:], op=ALU.mult),
                            r=['ETr', 'mk%d' % ms], w=['eb%d' % sl])
                        P.dma('pool', S['Emask'][ty, h, :, ch, :], eb[sl][:], r=['eb%d' % sl], w=['Emask'])
            P.barrier()

    def phase_B(l, seqs=(0, 1, 2)):
        with ExitStack() as st:
            T = alloc_common(st)
            KT = salloc('KT', (128, 4, PAST + NS), BF16, st)
            Vt = salloc('Vt', (128, (PAST + NS) // 128, 512), BF16, st)
            QT = [salloc('QT%d' % i, (128, 4, 512), BF16, st) for i in range(2)]
            pb = [salloc('pb%d' % i, (128, 512), BF16, st) for i in range(6)]
            rinv = [salloc('rinv%d' % i, (128, 512), F32, st) for i in range(2)]
            of = [salloc('of%d' % i, (128, 512), F32, st) for i in range(2)]
            Et = [salloc('Et%d' % i, (128, 8, 512), BF16, st) for i in range(2)]
            af = [salloc('af%d' % i, (128, 512), F32, st) for i in range(2)]
            onesf = cst[:, 128:256]
            rr['pool'] = [3]
            cn = {'s': 0, 'p': 0, 'q': 0, 'e': 0}

            LA = 3
            pend = []

            def emit_S(u):
                if u.get('pre') is not None:
                    u['pre']()
                cn['s'] += 1
                bs = cn['s'] % 4
                nq = u['nq']
                P.op('pe', lambda e: e.matmul(ps[:, bs, 0:nq], lhsT=u['kt'], rhs=u['q'], start=True, stop=True),
                     r=['KT', u['qn']], w=[PSB(bs)])
                cn['p'] += 1
                pi = cn['p'] % 6
                u['pi'] = pi
                P.op('act', lambda e: e.activation(out=pb[pi][:, 0:nq], in_=ps[:, bs, 0:nq], func=AF.Exp,
                                                   scale=u['scale']), r=[PSB(bs)], w=['pb%d' % pi])
                if u.get('emul') is not None:
                    eap, en = u['emul']
                    P.op('dve', lambda e: e.tensor_tensor(out=pb[pi][:, 0:nq], in0=pb[pi][:, 0:nq], in1=eap,
                                                          op=ALU.mult), r=['pb%d' % pi, en], w=['pb%d' % pi])

            def emit_PV(u):
                pi = u['pi']
                nq = u['nq']
                P.op('pe', lambda e: e.matmul(u['o'], lhsT=u['v'], rhs=pb[pi][:, 0:nq], start=u['first'],
                                              stop=u['last']), r=['Vt', 'pb%d' % pi], w=[PSB(u['ob'])])
                ab = u['accb']
                if u['first']:
                    P.op('dve', lambda e: e.tensor_copy(out=ps[:, ab, 0:nq], in_=pb[pi][:, 0:nq]),
                         r=['pb%d' % pi], w=[PSB(ab)])
                else:
                    P.op('dve', lambda e: e.tensor_tensor(out=ps[:, ab, 0:nq], in0=ps[:, ab, 0:nq],
                                                          in1=pb[pi][:, 0:nq], op=ALU.add),
                         r=['pb%d' % pi, PSB(ab)], w=[PSB(ab)])
                if u.get('post') is not None:
                    u['post']()

            def push(u):
                emit_S(u)
                pend.append(u)
                if len(pend) > LA:
                    emit_PV(pend.pop(0))

            def drain():
                while pend:
                    emit_PV(pend.pop(0))

            def mk(kt, q, v, nq, scale, o, s_, ones_, first, last, ob_, sb_, qn, emul=None, accb=None):
                return dict(kt=kt, q=q, v=v, nq=nq, scale=scale, o=o, s=s_, ones=ones_, first=first, last=last,
                            ob=ob_, sb=sb_, qn=qn, emul=emul, pre=None, post=None,
                            accb=(sb_ if accb is None else accb))

            def fin_sum(accb, nq, outs):
                fi = cn.get('af', 0) % 2
                cn['af'] = cn.get('af', 0) + 1
                P.op('dve', lambda e: e.tensor_copy(out=af[fi][:, 0:nq], in_=ps[:, accb, 0:nq]),
                     r=[PSB(accb)], w=['af%d' % fi])
                for (o_ap, l_ap, bk) in outs:
                    P.op('pe', lambda e, o_ap=o_ap, l_ap=l_ap: e.matmul(o_ap, lhsT=l_ap, rhs=af[fi][:, 0:nq],
                                                                        start=True, stop=True),
                         r=['af%d' % fi, 'cst'], w=[PSB(bk)])

            for m in _os.environ.get('B_MIX', 'dmn'):
                for s_ in seqs:
                    n = SEQ_N[s_]
                    nk = SEQ_CTX[s_] + n
                    nkc = nk // 128
                    kb = KB[s_]
                    Mk = 96 if m == 'm' else 128
                    drain()
                    P.dma('sp', KT[0:Mk, :, 0:nk], S['KT' + m][0:Mk, :, kb:kb + nk], r=['KT' + m], w=['KT'])
                    P.dma('sp', Vt[:, 0:nkc, :], S['V' + m][kb // 128:kb // 128 + nkc].rearrange("c p e -> p c e"),
                          r=['V' + m], w=['Vt'])
                    nq = min(512, n)
                    for qb in range(n // nq):
                        tok0 = SEQ_T0[s_] + qb * nq
                        cn['q'] += 1
                        qs = cn['q'] % 2
                        Q = QT[qs]
                        qnm = 'QT%d' % qs

                        def load_q(Q=Q, qs=qs, tok0=tok0, Mk=Mk, nq=nq, m=m):
                            P.dma('sp', Q[0:Mk, :, 0:nq], S['QT' + m][0:Mk, :, tok0:tok0 + nq], r=['QT' + m],
                                  w=['QT%d' % qs])
                        first_unit = True
                        if m == 'd':
                            sc = 64.0 ** -0.5
                            for h in range(4):
                                us = []
                                for kc in range(nkc):
                                    for j in range(2):
                                        us.append(mk(KT[j * 64:(j + 1) * 64, h, kc * 128:(kc + 1) * 128],
                                                     Q[j * 64:(j + 1) * 64, h, 0:nq], Vt[:, kc, h * 128:(h + 1) * 128],
                                                     nq, sc, ps[:, 4 + j, 0:nq], ps[:, 6 + j, 0:nq], onesb, kc == 0,
                                                     kc == nkc - 1, 4 + j, 6 + j, qnm))

                                def epi(h=h, tok0=tok0, nq=nq):
                                    for j in range(2):
                                        fin_sum(6 + j, nq, [(ps[:, 6 + j, 0:nq], onesf, 6 + j)])
                                    for j in range(2):
                                        P.op('dve', lambda e, j=j: e.reciprocal(out=rinv[j][:, 0:nq],
                                                                                in_=ps[:, 6 + j, 0:nq]),
                                             r=[PSB(6 + j)], w=['rinv%d' % j])
                                        P.op('dve', lambda e, j=j: e.tensor_tensor(
                                            out=of[j][:, 0:nq], in0=ps[:, 4 + j, 0:nq], in1=rinv[j][:, 0:nq],
                                            op=ALU.mult), r=[PSB(4 + j), 'rinv%d' % j], w=['of%d' % j])
                                    P.op('dve', lambda e: e.scalar_tensor_tensor(
                                        out=of[0][:, 0:nq], in0=of[1][:, 0:nq], scalar=gv(l, 15), in1=of[0][:, 0:nq],
                                        op0=ALU.mult, op1=ALU.add), r=['of0', 'of1', 'GV'], w=['of0'])
                                    ob, obn = head_norm(T, of[0][:, 0:nq], 'of0', 128, nq, onesb, 128.0, gv(l, 2))
                                    P.dma('pool', S['mixT'][h, :, tok0:tok0 + nq], ob[:, 0:nq], r=[obn], w=['mixT'])
                                us[-1]['post'] = epi
                                if first_unit:
                                    us[0]['pre'] = load_q
                                    first_unit = False
                                for u in us:
                                    push(u)
                        elif m == 'm':
                            sc = 96.0 ** -0.5
                            for h in range(4):
                                ob_, sb_ = 4 + h % 2, 6 + h % 2
                                us = []
                                for kc in range(nkc):
                                    us.append(mk(KT[0:96, h, kc * 128:(kc + 1) * 128], Q[0:96, h, 0:nq],
                                                 Vt[:, kc, h * 128:(h + 1) * 128], nq, sc, ps[:, ob_, 0:nq],
                                                 ps[:, sb_, 0:nq], onesb, kc == 0, kc == nkc - 1, ob_, sb_, qnm))

                                def epi(h=h, tok0=tok0, nq=nq, ob_=ob_, sb_=sb_):
                                    fin_sum(sb_, nq, [(ps[:, sb_, 0:nq], onesf, sb_)])
                                    P.op('dve', lambda e: e.reciprocal(out=rinv[0][:, 0:nq], in_=ps[:, sb_, 0:nq]),
                                         r=[PSB(sb_)], w=['rinv0'])
                                    oi = T['obi'] % 4
                                    T['obi'] += 1
                                    P.op('dve', lambda e: e.tensor_tensor(out=T['ob'][oi][:, 0:nq], in0=ps[:, ob_, 0:nq],
                                                                          in1=rinv[0][:, 0:nq], op=ALU.mult),
                                         r=[PSB(ob_), 'rinv0'], w=['ob%d' % oi])
                                    P.dma('pool', S['mixT'][4 + h, :, tok0:tok0 + nq], T['ob'][oi][:, 0:nq],
                                          r=['ob%d' % oi], w=['mixT'])
                                us[-1]['post'] = epi
                                if first_unit:
                                    us[0]['pre'] = load_q
                                    first_unit = False
                                for u in us:
                                    push(u)
                        else:
                            sc = 64.0 ** -0.5
                            if s_ == 2:
                                r0, kr0, ty = na_block_info(qb)
                                kcs = [(0, None), (1, None)] + [(2 + kr0 // 2 + ch, ch) for ch in range(8)]
                            else:
                                ty = 0
                                kcs = [(kc, None) for kc in range(nkc)]
                            for i in range(4):
                                ob_, sb_ = 4 + i % 2, 6
                                for hh in range(2):
                                    h = 2 * i + hh
                                    po = hh * 64
                                    us = []
                                    es_ = None
                                    if s_ == 2:
                                        cn['e'] += 1
                                        es_ = cn['e'] % 2
                                    for ki, (kc, ch) in enumerate(kcs):
                                        em = None
                                        if ch is not None:
                                            em = (Et[es_][:, ch, 0:nq], 'Et%d' % es_)
                                        us.append(mk(KT[po:po + 64, i, kc * 128:(kc + 1) * 128], Q[po:po + 64, i, 0:nq],
                                                     Vt[:, kc, h * 64:(h + 1) * 64], nq, sc, ps[po:po + 64, ob_, 0:nq],
                                                     ps[po:po + 64, sb_, 0:nq], onesb[:, 0:64], ki == 0,
                                                     ki == len(kcs) - 1, ob_, sb_, qnm, emul=em, accb=6 + hh))
                                    pres = []
                                    if first_unit:
                                        pres.append(load_q)
                                        first_unit = False
                                    if s_ == 2:
                                        def load_e(es_=es_, ty=ty, h=h):
                                            P.dma('sp', Et[es_][:], S['Emask'][ty, h], r=['Emask'], w=['Et%d' % es_])
                                        pres.append(load_e)
                                    if pres:
                                        us[0]['pre'] = (lambda pres=pres: [f() for f in pres])
                                    if hh == 1:
                                        def epi(i=i, tok0=tok0, nq=nq, ob_=ob_, sb_=sb_):
                                            fin_sum(6, nq, [(ps[0:64, 3, 0:nq], onesf[:, 0:64], 3)])
                                            fin_sum(7, nq, [(ps[64:128, 3, 0:nq], onesf[:, 0:64], 3)])
                                            sb_ = 3
                                            P.op('dve', lambda e: e.reciprocal(out=rinv[0][:, 0:nq],
                                                                               in_=ps[:, sb_, 0:nq]),
                                                 r=[PSB(sb_)], w=['rinv0'])
                                            oi = T['obi'] % 4
                                            T['obi'] += 1
                                            P.op('dve', lambda e: e.tensor_tensor(
                                                out=T['ob'][oi][:, 0:nq], in0=ps[:, ob_, 0:nq], in1=rinv[0][:, 0:nq],
                                                op=ALU.mult), r=[PSB(ob_), 'rinv0'], w=['ob%d' % oi])
                                            P.dma('pool', S['mixT'][8 + i, :, tok0:tok0 + nq], T['ob'][oi][:, 0:nq],
                                                  r=['ob%d' % oi], w=['mixT'])
                                        us[-1]['post'] = epi
                                    for u in us:
                                        push(u)
            drain()
            rr['pool'] = list(range(8))
            P.barrier()

    def phase_Ap(l, chunks=range(NCH)):
        with ExitStack() as st:
            pus = [salloc('pus%d' % i, (128, 528), F32, st) for i in range(2)]
            A = salloc('pA', (128, 528), F32, st)
            B = salloc('pB', (128, 528), F32, st)
            ivc = [salloc('ivc%d' % i, (128, 512), F32, st) for i in range(2)]
            db = [salloc('pdb%d' % i, (128, 512), BF16, st) for i in range(2)]
            yb = [salloc('pyb%d' % i, (128, 512), BF16, st) for i in range(2)]
            pwf = salloc('pwf', (128, 4, 128), F32, st)
            pw = salloc('pw', (128, 4, 128), BF16, st)
            P.dma('sp', pwf[:], I['pool_w'][l].rearrange("g c e -> c g e"), w=['pwf'])
            P.op('dve', lambda e: e.tensor_copy(out=pw[:], in_=pwf[:]), r=['pwf'], w=['pw'])
            k = 0
            for c in chunks:
                v, segs = chunk_info(c)
                for (s_, col0, n, tl0) in segs:
                    tok0 = c * 512 + col0
                    hasl = tl0 > 0
                    hasr = tl0 + n < SEQ_N[s_]
                    for g, w_ in enumerate((2, 4, 8, 16)):
                        sl = k % 2
                        k += 1
                        u = pus[sl]
                        un = 'pus%d' % sl
                        lo = 8 if not hasl else 0
                        hi = n + 8 if not hasr else n + 16
                        if not hasl:
                            P.op('dve', lambda e, u=u: e.memset(u[:, 0:8], 0.0), w=[un])
                        if not hasr:
                            P.op('dve', lambda e, u=u: e.memset(u[:, n + 8:n + 16], 0.0), w=[un])
                        P.dma('sp', u[:, lo:hi], S['puT'][g, :, tok0 - 8 + lo:tok0 - 8 + hi], r=['puT'], w=[un])
                        P.dma('sp', ivc[sl][:, 0:n], I['invc'][:, g, tok0:tok0 + n], w=['ivc%d' % sl])
                        P.op('dve', lambda e, u=u: e.tensor_tensor(out=A[:, 0:n + 15], in0=u[:, 0:n + 15],
                                                                   in1=u[:, 1:n + 16], op=ALU.add), r=[un], w=['pA'])
                        cur, curn, o = A, 'pA', 7
                        if w_ >= 4:
                            P.op('dve', lambda e: e.tensor_tensor(out=B[:, 0:n + 13], in0=A[:, 0:n + 13],
                                                                  in1=A[:, 2:n + 15], op=ALU.add), r=['pA'], w=['pB'])
                            cur, curn, o = B, 'pB', 6
                        if w_ >= 8:
                            P.op('dve', lambda e: e.tensor_tensor(out=A[:, 0:n + 9], in0=B[:, 0:n + 9],
                                                                  in1=B[:, 4:n + 13], op=ALU.add), r=['pB'], w=['pA'])
                            cur, curn, o = A, 'pA', 4
                        if w_ >= 16:
                            P.op('dve', lambda e: e.tensor_tensor(out=B[:, 0:n + 1], in0=A[:, 0:n + 1],
                                                                  in1=A[:, 8:n + 9], op=ALU.add), r=['pA'], w=['pB'])
                            cur, curn, o = B, 'pB', 0
                        o = 8 - w_ // 2
                        P.op('dve', lambda e, cur=cur, o=o, sl=sl: e.tensor_tensor(
                            out=cur[:, o:o + n], in0=cur[:, o:o + n], in1=ivc[sl][:, 0:n], op=ALU.mult),
                            r=[curn, 'ivc%d' % sl], w=[curn])
                        P.op('dve', lambda e, cur=cur, o=o, sl=sl, u=u: e.tensor_tensor(
                            out=db[sl][:, 0:n], in0=cur[:, o:o + n], in1=u[:, 8:8 + n], op=ALU.subtract),
                            r=[curn, un], w=['pdb%d' % sl])
                        bank = nb()
                        P.op('pe', lambda e, g=g, sl=sl, bank=bank: e.matmul(ps[:, bank, 0:n], lhsT=pw[:, g, :],
                                                                             rhs=db[sl][:, 0:n], start=True, stop=True),
                             r=['pw', 'pdb%d' % sl], w=[PSB(bank)])
                        P.op('act', lambda e, g=g, sl=sl, bank=bank: e.activation(
                            out=yb[sl][:, 0:n], in_=ps[:, bank, 0:n], func=AF.Copy, scale=gv(l, 11 + g)),
                            r=[PSB(bank), 'GV'], w=['pyb%d' % sl])
                        P.dma('pool', S['mixT'][12 + g, :, tok0:tok0 + n], yb[sl][:, 0:n], r=['pyb%d' % sl], w=['mixT'])
            P.barrier()

    def phase_C1(l, chunks=range(NCH)):
        with ExitStack() as st:
            xT = [salloc('xT%d' % i, (128, FC, 512), F32, st) for i in range(2)]
            mx = [salloc('mx%d' % i, (128, FC, 512), BF16, st) for i in range(2)]
            wb = [salloc('wb%d' % i, (128, FC, 512), BF16, st) for i in range(3)]
            xsrc = 'xTa' if l % 2 == 0 else 'xTb'
            stream = [(c, g) for c in chunks for g in range(4)]
            loaded = {}

            def wload(i):
                c, g = stream[i]
                sl = i % 3
                P.dma('sp', wb[sl][:], S['Wout'][l, g], r=['Wscr'], w=['wb%d' % sl])
                loaded[i] = sl
            for i in range(min(2, len(stream))):
                wload(i)
            for i, (c, g) in enumerate(stream):
                if i + 2 < len(stream):
                    wload(i + 2)
                sl = loaded[i]
                v, segs = chunk_info(c)
                tok0 = c * 512
                cs = (i // 4) % 2
                if g == 0:
                    P.dma('sp', xT[cs][:], S[xsrc][:, :, tok0:tok0 + 512].rearrange("c p t -> p c t"), r=[xsrc],
                          w=['xT%d' % cs])
                    P.dma('sp', mx[cs][:], S['mixT'][:, :, tok0:tok0 + 512].rearrange("c p t -> p c t"), r=['mixT'],
                          w=['mx%d' % cs])
                for j in range(4):
                    dc = g * 4 + j
                    bank = nb()
                    for fc in range(FC):
                        P.op('pe', lambda e, fc=fc, j=j, sl=sl, cs=cs, bank=bank: e.matmul(
                            ps[:, bank, :], lhsT=wb[sl][:, fc, j * 128:(j + 1) * 128], rhs=mx[cs][:, fc, :],
                            start=(fc == 0), stop=(fc == FC - 1)), r=['wb%d' % sl, 'mx%d' % cs], w=[PSB(bank)])
                    g1 = prm(l, v, 2)
                    P.op('dve', lambda e, dc=dc, cs=cs, bank=bank, g1=g1: e.scalar_tensor_tensor(
                        out=xT[cs][:, dc, :], in0=ps[:, bank, :], scalar=g1[:, dc:dc + 1], in1=xT[cs][:, dc, :],
                        op0=ALU.mult, op1=ALU.add), r=[PSB(bank), 'xT%d' % cs, 'PRM'], w=['xT%d' % cs])
                if g == 3:
                    P.dma('pool', S['x1T'][:, :, tok0:tok0 + 512].rearrange("c p t -> p c t"), xT[cs][:],
                          r=['xT%d' % cs], w=['x1T'])
            P.barrier()

    def phase_C2(l, chunks=range(NCH)):
        with ExitStack() as st:
            T = {}
            T['sq'] = [salloc('sq%d' % i, (128, 512), BF16, st) for i in range(2)]
            T['rt'] = salloc('rt', (128, 512), F32, st)
            T['rsx'] = salloc('rsx', (128, 512), F32, st)
            T['tmp'] = [salloc('tmp%d' % i, (128, 512), F32, st) for i in range(2)]
            xT = salloc('xT', (128, FC, 512), F32, st)
            xh = salloc('xh', (128, FC, 2), F32, st)
            hT = salloc('hT', (128, FC, 512), BF16, st)
            hTh = salloc('hTh', (128, FC, 2), BF16, st)
            actT = salloc('actT', (128, 44, 512), BF16, st)
            wb = [salloc('wb%d' % i, (128, FC, 512), BF16, st) for i in range(3)]
            Ab = [salloc('Ab%d' % i, (128, 512), F32, st) for i in range(2)]
            sg = [salloc('sg%d' % i, (128, 512), F32, st) for i in range(4)]
            xdst = 'xTb' if l % 2 == 0 else 'xTa'
            items = []
            for k in range(11):
                items += [('u', k, 0), ('u', k, 1)]
            for dg in range(4):
                for fq in range(4):
                    items.append(('d', dg, fq))
            stream = [(c,) + it for c in chunks for it in items]
            loaded = {}

            def wload(i):
                c, kind, a, b = stream[i]
                sl = i % 3
                if kind == 'u':
                    P.dma('sp', wb[sl][:], S['Wup'][l, a + 11 * b], r=['Wscr'], w=['wb%d' % sl])
                else:
                    P.dma('sp', wb[sl][:, 0:11, :], S['Wdn'][l, a, b], r=['Wscr'], w=['wb%d' % sl])
                loaded[i] = sl
            for i in range(min(2, len(stream))):
                wload(i)
            ai = 0
            dbanks = None
            for i, (c, kind, a, b) in enumerate(stream):
                if i + 2 < len(stream):
                    wload(i + 2)
                sl = loaded[i]
                W = wb[sl]
                Wn = 'wb%d' % sl
                v, segs = chunk_info(c)
                tok0 = c * 512
                s_ = segs[0][0]
                hasl = c > 1
                hasr = (c > 0) and (c < NCH - 1)
                if kind == 'u' and a == 0 and b == 0:
                    P.dma('sp', xT[:], S['x1T'][:, :, tok0:tok0 + 512].rearrange("c p t -> p c t"), r=['x1T'], w=['xT'])
                    make_hT(T, xT, 'xT', hT, 'hT', l, v, 2)
                    if c > 0:
                        P.op('dve', lambda e: e.memset(xh[:], 0.0), w=['xh'])
                        if hasl:
                            P.dma('sp', xh[:, :, 0:1], S['x1T'][:, :, tok0 - 1:tok0].rearrange("c p t -> p c t"),
                                  r=['x1T'], w=['xh'], slow=True)
                        if hasr:
                            P.dma('sp', xh[:, :, 1:2], S['x1T'][:, :, tok0 + 512:tok0 + 513].rearrange("c p t -> p c t"),
                                  r=['x1T'], w=['xh'], slow=True)
                        make_hT(T, xh, 'xh', hTh, 'hTh', l, v, 2, N=2)
                if kind == 'u':
                    hb = 6 + (i % 2)
                    for j in range(4):
                        ti = a * 4 + j
                        ct = ti + 44 * b
                        bank = rr['pool'][0]
                        rr['b'] += 1
                        bank = rr['b'] % 6
                        for fc in range(FC):
                            P.op('pe', lambda e, fc=fc, j=j: e.matmul(ps[:, bank, :], lhsT=W[:, fc, j * 128:(j + 1) * 128],
                                                                      rhs=hT[:, fc, :], start=(fc == 0),
                                                                      stop=(fc == FC - 1)), r=[Wn, 'hT'], w=[PSB(bank)])
                        if c > 0:
                            for fc in range(FC):
                                P.op('pe', lambda e, fc=fc, j=j: e.matmul(
                                    ps[:, hb, j * 2:j * 2 + 2], lhsT=W[:, fc, j * 128:(j + 1) * 128], rhs=hTh[:, fc, :],
                                    start=(fc == 0), stop=(fc == FC - 1)), r=[Wn, 'hTh'], w=[PSB(hb)])
                        ai += 1
                        A = Ab[ai % 2]
                        An = 'Ab%d' % (ai % 2)
                        P.op('act', lambda e, ct=ct, A=A: e.activation(out=A[:], in_=ps[:, bank, :], func=AF.Identity,
                                                                       bias=cw(l, 3, ct), scale=cw(l, 1, ct)),
                             r=[PSB(bank), 'CW'], w=[An])
                        for (sq_, col0, n, tl0) in segs:
                            P.op('dve', lambda e, ct=ct, A=A, col0=col0, n=n: e.scalar_tensor_tensor(
                                out=A[:, col0 + 1:col0 + n], in0=ps[:, bank, col0:col0 + n - 1], scalar=cw(l, 0, ct),
                                in1=A[:, col0 + 1:col0 + n], op0=ALU.mult, op1=ALU.add),
                                r=[PSB(bank), An, 'CW'], w=[An])
                            P.op('dve', lambda e, ct=ct, A=A, col0=col0, n=n: e.scalar_tensor_tensor(
                                out=A[:, col0:col0 + n - 1], in0=ps[:, bank, col0 + 1:col0 + n], scalar=cw(l, 2, ct),
                                in1=A[:, col0:col0 + n - 1], op0=ALU.mult, op1=ALU.add),
                                r=[PSB(bank), An, 'CW'], w=[An])
                        if hasl:
                            P.op('dve', lambda e, ct=ct, A=A, j=j: e.scalar_tensor_tensor(
                                out=A[:, 0:1], in0=ps[:, hb, j * 2:j * 2 + 1], scalar=cw(l, 0, ct), in1=A[:, 0:1],
                                op0=ALU.mult, op1=ALU.add), r=[PSB(hb), An, 'CW'], w=[An])
                        if hasr:
                            P.op('dve', lambda e, ct=ct, A=A, j=j: e.scalar_tensor_tensor(
                                out=A[:, 511:512], in0=ps[:, hb, j * 2 + 1:j * 2 + 2], scalar=cw(l, 2, ct),
                                in1=A[:, 511:512], op0=ALU.mult, op1=ALU.add), r=[PSB(hb), An, 'CW'], w=[An])
                        if b == 0:
                            P.op('act', lambda e, A=A, j=j: e.activation(out=sg[j][:], in_=A[:], func=AF.Silu),
                                 r=[An], w=['sg%d' % j])
                        else:
                            P.op('dve', lambda e, A=A, j=j, ti=ti: e.tensor_tensor(out=actT[:, ti, :], in0=A[:],
                                                                                   in1=sg[j][:], op=ALU.mult),
                                 r=[An, 'sg%d' % j], w=['actT'])
                else:
                    dg, fq = a, b
                    base = 0 if dg % 2 == 0 else 4
                    for j in range(4):
                        bank = base + j
                        for kk in range(11):
                            P.op('pe', lambda e, kk=kk, j=j, bank=bank: e.matmul(
                                ps[:, bank, :], lhsT=W[:, kk, j * 128:(j + 1) * 128], rhs=actT[:, fq * 11 + kk, :],
                                start=(fq == 0 and kk == 0), stop=(fq == 3 and kk == 10)),
                                r=[Wn, 'actT'], w=[PSB(bank)])
                    if fq == 3:
                        g2 = prm(l, v, 5)
                        for j in range(4):
                            dc = dg * 4 + j
                            bank = base + j
                            P.op('dve', lambda e, dc=dc, bank=bank, g2=g2: e.scalar_tensor_tensor(
                                out=xT[:, dc, :], in0=ps[:, bank, :], scalar=g2[:, dc:dc + 1], in1=xT[:, dc, :],
                                op0=ALU.mult, op1=ALU.add), r=[PSB(bank), 'xT', 'PRM'], w=['xT'])
                        if dg == 3:
                            P.dma('pool', S[xdst][:, :, tok0:tok0 + 512].rearrange("c p t -> p c t"), xT[:],
                                  r=['xT'], w=[xdst])
            P.barrier()

    def phase_T1(src, tiles=range(NT // 128)):
        with ExitStack() as st:
            xtok = [salloc('xtok%d' % i, (128, D), F32, st) for i in range(2)]
            xTt = [salloc('xTt%d' % i, (128, FC, 128), F32, st) for i in range(2)]
            tiles = list(tiles)

            def t1load(k):
                P.dma('sp', xTt[k % 2][:], S[src][:, :, tiles[k] * 128:tiles[k] * 128 + 128].rearrange("c p t -> p c t"),
                      r=[src], w=['xTt%d' % (k % 2)])
            t1load(0)
            for k, tt in enumerate(tiles):
                slot = k % 2
                t0 = tt * 128
                if k + 1 < len(tiles):
                    t1load(k + 1)
                for q in range(4):
                    bank = nb()
                    for j in range(4):
                        fc = q * 4 + j
                        P.op('pe', lambda e, slot=slot, fc=fc, j=j, bank=bank: e.transpose(
                            out=ps[:, bank, j * 128:(j + 1) * 128], in_=xTt[slot][:, fc, :], identity=identf),
                            r=['xTt%d' % slot, 'cst'], w=[PSB(bank)])
                    copy_op(xtok[slot][:, q * 512:(q + 1) * 512], ps[:, bank, :], r=[PSB(bank)], w=['xtok%d' % slot])
                dst = O['yp'][t0:t0 + 128, :] if t0 < NP else O['ys'][t0 - NP:t0 - NP + 128, :]
                P.dma('pool', dst, xtok[slot][:], r=['xtok%d' % slot], w=['yout'])
            P.barrier()

    dbg_chunks = range(NCH)
    if stop_after in ('pB', 'pAp', 'pC1', 'pC2'):
        phase_A(0, chunks=[0])
        phase_B(0, seqs=(0, 1))
        if stop_after != 'pB':
            phase_Ap(0, chunks=[0])
        if stop_after in ('pC1', 'pC2'):
            phase_C1(0, chunks=[0])
        if stop_after == 'pC2':
            phase_C2(0, chunks=[0])
        return finish(nc, es, P)
    if stop_after in ('sA', 'sB', 'sAp', 'sC1', 'sC2'):
        phase_A(0, chunks=[1, 2])
        if stop_after != 'sA':
            phase_E(0)
            phase_B(0, seqs=(2,))
        if stop_after in ('sAp', 'sC1', 'sC2'):
            phase_Ap(0, chunks=[1, 2])
        if stop_after in ('sC1', 'sC2'):
            phase_C1(0, chunks=[1, 2])
        if stop_after == 'sC2':
            phase_C2(0, chunks=[1, 2])
        return finish(nc, es, P)
    if stop_after == 'E':
        phase_E(0)
        return finish(nc, es, P)
    if stop_after == 'prompt':
        for l in range(L):
            phase_A(l, chunks=[0])
            phase_B(l, seqs=(0, 1))
            phase_Ap(l, chunks=[0])
            phase_C1(l, chunks=[0])
            phase_C2(l, chunks=[0])
        phase_T1('xTa', tiles=range(4))
        return finish(nc, es, P)
    if stop_after is None:
        for l in range(L):
            phase_A(l)
            phase_E(l)
            phase_B(l)
            phase_Ap(l)
            phase_C1(l)
            phase_C2(l)
        phase_T1('xTa')
        return finish(nc, es, P)
    dbg_chunks = range(NCH)
    if stop_after == 'A0c0':
        phase_A(0, chunks=[0])
        return finish(nc, es, P)
    if stop_after == 'A0c01':
        phase_A(0, chunks=[0, 1])
        return finish(nc, es, P)
    raise NotImplementedError


def finish(nc, es, P):
    P.barrier()
    P.flush()
    es.close()
    return nc


_CONSTS = None


def make_in_maps(inp):
    global _CONSTS
    if _CONSTS is None:
        _CONSTS = _host_consts()
    f = lambda a: np.ascontiguousarray(np.asarray(a, dtype=np.float32))
    shared = {k: f(inp[k]) for k in ['norm1_g', 'norm2_g', 'ada_w', 'ada_b', 'w_in', 'diff_qn_g', 'diff_kn_g',
                                     'diff_lam', 'diff_sub_g', 'mla_qa_g', 'mla_kva_g', 'mla_w_uq', 'mla_w_ukv',
                                     'mla_qn_g', 'mla_kn_g', 'na_qn_g', 'na_kn_g', 'na_bias', 'pool_w', 'pool_scale',
                                     'w_out', 'w_up', 'conv_w', 'conv_b', 'w_down']}
    shared.update(_CONSTS)
    maps = []
    for c in range(8):
        m = dict(shared)
        m['xp'] = f(inp['x_prompt'][2 * c:2 * c + 2]).reshape(NP, D)
        m['xs'] = f(inp['x_sample'][c])
        m['cdk'] = f(inp['cache_diff_k'][c]).reshape(L, PAST, 512)
        m['cdv'] = f(inp['cache_diff_v'][c]).reshape(L, PAST, 512)
        m['cckv'] = f(inp['cache_mla_ckv'][c])
        m['ckpe'] = f(inp['cache_mla_kpe'][c])
        m['cnk'] = f(inp['cache_na_k'][c]).reshape(L, PAST, 512)
        m['cnv'] = f(inp['cache_na_v'][c]).reshape(L, PAST, 512)
        m['cvec'] = np.ascontiguousarray(np.stack([f(inp['c_ctx']), f(inp['c'][c])], 0))
        maps.append(m)
    return maps


_NC = None


def kernel(**inp):
    global _NC
    if _NC is None:
        _NC = build_program()
    maps = make_in_maps(inp)
    res = run_bass_kernel_spmd(_NC, maps, core_ids=list(range(8)))
    R = res.results
    yp = np.concatenate([r['yp'].reshape(2, 256, D) for r in R], 0)
    ys = np.stack([r['ys'] for r in R], 0)
    dk = np.concatenate([r['o_dk'].reshape(2, L, 256, 4, 2, 64) for r in R], 0)
    dv = np.concatenate([r['o_dv'].reshape(2, L, 256, 4, 128) for r in R], 0)
    ckv = np.concatenate([r['o_ckv'].reshape(2, L, 256, 128) for r in R], 0)
    kpe = np.concatenate([r['o_kpe'].reshape(2, L, 256, 32) for r in R], 0)
    nk = np.concatenate([r['o_nk'].reshape(2, L, 256, 8, 64) for r in R], 0)
    nv = np.concatenate([r['o_nv'].reshape(2, L, 256, 8, 64) for r in R], 0)
    return tuple(np.ascontiguousarray(a.astype(np.float32)) for a in (yp, ys, dk, dv, ckv, kpe, nk, nv))
```

```python
import math
import os as _os
from contextlib import ExitStack
import numpy as np
import concourse.bass as bass
import concourse.mybir as mybir
from concourse.bass_utils import run_bass_kernel_spmd

F32, BF16 = mybir.dt.float32, mybir.dt.bfloat16
AF = mybir.ActivationFunctionType
ALU = mybir.AluOpType

D = 2048
FC = 16
L = 2
NP = 512
NS = 4096
NT = NP + NS
NCH = NT // 512
PAST = 256
NKEY = 256 + 256 + PAST + NS
KB = [0, 256, 512]
SEQ_T0 = [0, 256, 512]
SEQ_N = [256, 256, 4096]
SEQ_CTX = [0, 0, 256]
IN_W = 4128
DFF = 5632
EPS = 1e-6
COMPUTE = ['pe', 'act', 'dve', 'pool']
STORE_Q = 'pool'


class Prog:
    def __init__(s, nc, es):
        s.nc = nc
        s.ops = {e: [] for e in ['pe', 'act', 'dve', 'pool', 'sp']}
        s.cnt = {e: 0 for e in COMPUTE}
        s.sem = {}
        for e in COMPUTE:
            s.sem['c_' + e] = es.enter_context(nc.semaphore('c_' + e))
        s.dq = {'sp': ['d_sp%d' % i for i in range(20)], 'pool': ['d_pl%d' % i for i in range(12)]}
        s.duse = {}
        for q in s.dq:
            for n in s.dq[q]:
                s.sem[n] = es.enter_context(nc.semaphore(n))
                s.duse[n] = 0
        s.drr = {'sp': 0, 'pool': 0}
        s.waited = {e: {} for e in s.ops}
        s.lastw = {}
        s.readers = {}
        s.nins = 0

    def _deps(s, eng, reads, writes):
        need = {}

        def add(tok):
            if tok is None:
                return
            n, v = tok
            if eng == 'pe' and n == 'c_pe':
                return
            if need.get(n, 0) < v:
                need[n] = v
        for r in reads:
            add(s.lastw.get(r))
        for r in writes:
            add(s.lastw.get(r))
            for n, v in s.readers.get(r, {}).items():
                add((n, v))
        out = []
        for n, v in need.items():
            if s.waited[eng].get(n, 0) < v:
                s.waited[eng][n] = v
                out.append((n, v))
        return out

    def _commit(s, tok, reads, writes):
        for r in reads:
            d = s.readers.setdefault(r, {})
            if d.get(tok[0], 0) < tok[1]:
                d[tok[0]] = tok[1]
        for r in writes:
            s.lastw[r] = tok
            s.readers[r] = {}

    ENGMAP = {'pe': 'tensor', 'act': 'scalar', 'dve': 'vector', 'pool': 'gpsimd', 'sp': 'sync'}

    def _issue(s, e, fn, waits, inc):
        eng = getattr(s.nc, s.ENGMAP[e])
        if fn is None:
            for n, v in waits:
                eng.wait_ge(s.sem[n], v)
            return
        if e == 'pe':
            pre, emb = waits, None
        else:
            pre, emb = waits[1:], (waits[0] if waits else None)
        for n, v in pre:
            eng.wait_ge(s.sem[n], v)
        ins = fn(eng)
        if emb is not None:
            ins.wait_op(s.sem[emb[0]], emb[1], "sem-ge")
        if inc is not None:
            ins.then_inc(s.sem[inc[0]], inc[1])
        s.nins += 1

    def op(s, eng, fn, r=(), w=()):
        waits = s._deps(eng, r, w)
        s.cnt[eng] += 1
        tok = ('c_' + eng, s.cnt[eng])
        s._issue(eng, fn, waits, (tok[0], 1))
        s._commit(tok, r, w)

    def dma(s, q, out, in_, r=(), w=(), slow=False):
        if q == 'pool':
            q = STORE_Q
        waits = s._deps(q, r, w)
        names = s.dq[q]
        n = names[s.drr[q] % len(names)]
        s.drr[q] += 1
        if s.duse[n] > 0 and s.waited[q].get(n, 0) < 16 * s.duse[n]:
            s.waited[q][n] = 16 * s.duse[n]
            waits.append((n, 16 * s.duse[n]))
        s.duse[n] += 1
        tok = (n, 16 * s.duse[n])
        if slow:
            s._issue(q, lambda e: e.dma_start(out=out, in_=in_, allow_slow_non_contiguous=True), waits, (n, 16))
        else:
            s._issue(q, lambda e: e.dma_start(out=out, in_=in_), waits, (n, 16))
        s._commit(tok, r, w)

    def barrier(s):
        toks = [('c_' + e, s.cnt[e]) for e in COMPUTE if s.cnt[e] > 0]
        toks += [(n, 16 * u) for n, u in s.duse.items() if u > 0]
        for e in s.ops:
            waits = [(n, v) for n, v in toks if s.waited[e].get(n, 0) < v]
            for n, v in waits:
                s.waited[e][n] = v
            if waits:
                s._issue(e, None, waits, None)
        s.lastw.clear()
        s.readers.clear()

    def flush(s):
        pass


def _host_consts():
    c = {}
    cst = np.zeros((128, 832), np.float32)
    cst[:, 0:128] = np.eye(128)
    cst[:, 128:256] = 1.0
    for b in range(2):
        cst[b * 64:(b + 1) * 64, 256 + b * 64:256 + (b + 1) * 64] = 1.0
    R = np.zeros((128, 128), np.float32)
    for b in range(2):
        for m in range(32):
            R[b * 64 + m + 32, b * 64 + m] = -1.0
            R[b * 64 + m, b * 64 + m + 32] = 1.0
    cst[:, 384:512] = R
    Rm = np.zeros((128, 128), np.float32)
    for m in range(16):
        Rm[64 + m + 16, 64 + m] = -1.0
        Rm[64 + m, 64 + m + 16] = 1.0
    cst[:, 512:640] = Rm
    sh = np.zeros((128, 128), np.float32)
    for i in range(32):
        sh[i, 64 + i] = 1.0
    cst[:, 640:768] = sh
    for i in range(64):
        cst[i, 768 + 63 - i] = 1.0
    c['cst'] = cst
    t = np.arange(NS)
    rows = (t // 64).astype(np.float32)
    cols = (t % 64).astype(np.float32)

    def ang(dim):
        q = dim // 4
        inv = (10000.0 ** (-np.arange(q, dtype=np.float32) / q)).astype(np.float32)
        return np.concatenate([rows[:, None] * inv, cols[:, None] * inv], axis=-1).astype(np.float32)
    a64 = ang(64)
    a32 = ang(32)
    rope = np.zeros((4, 128, NS), np.float32)
    for p in range(128):
        d = p % 64
        rope[0, p] = np.cos(a64[:, d % 32])
        rope[1, p] = np.sin(a64[:, d % 32])
    rope[2, :64] = 1.0
    for i in range(32):
        rope[2, 64 + i] = np.cos(a32[:, i % 16])
        rope[3, 64 + i] = np.sin(a32[:, i % 16])
    c['rope'] = rope
    invc = np.zeros((4, NT), np.float32)
    for g, w in enumerate((2, 4, 8, 16)):
        for s in range(3):
            n = SEQ_N[s]
            tt = np.arange(n)
            lo = np.clip(tt - w // 2, 0, n)
            hi = np.clip(tt - w // 2 + w, 0, n)
            invc[g, SEQ_T0[s]:SEQ_T0[s] + n] = 1.0 / (hi - lo)
    c['invc'] = np.ascontiguousarray(np.broadcast_to(invc[None], (128, 4, NT))).astype(np.float32)
    nam = np.zeros((3, 8, 128, 512), np.float32)
    for ty, (r0, kr0) in enumerate(((0, 0), (8, 4), (56, 48))):
        for ch in range(8):
            for krl in range(2):
                kr = kr0 + 2 * ch + krl
                for qrl in range(8):
                    qr = r0 + qrl
                    rs = min(max(qr - 4, 0), 56)
                    if not (rs <= kr < rs + 8):
                        continue
                    qc = np.arange(64)
                    cs = np.clip(qc - 8, 0, 48)
                    kc = np.arange(64)
                    ok = (kc[:, None] >= cs[None, :]) & (kc[:, None] < cs[None, :] + 16)
                    nam[ty, ch, krl * 64:(krl + 1) * 64, qrl * 64:(qrl + 1) * 64] = ok
    c['nam'] = nam
    return c


NA_TYPES = ((0, 0), (8, 4), (56, 48))


def na_block_info(qb):
    r0 = qb * 8
    kr0 = min(max(r0 - 4, 0), 48)
    ty = 0 if qb == 0 else (2 if qb == 7 else 1)
    return r0, kr0, ty


def build_program(stop_after=None, dbg=False):
    nc = bass.Bass("TRN2", target_bir_lowering=False)
    es = ExitStack()

    def din(name, shape, dt=F32):
        return nc.dram_tensor(name, list(shape), dt, kind="ExternalInput").ap()

    def dout(name, shape):
        return nc.dram_tensor(name, list(shape), F32, kind="ExternalOutput").ap()

    def dscr(name, shape, dt, out=False):
        return nc.dram_tensor(name, list(shape), dt, kind="ExternalOutput" if (out and dbg) else "Internal").ap()

    I = {}
    for name, shape in [
        ('xp', (NP, D)), ('xs', (NS, D)), ('cdk', (L, PAST, 512)), ('cdv', (L, PAST, 512)),
        ('cckv', (L, PAST, 128)), ('ckpe', (L, PAST, 32)), ('cnk', (L, PAST, 512)), ('cnv', (L, PAST, 512)),
        ('cvec', (2, D)), ('norm1_g', (L, D)), ('norm2_g', (L, D)), ('ada_w', (L, D, 6 * D)), ('ada_b', (L, 6 * D)),
        ('w_in', (L, D, IN_W)), ('diff_qn_g', (L, 64)), ('diff_kn_g', (L, 64)), ('diff_lam', (L, 4, 64)),
        ('diff_sub_g', (L, 128)), ('mla_qa_g', (L, 384)), ('mla_kva_g', (L, 128)), ('mla_w_uq', (L, 384, 384)),
        ('mla_w_ukv', (L, 128, 768)), ('mla_qn_g', (L, 96)), ('mla_kn_g', (L, 96)), ('na_qn_g', (L, 64)),
        ('na_kn_g', (L, 64)), ('na_bias', (L, 8, 15, 31)), ('pool_w', (L, 4, 128, 128)), ('pool_scale', (L, 512)),
        ('w_out', (L, D, D)), ('w_up', (L, D, 2 * DFF)), ('conv_w', (L, 3, 2 * DFF)), ('conv_b', (L, 2 * DFF)),
        ('w_down', (L, DFF, D)),
        ('cst', (128, 832)), ('rope', (4, 128, NS)), ('invc', (128, 4, NT)), ('nam', (3, 8, 128, 512)),
    ]:
        I[name] = din(name, shape)
    O = {}
    for name, shape in [('yp', (NP, D)), ('ys', (NS, D)), ('o_dk', (2 * L * 256, 512)), ('o_dv', (2 * L * 256, 512)),
                        ('o_ckv', (2 * L * 256, 128)), ('o_kpe', (2 * L * 256, 32)), ('o_nk', (2 * L * 256, 512)),
                        ('o_nv', (2 * L * 256, 512))]:
        O[name] = dout(name, shape)

    S = {}
    S['xTa'] = dscr('xTa', (FC, 128, NT), F32, True)
    S['xTb'] = dscr('xTb', (FC, 128, NT), F32, True)
    S['x1T'] = dscr('x1T', (FC, 128, NT), F32, True)
    S['mixT'] = dscr('mixT', (FC, 128, NT), BF16, True)
    S['puT'] = dscr('puT', (4, 128, NT), F32, True)
    for m in 'dmn':
        S['QT' + m] = dscr('QT' + m, (128, 4, NT), BF16, True)
        S['KT' + m] = dscr('KT' + m, (128, 4, NKEY), BF16, True)
        S['V' + m] = dscr('V' + m, (NKEY // 128, 128, 512), BF16, True)
    S['WinG'] = dscr('WinG', (L, 8, 128, FC, 512), BF16)
    S['Wout'] = dscr('Wout', (L, 4, 128, FC, 512), BF16)
    S['Wup'] = dscr('Wup', (L, 22, 128, FC, 512), BF16)
    S['Wdn'] = dscr('Wdn', (L, 4, 4, 128, 11, 512), BF16)
    S['biasP'] = dscr('biasP', (8, 15, 127), F32)
    S['Emask'] = dscr('Emask', (3, 8, 128, 8, 512), BF16, True)

    P = Prog(nc, es)
    sb = {}

    uid = [0]

    def salloc(name, shape, dt, stack):
        uid[0] += 1
        t = stack.enter_context(nc.sbuf_tensor('sb%d_%s' % (uid[0], name), list(shape), dt))
        sb[name] = t
        return t

    ps = es.enter_context(nc.psum_tensor("ps", [128, 8, 512], F32))

    def PSB(b):
        return 'ps%d' % b

    cst = salloc('cst', (128, 832), F32, es)
    cstb = salloc('cstb', (128, 832), BF16, es)
    epsT = salloc('epsT', (128, 1), F32, es)
    PRM = salloc('PRM', (128, L * 2 * 6 * FC), F32, es)
    n1g = salloc('n1g', (128, L * FC), F32, es)
    n2g = salloc('n2g', (128, L * FC), F32, es)
    GV = salloc('GV', (128, L * 16), F32, es)
    CW = salloc('CW', (128, L * 4 * 88), F32, es)
    identf = cst[:, 0:128]
    identb = cstb[:, 0:128]
    onesb = cstb[:, 128:256]
    blk64b = cstb[:, 256:384]
    Rd = cst[:, 384:512]
    Rm = cst[:, 512:640]
    shiftb = cstb[:, 640:768]

    def prm(l, v, m):
        o = ((l * 2 + v) * 6 + m) * FC
        return PRM[:, o:o + FC]

    def gv(l, j, n=128):
        return GV[0:n, l * 16 + j:l * 16 + j + 1]

    def cw(l, k, t):
        o = (l * 4 + k) * 88 + t
        return CW[:, o:o + 1]

    P.dma('sp', cst[:], I['cst'][:, :], w=['cst'])
    P.op('dve', lambda e: e.tensor_copy(out=cstb[:], in_=cst[:]), r=['cst'], w=['cstb'])
    P.op('dve', lambda e: e.memset(epsT[:], EPS), w=['epsT'])
    for l in range(L):
        P.dma('sp', n1g[:, l * FC:(l + 1) * FC], I['norm1_g'][l].rearrange("(c p) -> p c", p=128), w=['n1g'], slow=True)
        P.dma('sp', n2g[:, l * FC:(l + 1) * FC], I['norm2_g'][l].rearrange("(c p) -> p c", p=128), w=['n2g'], slow=True)
        for j, (nm, n) in enumerate([('diff_qn_g', 64), ('diff_kn_g', 64), ('diff_sub_g', 128), ('mla_kva_g', 128),
                                     ('mla_qn_g', 96), ('mla_kn_g', 96), ('na_qn_g', 64), ('na_kn_g', 64)]):
            src = I[nm][l].rearrange("(p o) -> p o", o=1)
            P.dma('sp', GV[0:n, l * 16 + j:l * 16 + j + 1], src, w=['GV'], slow=True)
            if n == 64:
                P.dma('sp', GV[64:128, l * 16 + j:l * 16 + j + 1], src, w=['GV'], slow=True)
        P.dma('sp', GV[:, l * 16 + 8:l * 16 + 11], I['mla_qa_g'][l].rearrange("(c p) -> p c", p=128), w=['GV'], slow=True)
        P.dma('sp', GV[:, l * 16 + 11:l * 16 + 15], I['pool_scale'][l].rearrange("(c p) -> p c", p=128), w=['GV'], slow=True)
        P.op('dve', lambda e, l=l: e.tensor_scalar(out=GV[:, l * 16 + 2:l * 16 + 3], in0=GV[:, l * 16 + 2:l * 16 + 3],
                                                   scalar1=1.0 - (0.8 - 0.6 * math.exp(-0.3 * l)), scalar2=None,
                                                   op0=ALU.mult), r=['GV'], w=['GV'])
        for k in range(3):
            P.dma('sp', CW[:, (l * 4 + k) * 88:(l * 4 + k + 1) * 88],
                  I['conv_w'][l, k].rearrange("(c p) -> p c", p=128), w=['CW'], slow=True)
        P.dma('sp', CW[:, (l * 4 + 3) * 88:(l * 4 + 4) * 88], I['conv_b'][l].rearrange("(c p) -> p c", p=128),
              w=['CW'], slow=True)

    with ExitStack() as st:
        lamt = salloc('lamt', (128, L * 256), F32, st)
        lamw = salloc('lamw', (128, 8), F32, st)
        for l in range(L):
            P.dma('sp', lamt[:, l * 256:(l + 1) * 256],
                  I['diff_lam'][l].rearrange("a b -> (a b)").partition_broadcast(128), w=['lamt'])
        for l in range(L):
            lam_init = 0.8 - 0.6 * math.exp(-0.3 * l)
            o = l * 256
            P.op('dve', lambda e, o=o: e.tensor_tensor(out=lamt[:, o:o + 64], in0=lamt[:, o:o + 64],
                                                       in1=lamt[:, o + 64:o + 128], op=ALU.mult),
                 r=['lamt'], w=['lamt'])
            P.op('dve', lambda e, o=o: e.tensor_tensor(out=lamt[:, o + 128:o + 192], in0=lamt[:, o + 128:o + 192],
                                                       in1=lamt[:, o + 192:o + 256], op=ALU.mult),
                 r=['lamt'], w=['lamt'])
            P.op('dve', lambda e, o=o: e.reduce_sum(out=lamw[:, 0:1], in_=lamt[:, o:o + 64],
                                                    axis=mybir.AxisListType.X), r=['lamt'], w=['lamw'])
            P.op('dve', lambda e, o=o: e.reduce_sum(out=lamw[:, 1:2], in_=lamt[:, o + 128:o + 192],
                                                    axis=mybir.AxisListType.X), r=['lamt'], w=['lamw'])
            P.op('act', lambda e: e.activation(out=lamw[:, 2:4], in_=lamw[:, 0:2], func=AF.Exp),
                 r=['lamw'], w=['lamw'])
            P.op('dve', lambda e, l=l, li=lam_init: e.scalar_tensor_tensor(
                out=GV[:, l * 16 + 15:l * 16 + 16], in0=lamw[:, 3:4], scalar=-li, in1=lamw[:, 2:3],
                op0=ALU.add, op1=ALU.subtract), r=['lamw'], w=['GV'])
        P.barrier()
        P.flush()

    with ExitStack() as st:
        cT = salloc('cT', (128, FC, 2), F32, st)
        adb = salloc('adb', (128, L * 96), F32, st)
        aw = [salloc('aw%d' % i, (128, FC, 512), F32, st) for i in range(2)]
        for v in range(2):
            P.dma('sp', cT[:, :, v], I['cvec'][v].rearrange("(c p) -> p c", p=128), w=['cT'], slow=True)
        P.op('act', lambda e: e.activation(out=cT[:], in_=cT[:], func=AF.Silu), r=['cT'], w=['cT'])
        for l in range(L):
            P.dma('sp', adb[:, l * 96:(l + 1) * 96], I['ada_b'][l].rearrange("(c p) -> p c", p=128), w=['adb'], slow=True)
        gi = 0
        for l in range(L):
            awv = I['ada_w'][l].rearrange("(c p) n -> p c n", p=128)
            for g in range(24):
                slot = gi % 2
                gi += 1
                P.dma('sp', aw[slot][:], awv[:, :, g * 512:(g + 1) * 512], w=['aw%d' % slot])
                bank = g % 2
                for j in range(4):
                    for fc in range(FC):
                        P.op('pe', lambda e, slot=slot, j=j, fc=fc, bank=bank: e.matmul(
                            ps[:, bank, j * 2:j * 2 + 2], lhsT=aw[slot][:, fc, j * 128:(j + 1) * 128],
                            rhs=cT[:, fc, :], start=(fc == 0), stop=(fc == FC - 1)),
                            r=['aw%d' % slot, 'cT'], w=[PSB(bank)])
                for j in range(4):
                    ct = g * 4 + j
                    m, fc = ct // 16, ct % 16
                    for v in range(2):
                        o = ((l * 2 + v) * 6 + m) * FC + fc
                        P.op('dve', lambda e, o=o, j=j, v=v, bank=bank, l=l, ct=ct: e.tensor_tensor(
                            out=PRM[:, o:o + 1], in0=ps[:, bank, j * 2 + v:j * 2 + v + 1],
                            in1=adb[:, l * 96 + ct:l * 96 + ct + 1], op=ALU.add),
                            r=[PSB(bank), 'adb'], w=['PRM'])
        for l in range(L):
            for v in range(2):
                for (m, ng) in ((1, n1g), (4, n2g)):
                    P.op('dve', lambda e, l=l, v=v, m=m, ng=ng: e.scalar_tensor_tensor(
                        out=prm(l, v, m), in0=prm(l, v, m), scalar=1.0, in1=ng[:, l * FC:(l + 1) * FC],
                        op0=ALU.add, op1=ALU.mult), r=['PRM', 'n1g', 'n2g'], w=['PRM'])
        P.barrier()
        P.flush()

    with ExitStack() as st:
        stg = [salloc('stg%d' % i, (128, FC * 512), F32, st) for i in range(2)]
        stb = [salloc('stb%d' % i, (128, FC * 512), BF16, st) for i in range(2)]
        jobs = []
        WIN_G = [0, 512, 1024, 1536, 2080, 2592, 3104, 3616]
        for l in range(L):
            wv = I['w_in'][l].rearrange("(c p) n -> p c n", p=128)
            for g in range(8):
                jobs.append((wv[:, :, WIN_G[g]:WIN_G[g] + 512], S['WinG'][l, g], FC))
            wv = I['w_out'][l].rearrange("(c p) n -> p c n", p=128)
            for g in range(4):
                jobs.append((wv[:, :, g * 512:(g + 1) * 512], S['Wout'][l, g], FC))
            wv = I['w_up'][l].rearrange("(c p) n -> p c n", p=128)
            for g in range(22):
                jobs.append((wv[:, :, g * 512:(g + 1) * 512], S['Wup'][l, g], FC))
            wv = I['w_down'][l].rearrange("(c p) n -> p c n", p=128)
            for dg in range(4):
                for fq in range(4):
                    jobs.append((wv[:, fq * 11:(fq + 1) * 11, dg * 512:(dg + 1) * 512], S['Wdn'][l, dg, fq], 11))
        cengs = ['dve', 'act']
        def cload(i):
            src, dst, nk = jobs[i]
            slot = i % 2
            a = stg[slot][:, 0:nk * 512].rearrange("p (c n) -> p c n", n=512)
            P.dma('sp', a, src, w=['stg%d' % slot])
        cload(0)
        for i, (src, dst, nk) in enumerate(jobs):
            slot = i % 2
            b = stb[slot][:, 0:nk * 512]
            if i + 1 < len(jobs):
                cload(i + 1)
            ce = cengs[i % 2]
            if ce == 'act':
                P.op('act', lambda e, slot=slot, nk=nk: e.copy(out=stb[slot][:, 0:nk * 512],
                                                                in_=stg[slot][:, 0:nk * 512]),
                     r=['stg%d' % slot], w=['stb%d' % slot])
            else:
                P.op(ce, lambda e, slot=slot, nk=nk: e.tensor_copy(out=stb[slot][:, 0:nk * 512],
                                                                    in_=stg[slot][:, 0:nk * 512]),
                     r=['stg%d' % slot], w=['stb%d' % slot])
            P.dma('pool', dst.rearrange("p c n -> p (c n)"), b, r=['stb%d' % slot], w=['Wscr'])
        P.barrier()
        P.flush()

    with ExitStack() as st:
        xtok = [salloc('xtok%d' % i, (128, D), F32, st) for i in range(2)]
        xTt = [salloc('xTt%d' % i, (128, FC, 128), F32, st) for i in range(2)]
        def t0load(tt):
            slot = tt % 2
            t0 = tt * 128
            src = I['xp'][t0:t0 + 128, :] if t0 < NP else I['xs'][t0 - NP:t0 - NP + 128, :]
            P.dma('sp', xtok[slot][:], src, w=['xtok%d' % slot])
        t0load(0)
        for tt in range(NT // 128):
            slot = tt % 2
            t0 = tt * 128
            if tt + 1 < NT // 128:
                t0load(tt + 1)
            for q in range(4):
                bank = (tt * 4 + q) % 8
                for j in range(4):
                    fc = q * 4 + j
                    P.op('pe', lambda e, slot=slot, fc=fc, j=j, bank=bank: e.transpose(
                        out=ps[:, bank, j * 128:(j + 1) * 128], in_=xtok[slot][:, fc * 128:(fc + 1) * 128],
                        identity=identf), r=['xtok%d' % slot, 'cst'], w=[PSB(bank)])
                eng = 'act' if q % 2 else 'dve'
                if eng == 'act':
                    P.op('act', lambda e, slot=slot, q=q, bank=bank: e.copy(
                        out=xTt[slot][:, q * 4:(q + 1) * 4, :].rearrange("p c t -> p (c t)"), in_=ps[:, bank, :]),
                        r=[PSB(bank)], w=['xTt%d' % slot])
                else:
                    P.op('dve', lambda e, slot=slot, q=q, bank=bank: e.tensor_copy(
                        out=xTt[slot][:, q * 4:(q + 1) * 4, :].rearrange("p c t -> p (c t)"), in_=ps[:, bank, :]),
                        r=[PSB(bank)], w=['xTt%d' % slot])
            P.dma('pool', S['xTa'][:, :, t0:t0 + 128].rearrange("c p t -> p c t"), xTt[slot][:],
                  r=['xTt%d' % slot], w=['xTa'])
        P.barrier()
        P.flush()

    if stop_after == 'T0':
        return finish(nc, es, P)

    rr = {'b': 0, 'pool': list(range(8))}

    def nb():
        rr['b'] += 1
        return rr['pool'][rr['b'] % len(rr['pool'])]

    cnt_eng = {'i': 0}

    def copy_op(out, in_, r, w, eng=None):
        if eng is None:
            cnt_eng['i'] += 1
            eng = 'act' if cnt_eng['i'] % 2 else 'dve'
        if eng == 'act':
            P.op('act', lambda e: e.copy(out=out, in_=in_), r=r, w=w)
        else:
            P.op(eng, lambda e: e.tensor_copy(out=out, in_=in_), r=r, w=w)

    def chunk_info(c):
        if c == 0:
            return 0, [(0, 0, 256, 0), (1, 256, 256, 0)]
        return 1, [(2, 0, 512, (c - 1) * 512)]

    def rstd_from_ps(bank, M, N, nfeat, rt, rs, rtn, rsn):
        P.op('act', lambda e: e.activation(out=rt[0:M, 0:N], in_=ps[0:M, bank, 0:N], func=AF.Ln,
                                           bias=epsT[0:M, :], scale=1.0 / nfeat), r=[PSB(bank), 'epsT'], w=[rtn])
        P.op('act', lambda e: e.activation(out=rs[0:M, 0:N], in_=rt[0:M, 0:N], func=AF.Exp, scale=-0.5),
             r=[rtn], w=[rsn])

    def make_hT(T, xT, xTn, hT, hTn, l, v, which, N=512, ncols_off=0):
        o = ncols_off
        gg = prm(l, v, 1 if which == 1 else 4)
        shv = prm(l, v, 0 if which == 1 else 3)
        bank = nb()
        for fc in range(FC):
            sl = fc % 2
            P.op('act', lambda e, fc=fc, sl=sl: e.activation(out=T['sq'][sl][:, 0:N], in_=xT[:, fc, o:o + N],
                                                            func=AF.Square), r=[xTn], w=['sq%d' % sl])
            P.op('pe', lambda e, fc=fc, sl=sl: e.matmul(ps[:, bank, 0:N], lhsT=onesb, rhs=T['sq'][sl][:, 0:N],
                                                        start=(fc == 0), stop=(fc == FC - 1)),
                 r=['sq%d' % sl, 'cstb'], w=[PSB(bank)])
        rstd_from_ps(bank, 128, N, D, T['rt'], T['rsx'], 'rt', 'rsx')
        for fc in range(FC):
            sl = fc % 2
            P.op('dve', lambda e, fc=fc, sl=sl: e.scalar_tensor_tensor(
                out=T['tmp'][sl][:, 0:N], in0=xT[:, fc, o:o + N], scalar=gg[:, fc:fc + 1], in1=T['rsx'][:, 0:N],
                op0=ALU.mult, op1=ALU.mult), r=[xTn, 'rsx', 'PRM'], w=['tmp%d' % sl])
            P.op('act', lambda e, fc=fc, sl=sl: e.activation(out=hT[:, fc, o:o + N], in_=T['tmp'][sl][:, 0:N],
                                                            func=AF.Identity, bias=shv[:, fc:fc + 1], scale=1.0),
                 r=['tmp%d' % sl, 'PRM'], w=[hTn])

    def alloc_common(st):
        T = {}
        T['sq'] = [salloc('sq%d' % i, (128, 512), BF16, st) for i in range(2)]
        T['rt'] = salloc('rt', (128, 512), F32, st)
        T['rsx'] = salloc('rsx', (128, 512), F32, st)
        T['rs'] = salloc('rs', (128, 512), F32, st)
        T['tmp'] = [salloc('tmp%d' % i, (128, 512), F32, st) for i in range(2)]
        T['qn'] = [salloc('qn%d' % i, (128, 512), F32, st) for i in range(4)]
        T['t1'] = salloc('t1', (128, 512), F32, st)
        T['t2'] = salloc('t2', (128, 512), F32, st)
        T['ob'] = [salloc('ob%d' % i, (128, 512), BF16, st) for i in range(4)]
        T['obi'] = 0
        T['qni'] = 0
        return T

    def head_norm(T, src, srcn, M, N, ones_l, nfeat, gcol, rope=None, f32_out=None, f32n=None, defer=False):
        sl = T['qni'] % 2
        T['qni'] += 1
        P.op('act', lambda e: e.activation(out=T['sq'][sl][0:M, 0:N], in_=src, func=AF.Square),
             r=[srcn], w=['sq%d' % sl])
        import os as _os
        hs = int(_os.environ.get('HN_STOP', '9'))
        if hs <= 1:
            return T['ob'][0], 'ob0'
        bank = nb()
        P.op('pe', lambda e: e.matmul(ps[0:M, bank, 0:N], lhsT=ones_l, rhs=T['sq'][sl][0:M, 0:N],
                                      start=True, stop=True), r=['sq%d' % sl, 'cstb'], w=[PSB(bank)])
        if hs <= 2:
            return T['ob'][0], 'ob0'
        rstd_from_ps(bank, M, N, nfeat, T['rt'], T['rs'], 'rt', 'rs')
        if hs <= 3:
            return T['ob'][0], 'ob0'
        qi = T.get('qn4', 0) % 4
        T['qn4'] = T.get('qn4', 0) + 1
        qn = T['qn'][qi]
        qnn = 'qn%d' % qi
        if rope is None and f32_out is None:
            oi = T['obi'] % 4
            T['obi'] += 1
            ob = T['ob'][oi]
            obn = 'ob%d' % oi
            P.op('dve', lambda e: e.scalar_tensor_tensor(out=ob[0:M, 0:N], in0=src, scalar=gcol,
                                                         in1=T['rs'][0:M, 0:N], op0=ALU.mult, op1=ALU.mult),
                 r=[srcn, 'rs', 'GV'], w=[obn])
            return ob, obn
        P.op('dve', lambda e: e.scalar_tensor_tensor(out=qn[0:M, 0:N], in0=src, scalar=gcol, in1=T['rs'][0:M, 0:N],
                                                     op0=ALU.mult, op1=ALU.mult), r=[srcn, 'rs', 'GV'], w=[qnn])
        if hs <= 4:
            return T['ob'][0], 'ob0'
        if f32_out is not None:
            copy_op(f32_out, qn[0:M, 0:N], r=[qnn], w=[f32n])
        oi = T['obi'] % 4
        T['obi'] += 1
        ob = T['ob'][oi]
        obn = 'ob%d' % oi
        if rope is None:
            copy_op(ob[0:M, 0:N], qn[0:M, 0:N], r=[qnn], w=[obn])
            if hs <= 5:
                return T['ob'][0], 'ob0'
        elif defer:
            def part_b():
                Rl, Ct, St, ropen = rope
                b2 = nb()
                P.op('pe', lambda e: e.matmul(ps[0:M, b2, 0:N], lhsT=Rl, rhs=qn[0:M, 0:N], start=True, stop=True),
                     r=[qnn, 'cst'], w=[PSB(b2)])
                P.op('dve', lambda e: e.tensor_tensor(out=T['t1'][0:M, 0:N], in0=qn[0:M, 0:N], in1=Ct, op=ALU.mult),
                     r=[qnn, ropen], w=['t1'])
                P.op('dve', lambda e: e.tensor_tensor(out=T['t2'][0:M, 0:N], in0=ps[0:M, b2, 0:N], in1=St,
                                                      op=ALU.mult), r=[PSB(b2), ropen], w=['t2'])
                P.op('dve', lambda e: e.tensor_tensor(out=ob[0:M, 0:N], in0=T['t1'][0:M, 0:N], in1=T['t2'][0:M, 0:N],
                                                      op=ALU.add), r=['t1', 't2'], w=[obn])
                return ob, obn
            return part_b
        else:
            Rl, Ct, St, ropen = rope
            rm = int(_os.environ.get('ROPE_MODE', '3'))
            if rm == 0:
                copy_op(ob[0:M, 0:N], qn[0:M, 0:N], r=[qnn], w=[obn])
                return ob, obn
            b2 = nb()
            P.op('pe', lambda e: e.matmul(ps[0:M, b2, 0:N], lhsT=Rl, rhs=qn[0:M, 0:N], start=True, stop=True),
                 r=[qnn, 'cst'], w=[PSB(b2)])
            if rm == 1:
                copy_op(ob[0:M, 0:N], ps[0:M, b2, 0:N], r=[PSB(b2)], w=[obn])
                return ob, obn
            P.op('dve', lambda e: e.tensor_tensor(out=T['t1'][0:M, 0:N], in0=qn[0:M, 0:N], in1=Ct, op=ALU.mult),
                 r=[qnn, ropen], w=['t1'])
            P.op('dve', lambda e: e.tensor_tensor(out=T['t2'][0:M, 0:N], in0=ps[0:M, b2, 0:N], in1=St, op=ALU.mult),
                 r=[PSB(b2), ropen], w=['t2'])
            P.op('dve', lambda e: e.tensor_tensor(out=ob[0:M, 0:N], in0=T['t1'][0:M, 0:N], in1=T['t2'][0:M, 0:N],
                                                  op=ALU.add), r=['t1', 't2'], w=[obn])
        return ob, obn

    def transpose_out(T, src_fn, srcn, ncol, dst_fn, N):
        for tt in range(N // 128):
            bank = nb()
            off = 0
            for i in range(ncol):
                a, M = src_fn(i, tt)
                P.op('pe', lambda e, a=a, M=M, off=off: e.transpose(out=ps[:, bank, off:off + M], in_=a,
                                                                    identity=identf[0:M, 0:M]),
                     r=[srcn, 'cst'], w=[PSB(bank)])
                off += M
            sl = T['sti'] % 2
            T['sti'] += 1
            copy_op(T['st'][sl][:, 0:off], ps[:, bank, 0:off], r=[PSB(bank)], w=['st%d' % sl])
            P.dma('pool', dst_fn(tt), T['st'][sl][:, 0:off], r=['st%d' % sl], w=['stateout'])

    def mla_kv(T, l, ckvT, ckvn, kpeb, kpen, N, key0, rope, ropen=None):
        for h in range(4):
            bank = nb()
            P.op('pe', lambda e, h=h: e.matmul(ps[0:96, bank, 0:N], lhsT=T['wukvK'][:, h, :], rhs=ckvT,
                                               start=True, stop=False), r=['wukv', ckvn], w=[PSB(bank)])
            P.op('pe', lambda e: e.matmul(ps[0:96, bank, 0:N], lhsT=shiftb[0:32, 0:96], rhs=kpeb,
                                          start=False, stop=True), r=['cstb', kpen], w=[PSB(bank)])
            rp = None
            if rope is not None:
                rp = (Rm[0:96, 0:96], rope[0], rope[1], ropen)
            ob, obn = head_norm(T, ps[0:96, bank, 0:N], PSB(bank), 96, N, onesb[0:96, 0:96], 96.0, gv(l, 5, 96),
                                rope=rp)
            P.dma('pool', S['KTm'][0:96, h, key0:key0 + N], ob[0:96, 0:N], r=[obn], w=['KTm'])
        for tt in range(N // 128):
            bank = nb()
            P.op('pe', lambda e, tt=tt: e.matmul(ps[:, bank, :], lhsT=ckvT[:, tt * 128:(tt + 1) * 128],
                                                 rhs=T['wukvV'][:], start=True, stop=True),
                 r=['wukv', ckvn], w=[PSB(bank)])
            oi = T['obi'] % 4
            T['obi'] += 1
            copy_op(T['ob'][oi][:], ps[:, bank, :], r=[PSB(bank)], w=['ob%d' % oi])
            P.dma('pool', S['Vm'][(key0 + tt * 128) // 128], T['ob'][oi][:], r=['ob%d' % oi], w=['Vm'])

    def load_small_weights(T, l, st):
        T['wuq'] = salloc('wuq', (128, 3, 384), BF16, st)
        T['wukvK'] = salloc('wukvK', (128, 4, 96), BF16, st)
        T['wukvV'] = salloc('wukvV', (128, 512), BF16, st)
        T['wkpe'] = salloc('wkpe', (128, FC, 32), BF16, st)
        with ExitStack() as s2:
            a = salloc('swA', (128, 3, 384), F32, s2)
            b = salloc('swB', (128, 768), F32, s2)
            cc = salloc('swC', (128, FC, 32), F32, s2)
            P.dma('sp', a[:], I['mla_w_uq'][l].rearrange("(c p) n -> p c n", p=128), w=['swA'])
            P.dma('sp', b[:], I['mla_w_ukv'][l], w=['swB'])
            P.dma('sp', cc[:], I['w_in'][l].rearrange("(c p) n -> p c n", p=128)[:, :, 2048:2080], w=['swC'])
            P.op('dve', lambda e: e.tensor_copy(out=T['wuq'][:], in_=a[:]), r=['swA'], w=['wuq'])
            P.op('dve', lambda e: e.memset(T['wukvK'][:], 0.0), w=['wukv'])
            bv = b[:].rearrange("p (h x) -> p h x", x=192)
            P.op('dve', lambda e: e.tensor_copy(out=T['wukvK'][:, :, 0:64], in_=bv[:, :, 0:64]),
                 r=['swB'], w=['wukv'])
            P.op('dve', lambda e: e.tensor_copy(out=T['wukvV'][:].rearrange("p (h x) -> p h x", x=128),
                                                in_=bv[:, :, 64:192]), r=['swB'], w=['wukv'])
            P.op('dve', lambda e: e.tensor_copy(out=T['wkpe'][:], in_=cc[:]), r=['swC'], w=['wkpe'])
            P.barrier()
            P.flush()

    def phase_A(l, chunks=range(NCH)):
        with ExitStack() as st:
            T = alloc_common(st)
            load_small_weights(T, l, st)
            xT = salloc('xT', (128, FC, 512), F32, st)
            hT = salloc('hT', (128, FC, 512), BF16, st)
            wb = [salloc('wb%d' % i, (128, FC, 512), BF16, st) for i in range(3)]
            T['st'] = [salloc('st%d' % i, (128, 512), F32, st) for i in range(2)]
            T['sti'] = 0
            ropeT = salloc('ropeT', (128, 4, 512), F32, st)
            kst = salloc('kst', (128, 4, 512), F32, st)
            cqn = salloc('cqn', (128, 3, 512), BF16, st)
            ckvb = salloc('ckvb', (128, 512), BF16, st)
            ckvf = salloc('ckvf', (128, 512), F32, st)
            kpef = salloc('kpef', (32, 512), F32, st)
            kpeb = salloc('kpeb', (32, 512), BF16, st)
            puf = [salloc('puf%d' % i, (128, 512), F32, st) for i in range(2)]
            xsrc = 'xTa' if l % 2 == 0 else 'xTb'
            ctok = [salloc('ctok%d' % i, (128, 512), F32, st) for i in range(2)]
            ci = 0
            for (src, KT, Vn_) in ((I['cdk'], 'KTd', None), (I['cnk'], 'KTn', None)):
                for tt in range(2):
                    sl = ci % 2
                    ci += 1
                    P.dma('sp', ctok[sl][:], src[l, tt * 128:(tt + 1) * 128, :], w=['ctok%d' % sl])
                    for h in range(4):
                        bank = nb()
                        P.op('pe', lambda e, sl=sl, h=h, bank=bank: e.transpose(
                            out=ps[:, bank, 0:128], in_=ctok[sl][:, h * 128:(h + 1) * 128], identity=identf),
                            r=['ctok%d' % sl, 'cst'], w=[PSB(bank)])
                        oi = T['obi'] % 4
                        T['obi'] += 1
                        copy_op(T['ob'][oi][:, 0:128], ps[:, bank, 0:128], r=[PSB(bank)], w=['ob%d' % oi])
                        P.dma('pool', S[KT][:, h, 512 + tt * 128:512 + (tt + 1) * 128], T['ob'][oi][:, 0:128],
                              r=['ob%d' % oi], w=[KT])
            for (src, Vn_) in ((I['cdv'], 'Vd'), (I['cnv'], 'Vn')):
                for tt in range(2):
                    sl = ci % 2
                    ci += 1
                    P.dma('sp', ctok[sl][:], src[l, tt * 128:(tt + 1) * 128, :], w=['ctok%d' % sl])
                    oi = T['obi'] % 4
                    T['obi'] += 1
                    copy_op(T['ob'][oi][:], ctok[sl][:], r=['ctok%d' % sl], w=['ob%d' % oi])
                    P.dma('pool', S[Vn_][4 + tt], T['ob'][oi][:], r=['ob%d' % oi], w=[Vn_])
            for tt in range(2):
                sl = ci % 2
                ci += 1
                P.dma('sp', ctok[sl][:, 0:128], I['cckv'][l, tt * 128:(tt + 1) * 128, :], w=['ctok%d' % sl])
                P.dma('sp', ctok[sl][:, 128:160], I['ckpe'][l, tt * 128:(tt + 1) * 128, :], w=['ctok%d' % sl])
                bank = nb()
                P.op('pe', lambda e, sl=sl, bank=bank: e.transpose(out=ps[:, bank, 0:128], in_=ctok[sl][:, 0:128],
                                                                   identity=identf),
                     r=['ctok%d' % sl, 'cst'], w=[PSB(bank)])
                copy_op(ckvb[:, tt * 128:(tt + 1) * 128], ps[:, bank, 0:128], r=[PSB(bank)], w=['ckvb'])
                bank = nb()
                P.op('pe', lambda e, sl=sl, bank=bank: e.transpose(out=ps[0:32, bank, 0:128],
                                                                   in_=ctok[sl][:, 128:160], identity=identf),
                     r=['ctok%d' % sl, 'cst'], w=[PSB(bank)])
                copy_op(kpeb[:, tt * 128:(tt + 1) * 128], ps[0:32, bank, 0:128], r=[PSB(bank)], w=['kpeb'])
            mla_kv(T, l, ckvb[:, 0:256], 'ckvb', kpeb[:, 0:256], 'kpeb', 256, 512, None)

            stream = [(c, g) for c in chunks for g in range(8)]
            if _os.environ.get('A_PARTS') == '1':
                stream = []
            if _os.environ.get('A_GROUPS'):
                stream = [(c, g) for c in chunks for g in range(8) if str(g) in _os.environ['A_GROUPS']]
            loaded = {}

            def wload(i):
                c, g = stream[i]
                sl = i % 3
                P.dma('sp', wb[sl][:], S['WinG'][l, g], r=['Wscr'], w=['wb%d' % sl])
                loaded[i] = sl
            for i in range(min(2, len(stream))):
                wload(i)
            for i, (c, g) in enumerate(stream):
                if i + 2 < len(stream):
                    wload(i + 2)
                sl = loaded[i]
                W = wb[sl]
                Wn = 'wb%d' % sl
                v, segs = chunk_info(c)
                tok0 = c * 512
                key0 = tok0 + (256 if c > 0 else 0)
                is_s = c > 0
                if i == 0 or stream[i - 1][0] != c:
                    P.dma('sp', xT[:], S[xsrc][:, :, tok0:tok0 + 512].rearrange("c p t -> p c t"),
                          r=[xsrc], w=['xT'])
                    if is_s:
                        t0l = (c - 1) * 512
                        P.dma('sp', ropeT[:], I['rope'][:, :, t0l:t0l + 512].rearrange("k p t -> p k t"),
                              w=['ropeT'])
                    make_hT(T, xT, 'xT', hT, 'hT', l, v, 1)

                def fm_tile(j, M=128, W=W, Wn=Wn):
                    bank = nb()
                    for fc in range(FC):
                        P.op('pe', lambda e, fc=fc: e.matmul(ps[0:M, bank, :], lhsT=W[:, fc, j * 128:j * 128 + M],
                                                             rhs=hT[:, fc, :], start=(fc == 0), stop=(fc == FC - 1)),
                             r=[Wn, 'hT'], w=[PSB(bank)])
                    return bank

                def tm_group(Vn_, onm, W=W, Wn=Wn):
                    for tt in range(4):
                        bank = nb()
                        for fc in range(FC):
                            P.op('pe', lambda e, fc=fc, tt=tt: e.matmul(
                                ps[:, bank, :], lhsT=hT[:, fc, tt * 128:(tt + 1) * 128], rhs=W[:, fc, :],
                                start=(fc == 0), stop=(fc == FC - 1)), r=[Wn, 'hT'], w=[PSB(bank)])
                        oi = T['obi'] % 4
                        T['obi'] += 1
                        copy_op(T['ob'][oi][:], ps[:, bank, :], r=[PSB(bank)], w=['ob%d' % oi], eng='dve')
                        P.dma('pool', S[Vn_][(key0 + tt * 128) // 128], T['ob'][oi][:], r=['ob%d' % oi], w=[Vn_])
                        if (not is_s) and _os.environ.get('NO_VOUT') != '1':
                            sl2 = T['sti'] % 2
                            T['sti'] += 1
                            copy_op(T['st'][sl2][:], ps[:, bank, :], r=[PSB(bank)], w=['st%d' % sl2], eng='dve')
                            b_, t_ = tt // 2, (tt % 2) * 128
                            if _os.environ.get('VOUT_MODE') == 'scratch':
                                P.dma('pool', S['puT'][0, :, 0:512], T['st'][sl2][:], r=['st%d' % sl2], w=['stateout'])
                            elif _os.environ.get('VOUT_MODE') != 'copyonly':
                                P.dma('pool', O[onm][(b_ * L + l) * 256 + t_:(b_ * L + l) * 256 + t_ + 128, :], T['st'][sl2][:], r=['st%d' % sl2],
                                      w=['stateout'])

                if g in (0, 1, 4, 5):
                    isq = g in (0, 4)
                    isd = g in (0, 1)
                    gcol = gv(l, {0: 0, 1: 1, 4: 6, 5: 7}[g])
                    dst = {0: 'QTd', 1: 'KTd', 4: 'QTn', 5: 'KTn'}[g]
                    gbanks = [fm_tile(j) for j in range(4)]
                    res_ = []
                    for j in range(4):
                        bank = gbanks[j]
                        rp = None
                        if isd and is_s:
                            rp = (Rd, ropeT[:, 0, :], ropeT[:, 1, :], 'ropeT')
                        f32o = None
                        if (not is_s) and (not isq):
                            f32o = kst[:, j, :]
                        res_.append(head_norm(T, ps[:, bank, :], PSB(bank), 128, 512, blk64b, 64.0, gcol, rope=rp,
                                              f32_out=f32o, f32n='kst', defer=True))
                    for j in range(4):
                        rj = res_[j]
                        ob, obn = rj() if callable(rj) else rj
                        if isq:
                            P.dma('pool', S[dst][:, j, tok0:tok0 + 512], ob[:], r=[obn], w=[dst])
                        else:
                            P.dma('pool', S[dst][:, j, key0:key0 + 512], ob[:], r=[obn], w=[dst])
                    if (not is_s) and (not isq):
                        onm = 'o_dk' if isd else 'o_nk'
                        transpose_out(T, lambda i, tt: (kst[:, i, tt * 128:(tt + 1) * 128], 128), 'kst', 4,
                                      lambda tt: O[onm][((tt // 2) * L + l) * 256 + (tt % 2) * 128:((tt // 2) * L + l) * 256 + (tt % 2) * 128 + 128, :], 512)
                elif g == 2:
                    tm_group('Vd', 'o_dv')
                elif g == 6:
                    tm_group('Vn', 'o_nv')
                elif g == 7:
                    for j in range(4):
                        bank = fm_tile(j)
                        sl2 = j % 2
                        copy_op(puf[sl2][:], ps[:, bank, :], r=[PSB(bank)], w=['puf%d' % sl2])
                        P.dma('pool', S['puT'][j, :, tok0:tok0 + 512], puf[sl2][:], r=['puf%d' % sl2], w=['puT'])
                elif g == 3:
                    banks = [fm_tile(j) for j in range(3)]
                    sb_ = nb()
                    for j in range(3):
                        sl2 = j % 2
                        P.op('act', lambda e, j=j, sl2=sl2: e.activation(out=T['sq'][sl2][:], in_=ps[:, banks[j], :],
                                                                         func=AF.Square),
                             r=[PSB(banks[j])], w=['sq%d' % sl2])
                        P.op('pe', lambda e, j=j, sl2=sl2: e.matmul(ps[:, sb_, :], lhsT=onesb, rhs=T['sq'][sl2][:],
                                                                    start=(j == 0), stop=(j == 2)),
                             r=['sq%d' % sl2, 'cstb'], w=[PSB(sb_)])
                    rstd_from_ps(sb_, 128, 512, 384.0, T['rt'], T['rs'], 'rt', 'rs')
                    for j in range(3):
                        P.op('dve', lambda e, j=j: e.scalar_tensor_tensor(
                            out=cqn[:, j, :], in0=ps[:, banks[j], :], scalar=GV[:, l * 16 + 8 + j:l * 16 + 9 + j],
                            in1=T['rs'][:], op0=ALU.mult, op1=ALU.mult), r=[PSB(banks[j]), 'rs', 'GV'], w=['cqn'])
                    for h in range(4):
                        bank = nb()
                        for j in range(3):
                            P.op('pe', lambda e, h=h, j=j: e.matmul(ps[0:96, bank, :],
                                                                    lhsT=T['wuq'][:, j, h * 96:(h + 1) * 96],
                                                                    rhs=cqn[:, j, :], start=(j == 0), stop=(j == 2)),
                                 r=['wuq', 'cqn'], w=[PSB(bank)])
                        rp = (Rm[0:96, 0:96], ropeT[0:96, 2, :], ropeT[0:96, 3, :], 'ropeT') if is_s else None
                        ob, obn = head_norm(T, ps[0:96, bank, :], PSB(bank), 96, 512, onesb[0:96, 0:96], 96.0,
                                            gv(l, 4, 96), rope=rp)
                        P.dma('pool', S['QTm'][0:96, h, tok0:tok0 + 512], ob[0:96, :], r=[obn], w=['QTm'])
                    bank = fm_tile(3)
                    sl2 = T['qni'] % 2
                    T['qni'] += 1
                    P.op('act', lambda e: e.activation(out=T['sq'][sl2][:], in_=ps[:, bank, :], func=AF.Square),
                         r=[PSB(bank)], w=['sq%d' % sl2])
                    b2 = nb()
                    P.op('pe', lambda e: e.matmul(ps[:, b2, :], lhsT=onesb, rhs=T['sq'][sl2][:], start=True, stop=True),
                         r=['sq%d' % sl2, 'cstb'], w=[PSB(b2)])
                    rstd_from_ps(b2, 128, 512, 128.0, T['rt'], T['rs'], 'rt', 'rs')
                    P.op('dve', lambda e: e.scalar_tensor_tensor(out=ckvf[:], in0=ps[:, bank, :], scalar=gv(l, 3),
                                                                 in1=T['rs'][:], op0=ALU.mult, op1=ALU.mult),
                         r=[PSB(bank), 'rs', 'GV'], w=['ckvf'])
                    copy_op(ckvb[:], ckvf[:], r=['ckvf'], w=['ckvb'])
                    bank = nb()
                    for fc in range(FC):
                        P.op('pe', lambda e, fc=fc: e.matmul(ps[0:32, bank, :], lhsT=T['wkpe'][:, fc, :],
                                                             rhs=hT[:, fc, :], start=(fc == 0), stop=(fc == FC - 1)),
                             r=['wkpe', 'hT'], w=[PSB(bank)])
                    copy_op(kpef[:], ps[0:32, bank, :], r=[PSB(bank)], w=['kpef'], eng='dve')
                    copy_op(kpeb[:], ps[0:32, bank, :], r=[PSB(bank)], w=['kpeb'], eng='dve')
                    if not is_s:
                        transpose_out(T, lambda i, tt: (ckvf[:, tt * 128:(tt + 1) * 128], 128), 'ckvf', 1,
                                      lambda tt: O['o_ckv'][((tt // 2) * L + l) * 256 + (tt % 2) * 128:((tt // 2) * L + l) * 256 + (tt % 2) * 128 + 128, :], 512)
                        transpose_out(T, lambda i, tt: (kpef[:, tt * 128:(tt + 1) * 128], 32), 'kpef', 1,
                                      lambda tt: O['o_kpe'][((tt // 2) * L + l) * 256 + (tt % 2) * 128:((tt // 2) * L + l) * 256 + (tt % 2) * 128 + 128, :], 512)
                    mla_kv(T, l, ckvb[:], 'ckvb', kpeb[:], 'kpeb', 512, key0,
                           (ropeT[0:96, 2, :], ropeT[0:96, 3, :]) if is_s else None, 'ropeT')
            P.barrier()
            P.flush()

    def phase_E(l):
        with ExitStack() as st:
            zt = salloc('zt', (15, 127), F32, st)
            Hk = [salloc('Hk%d' % i, (64, 15, 64), F32, st) for i in range(2)]
            ETr = salloc('ETr', (128, 8, 31 * 64), F32, st)
            mk = [salloc('mk%d' % i, (128, 512), F32, st) for i in range(2)]
            eb = [salloc('eb%d' % i, (128, 512), BF16, st) for i in range(2)]
            Jx = cst[0:64, 768:832]
            P.op('dve', lambda e: e.memset(zt[:], 0.0), w=['zt'])
            P.op('dve', lambda e: e.memset(ETr[:], 0.0), w=['ETr'])
            for h in range(8):
                P.dma('sp', S['biasP'][h], zt[:], r=['zt'], w=['biasP'])
                P.dma('sp', S['biasP'][h, :, 48:79], I['na_bias'][l, h], r=[], w=['biasP'])
            bp = S['biasP']
            for h in range(8):
                sl = h % 2
                src = bass.AP(tensor=bp.tensor, offset=bp.offset + h * 15 * 127, ap=[[1, 64], [127, 15], [1, 64]])
                P.dma('sp', Hk[sl][:], src, r=['biasP'], w=['Hk%d' % sl])
                b0 = 1 + 2 * sl
                for half in range(2):
                    for i2 in range(8 + half, 23 + half):
                        dr = 15 - i2 + half
                        r_ = dr + 7
                        bank = b0 + (i2 - 8) // 8
                        col = ((i2 - 8) % 8) * 64
                        P.op('pe', lambda e, half=half, r_=r_, bank=bank, col=col, sl=sl: e.matmul(
                            ps[half * 64:(half + 1) * 64, bank, col:col + 64], lhsT=Hk[sl][:, r_, :], rhs=Jx,
                            start=True, stop=True), r=['Hk%d' % sl, 'cst'], w=[PSB(bank)])
                for half in range(2):
                    p0 = half * 64
                    lo1, hi1 = 8 + half, 16
                    lo2, hi2 = 16, 23 + half
                    P.op('act', lambda e, p0=p0, lo1=lo1, hi1=hi1, b0=b0, h=h: e.activation(
                        out=ETr[p0:p0 + 64, h, lo1 * 64:hi1 * 64], in_=ps[p0:p0 + 64, b0, (lo1 - 8) * 64:(hi1 - 8) * 64],
                        func=AF.Exp), r=[PSB(b0)], w=['ETr'])
                    P.op('act', lambda e, p0=p0, lo2=lo2, hi2=hi2, b0=b0, h=h: e.activation(
                        out=ETr[p0:p0 + 64, h, lo2 * 64:hi2 * 64],
                        in_=ps[p0:p0 + 64, b0 + 1, (lo2 - 16) * 64:(hi2 - 16) * 64],
                        func=AF.Exp), r=[PSB(b0 + 1)], w=['ETr'])
            k = 0
            for ty, (r0, kr0) in enumerate(NA_TYPES):
                off = kr0 - r0
                for ch in range(8):
                    ms = (ty * 8 + ch) % 2
                    P.dma('sp', mk[ms][:], I['nam'][ty, ch], w=['mk%d' % ms])
                    i0 = 15 - (2 * ch + off)
                    for h in range(8):
                        sl = k % 2
                        k += 1
                        eng = 'dve'
                        P.op(eng, lambda e, sl=sl, ms=ms, h=h, i0=i0: e.tensor_tensor(
                            out=eb[sl][:], in0=ETr[:, h, i0 * 64:(i0 + 8) * 64], in1=mk[ms][:], op=ALU.mult),
                            r=['ETr', 'mk%d' % ms], w=['eb%d' % sl])
                        P.dma('pool', S['Emask'][ty, h, :, ch, :], eb[sl][:], r=['eb%d' % sl], w=['Emask'])
            P.barrier()

    def phase_B(l, seqs=(0, 1, 2)):
        with ExitStack() as st:
            T = alloc_common(st)
            KT = salloc('KT', (128, 4, PAST + NS), BF16, st)
            Vt = salloc('Vt', (128, (PAST + NS) // 128, 512), BF16, st)
            QT = [salloc('QT%d' % i, (128, 4, 512), BF16, st) for i in range(2)]
            pb = [salloc('pb%d' % i, (128, 512), BF16, st) for i in range(6)]
            rinv = [salloc('rinv%d' % i, (128, 512), F32, st) for i in range(2)]
            of = [salloc('of%d' % i, (128, 512), F32, st) for i in range(2)]
            Et = [salloc('Et%d' % i, (128, 8, 512), BF16, st) for i in range(2)]
            af = [salloc('af%d' % i, (128, 512), F32, st) for i in range(2)]
            onesf = cst[:, 128:256]
            rr['pool'] = [3]
            cn = {'s': 0, 'p': 0, 'q': 0, 'e': 0}

            LA = 3
            pend = []

            def emit_S(u):
                if u.get('pre') is not None:
                    u['pre']()
                cn['s'] += 1
                bs = cn['s'] % 4
                nq = u['nq']
                P.op('pe', lambda e: e.matmul(ps[:, bs, 0:nq], lhsT=u['kt'], rhs=u['q'], start=True, stop=True),
                     r=['KT', u['qn']], w=[PSB(bs)])
                cn['p'] += 1
                pi = cn['p'] % 6
                u['pi'] = pi
                P.op('act', lambda e: e.activation(out=pb[pi][:, 0:nq], in_=ps[:, bs, 0:nq], func=AF.Exp,
                                                   scale=u['scale']), r=[PSB(bs)], w=['pb%d' % pi])
                if u.get('emul') is not None:
                    eap, en = u['emul']
                    P.op('dve', lambda e: e.tensor_tensor(out=pb[pi][:, 0:nq], in0=pb[pi][:, 0:nq], in1=eap,
                                                          op=ALU.mult), r=['pb%d' % pi, en], w=['pb%d' % pi])

            def emit_PV(u):
                pi = u['pi']
                nq = u['nq']
                P.op('pe', lambda e: e.matmul(u['o'], lhsT=u['v'], rhs=pb[pi][:, 0:nq], start=u['first'],
                                              stop=u['last']), r=['Vt', 'pb%d' % pi], w=[PSB(u['ob'])])
                ab = u['accb']
                if u['first']:
                    P.op('dve', lambda e: e.tensor_copy(out=ps[:, ab, 0:nq], in_=pb[pi][:, 0:nq]),
                         r=['pb%d' % pi], w=[PSB(ab)])
                else:
                    P.op('dve', lambda e: e.tensor_tensor(out=ps[:, ab, 0:nq], in0=ps[:, ab, 0:nq],
                                                          in1=pb[pi][:, 0:nq], op=ALU.add),
                         r=['pb%d' % pi, PSB(ab)], w=[PSB(ab)])
                if u.get('post') is not None:
                    u['post']()

            def push(u):
                emit_S(u)
                pend.append(u)
                if len(pend) > LA:
                    emit_PV(pend.pop(0))

            def drain():
                while pend:
                    emit_PV(pend.pop(0))

            def mk(kt, q, v, nq, scale, o, s_, ones_, first, last, ob_, sb_, qn, emul=None, accb=None):
                return dict(kt=kt, q=q, v=v, nq=nq, scale=scale, o=o, s=s_, ones=ones_, first=first, last=last,
                            ob=ob_, sb=sb_, qn=qn, emul=emul, pre=None, post=None,
                            accb=(sb_ if accb is None else accb))

            def fin_sum(accb, nq, outs):
                fi = cn.get('af', 0) % 2
                cn['af'] = cn.get('af', 0) + 1
                P.op('dve', lambda e: e.tensor_copy(out=af[fi][:, 0:nq], in_=ps[:, accb, 0:nq]),
                     r=[PSB(accb)], w=['af%d' % fi])
                for (o_ap, l_ap, bk) in outs:
                    P.op('pe', lambda e, o_ap=o_ap, l_ap=l_ap: e.matmul(o_ap, lhsT=l_ap, rhs=af[fi][:, 0:nq],
                                                                        start=True, stop=True),
                         r=['af%d' % fi, 'cst'], w=[PSB(bk)])

            for m in _os.environ.get('B_MIX', 'dmn'):
                for s_ in seqs:
                    n = SEQ_N[s_]
                    nk = SEQ_CTX[s_] + n
                    nkc = nk // 128
                    kb = KB[s_]
                    Mk = 96 if m == 'm' else 128
                    drain()
                    P.dma('sp', KT[0:Mk, :, 0:nk], S['KT' + m][0:Mk, :, kb:kb + nk], r=['KT' + m], w=['KT'])
                    P.dma('sp', Vt[:, 0:nkc, :], S['V' + m][kb // 128:kb // 128 + nkc].rearrange("c p e -> p c e"),
                          r=['V' + m], w=['Vt'])
                    nq = min(512, n)
                    for qb in range(n // nq):
                        tok0 = SEQ_T0[s_] + qb * nq
                        cn['q'] += 1
                        qs = cn['q'] % 2
                        Q = QT[qs]
                        qnm = 'QT%d' % qs

                        def load_q(Q=Q, qs=qs, tok0=tok0, Mk=Mk, nq=nq, m=m):
                            P.dma('sp', Q[0:Mk, :, 0:nq], S['QT' + m][0:Mk, :, tok0:tok0 + nq], r=['QT' + m],
                                  w=['QT%d' % qs])
                        first_unit = True
                        if m == 'd':
                            sc = 64.0 ** -0.5
                            for h in range(4):
                                us = []
                                for kc in range(nkc):
                                    for j in range(2):
                                        us.append(mk(KT[j * 64:(j + 1) * 64, h, kc * 128:(kc + 1) * 128],
                                                     Q[j * 64:(j + 1) * 64, h, 0:nq], Vt[:, kc, h * 128:(h + 1) * 128],
                                                     nq, sc, ps[:, 4 + j, 0:nq], ps[:, 6 + j, 0:nq], onesb, kc == 0,
                                                     kc == nkc - 1, 4 + j, 6 + j, qnm))

                                def epi(h=h, tok0=tok0, nq=nq):
                                    for j in range(2):
                                        fin_sum(6 + j, nq, [(ps[:, 6 + j, 0:nq], onesf, 6 + j)])
                                    for j in range(2):
                                        P.op('act', lambda e, j=j: e.activation(out=rinv[j][:, 0:nq],
                                                                                in_=ps[:, 6 + j, 0:nq], func=AF.Ln),
                                             r=[PSB(6 + j)], w=['rinv%d' % j])
                                        P.op('act', lambda e, j=j: e.activation(out=rinv[j][:, 0:nq],
                                                                                in_=rinv[j][:, 0:nq], func=AF.Exp,
                                                                                scale=-1.0),
                                             r=['rinv%d' % j], w=['rinv%d' % j])
                                        P.op('dve', lambda e, j=j: e.tensor_tensor(
                                            out=of[j][:, 0:nq], in0=ps[:, 4 + j, 0:nq], in1=rinv[j][:, 0:nq],
                                            op=ALU.mult), r=[PSB(4 + j), 'rinv%d' % j], w=['of%d' % j])
                                    P.op('dve', lambda e: e.scalar_tensor_tensor(
                                        out=of[0][:, 0:nq], in0=of[1][:, 0:nq], scalar=gv(l, 15), in1=of[0][:, 0:nq],
                                        op0=ALU.mult, op1=ALU.add), r=['of0', 'of1', 'GV'], w=['of0'])
                                    ob, obn = head_norm(T, of[0][:, 0:nq], 'of0', 128, nq, onesb, 128.0, gv(l, 2))
                                    P.dma('pool', S['mixT'][h, :, tok0:tok0 + nq], ob[:, 0:nq], r=[obn], w=['mixT'])
                                us[-1]['post'] = epi
                                if first_unit:
                                    us[0]['pre'] = load_q
                                    first_unit = False
                                for u in us:
                                    push(u)
                        elif m == 'm':
                            sc = 96.0 ** -0.5
                            for h in range(4):
                                ob_, sb_ = 4 + h % 2, 6 + h % 2
                                us = []
                                for kc in range(nkc):
                                    us.append(mk(KT[0:96, h, kc * 128:(kc + 1) * 128], Q[0:96, h, 0:nq],
                                                 Vt[:, kc, h * 128:(h + 1) * 128], nq, sc, ps[:, ob_, 0:nq],
                                                 ps[:, sb_, 0:nq], onesb, kc == 0, kc == nkc - 1, ob_, sb_, qnm))

                                def epi(h=h, tok0=tok0, nq=nq, ob_=ob_, sb_=sb_):
                                    fin_sum(sb_, nq, [(ps[:, sb_, 0:nq], onesf, sb_)])
                                    P.op('act', lambda e: e.activation(out=rinv[0][:, 0:nq], in_=ps[:, sb_, 0:nq],
                                                                       func=AF.Ln), r=[PSB(sb_)], w=['rinv0'])
                                    P.op('act', lambda e: e.activation(out=rinv[0][:, 0:nq], in_=rinv[0][:, 0:nq],
                                                                       func=AF.Exp, scale=-1.0),
                                         r=['rinv0'], w=['rinv0'])
                                    oi = T['obi'] % 4
                                    T['obi'] += 1
                                    P.op('dve', lambda e: e.tensor_tensor(out=T['ob'][oi][:, 0:nq], in0=ps[:, ob_, 0:nq],
                                                                          in1=rinv[0][:, 0:nq], op=ALU.mult),
                                         r=[PSB(ob_), 'rinv0'], w=['ob%d' % oi])
                                    P.dma('pool', S['mixT'][4 + h, :, tok0:tok0 + nq], T['ob'][oi][:, 0:nq],
                                          r=['ob%d' % oi], w=['mixT'])
                                us[-1]['post'] = epi
                                if first_unit:
                                    us[0]['pre'] = load_q
                                    first_unit = False
                                for u in us:
                                    push(u)
                        else:
                            sc = 64.0 ** -0.5
                            if s_ == 2:
                                r0, kr0, ty = na_block_info(qb)
                                kcs = [(0, None), (1, None)] + [(2 + kr0 // 2 + ch, ch) for ch in range(8)]
                            else:
                                ty = 0
                                kcs = [(kc, None) for kc in range(nkc)]
                            for i in range(4):
                                ob_, sb_ = 4 + i % 2, 6
                                for hh in range(2):
                                    h = 2 * i + hh
                                    po = hh * 64
                                    us = []
                                    es_ = None
                                    if s_ == 2:
                                        cn['e'] += 1
                                        es_ = cn['e'] % 2
                                    for ki, (kc, ch) in enumerate(kcs):
                                        em = None
                                        if ch is not None:
                                            em = (Et[es_][:, ch, 0:nq], 'Et%d' % es_)
                                        us.append(mk(KT[po:po + 64, i, kc * 128:(kc + 1) * 128], Q[po:po + 64, i, 0:nq],
                                                     Vt[:, kc, h * 64:(h + 1) * 64], nq, sc, ps[po:po + 64, ob_, 0:nq],
                                                     ps[po:po + 64, sb_, 0:nq], onesb[:, 0:64], ki == 0,
                                                     ki == len(kcs) - 1, ob_, sb_, qnm, emul=em, accb=6 + hh))
                                    pres = []
                                    if first_unit:
                                        pres.append(load_q)
                                        first_unit = False
                                    if s_ == 2:
                                        def load_e(es_=es_, ty=ty, h=h):
                                            P.dma('sp', Et[es_][:], S['Emask'][ty, h], r=['Emask'], w=['Et%d' % es_])
                                        pres.append(load_e)
                                    if pres:
                                        us[0]['pre'] = (lambda pres=pres: [f() for f in pres])
                                    if hh == 1:
                                        def epi(i=i, tok0=tok0, nq=nq, ob_=ob_, sb_=sb_):
                                            fin_sum(6, nq, [(ps[0:64, 3, 0:nq], onesf[:, 0:64], 3)])
                                            fin_sum(7, nq, [(ps[64:128, 3, 0:nq], onesf[:, 0:64], 3)])
                                            sb_ = 3
                                            P.op('act', lambda e: e.activation(out=rinv[0][:, 0:nq],
                                                                               in_=ps[:, sb_, 0:nq], func=AF.Ln),
                                                 r=[PSB(sb_)], w=['rinv0'])
                                            P.op('act', lambda e: e.activation(out=rinv[0][:, 0:nq],
                                                                               in_=rinv[0][:, 0:nq], func=AF.Exp,
                                                                               scale=-1.0),
                                                 r=['rinv0'], w=['rinv0'])
                                            oi = T['obi'] % 4
                                            T['obi'] += 1
                                            P.op('dve', lambda e: e.tensor_tensor(
                                                out=T['ob'][oi][:, 0:nq], in0=ps[:, ob_, 0:nq], in1=rinv[0][:, 0:nq],
                                                op=ALU.mult), r=[PSB(ob_), 'rinv0'], w=['ob%d' % oi])
                                            P.dma('pool', S['mixT'][8 + i, :, tok0:tok0 + nq], T['ob'][oi][:, 0:nq],
                                                  r=['ob%d' % oi], w=['mixT'])
                                        us[-1]['post'] = epi
                                    for u in us:
                                        push(u)
            drain()
            rr['pool'] = list(range(8))
            P.barrier()

    def phase_Ap(l, chunks=range(NCH)):
        with ExitStack() as st:
            pus = [salloc('pus%d' % i, (128, 528), F32, st) for i in range(2)]
            A = salloc('pA', (128, 528), F32, st)
            B = salloc('pB', (128, 528), F32, st)
            ivc = [salloc('ivc%d' % i, (128, 512), F32, st) for i in range(2)]
            db = [salloc('pdb%d' % i, (128, 512), BF16, st) for i in range(2)]
            yb = [salloc('pyb%d' % i, (128, 512), BF16, st) for i in range(2)]
            pwf = salloc('pwf', (128, 4, 128), F32, st)
            pw = salloc('pw', (128, 4, 128), BF16, st)
            P.dma('sp', pwf[:], I['pool_w'][l].rearrange("g c e -> c g e"), w=['pwf'])
            P.op('dve', lambda e: e.tensor_copy(out=pw[:], in_=pwf[:]), r=['pwf'], w=['pw'])
            k = 0
            for c in chunks:
                v, segs = chunk_info(c)
                for (s_, col0, n, tl0) in segs:
                    tok0 = c * 512 + col0
                    hasl = tl0 > 0
                    hasr = tl0 + n < SEQ_N[s_]
                    for g, w_ in enumerate((2, 4, 8, 16)):
                        sl = k % 2
                        k += 1
                        u = pus[sl]
                        un = 'pus%d' % sl
                        lo = 8 if not hasl else 0
                        hi = n + 8 if not hasr else n + 16
                        if not hasl:
                            P.op('dve', lambda e, u=u: e.memset(u[:, 0:8], 0.0), w=[un])
                        if not hasr:
                            P.op('dve', lambda e, u=u: e.memset(u[:, n + 8:n + 16], 0.0), w=[un])
                        P.dma('sp', u[:, lo:hi], S['puT'][g, :, tok0 - 8 + lo:tok0 - 8 + hi], r=['puT'], w=[un])
                        P.dma('sp', ivc[sl][:, 0:n], I['invc'][:, g, tok0:tok0 + n], w=['ivc%d' % sl])
                        P.op('dve', lambda e, u=u: e.tensor_tensor(out=A[:, 0:n + 15], in0=u[:, 0:n + 15],
                                                                   in1=u[:, 1:n + 16], op=ALU.add), r=[un], w=['pA'])
                        cur, curn, o = A, 'pA', 7
                        if w_ >= 4:
                            P.op('dve', lambda e: e.tensor_tensor(out=B[:, 0:n + 13], in0=A[:, 0:n + 13],
                                                                  in1=A[:, 2:n + 15], op=ALU.add), r=['pA'], w=['pB'])
                            cur, curn, o = B, 'pB', 6
                        if w_ >= 8:
                            P.op('dve', lambda e: e.tensor_tensor(out=A[:, 0:n + 9], in0=B[:, 0:n + 9],
                                                                  in1=B[:, 4:n + 13], op=ALU.add), r=['pB'], w=['pA'])
                            cur, curn, o = A, 'pA', 4
                        if w_ >= 16:
                            P.op('dve', lambda e: e.tensor_tensor(out=B[:, 0:n + 1], in0=A[:, 0:n + 1],
                                                                  in1=A[:, 8:n + 9], op=ALU.add), r=['pA'], w=['pB'])
                            cur, curn, o = B, 'pB', 0
                        o = 8 - w_ // 2
                        P.op('dve', lambda e, cur=cur, o=o, sl=sl: e.tensor_tensor(
                            out=cur[:, o:o + n], in0=cur[:, o:o + n], in1=ivc[sl][:, 0:n], op=ALU.mult),
                            r=[curn, 'ivc%d' % sl], w=[curn])
                        P.op('dve', lambda e, cur=cur, o=o, sl=sl, u=u: e.tensor_tensor(
                            out=db[sl][:, 0:n], in0=cur[:, o:o + n], in1=u[:, 8:8 + n], op=ALU.subtract),
                            r=[curn, un], w=['pdb%d' % sl])
                        bank = nb()
                        P.op('pe', lambda e, g=g, sl=sl, bank=bank: e.matmul(ps[:, bank, 0:n], lhsT=pw[:, g, :],
                                                                             rhs=db[sl][:, 0:n], start=True, stop=True),
                             r=['pw', 'pdb%d' % sl], w=[PSB(bank)])
                        P.op('act', lambda e, g=g, sl=sl, bank=bank: e.activation(
                            out=yb[sl][:, 0:n], in_=ps[:, bank, 0:n], func=AF.Copy, scale=gv(l, 11 + g)),
                            r=[PSB(bank), 'GV'], w=['pyb%d' % sl])
                        P.dma('pool', S['mixT'][12 + g, :, tok0:tok0 + n], yb[sl][:, 0:n], r=['pyb%d' % sl], w=['mixT'])
            P.barrier()

    def phase_C1(l, chunks=range(NCH)):
        with ExitStack() as st:
            xT = [salloc('xT%d' % i, (128, FC, 512), F32, st) for i in range(2)]
            mx = [salloc('mx%d' % i, (128, FC, 512), BF16, st) for i in range(2)]
            wb = [salloc('wb%d' % i, (128, FC, 512), BF16, st) for i in range(3)]
            xsrc = 'xTa' if l % 2 == 0 else 'xTb'
            stream = [(c, g) for c in chunks for g in range(4)]
            loaded = {}

            def wload(i):
                c, g = stream[i]
                sl = i % 3
                P.dma('sp', wb[sl][:], S['Wout'][l, g], r=['Wscr'], w=['wb%d' % sl])
                loaded[i] = sl
            for i in range(min(2, len(stream))):
                wload(i)
            for i, (c, g) in enumerate(stream):
                if i + 2 < len(stream):
                    wload(i + 2)
                sl = loaded[i]
                v, segs = chunk_info(c)
                tok0 = c * 512
                cs = (i // 4) % 2
                if g == 0:
                    P.dma('sp', xT[cs][:], S[xsrc][:, :, tok0:tok0 + 512].rearrange("c p t -> p c t"), r=[xsrc],
                          w=['xT%d' % cs])
                    P.dma('sp', mx[cs][:], S['mixT'][:, :, tok0:tok0 + 512].rearrange("c p t -> p c t"), r=['mixT'],
                          w=['mx%d' % cs])
                for j in range(4):
                    dc = g * 4 + j
                    bank = nb()
                    for fc in range(FC):
                        P.op('pe', lambda e, fc=fc, j=j, sl=sl, cs=cs, bank=bank: e.matmul(
                            ps[:, bank, :], lhsT=wb[sl][:, fc, j * 128:(j + 1) * 128], rhs=mx[cs][:, fc, :],
                            start=(fc == 0), stop=(fc == FC - 1)), r=['wb%d' % sl, 'mx%d' % cs], w=[PSB(bank)])
                    g1 = prm(l, v, 2)
                    P.op('dve', lambda e, dc=dc, cs=cs, bank=bank, g1=g1: e.scalar_tensor_tensor(
                        out=xT[cs][:, dc, :], in0=ps[:, bank, :], scalar=g1[:, dc:dc + 1], in1=xT[cs][:, dc, :],
                        op0=ALU.mult, op1=ALU.add), r=[PSB(bank), 'xT%d' % cs, 'PRM'], w=['xT%d' % cs])
                if g == 3:
                    P.dma('pool', S['x1T'][:, :, tok0:tok0 + 512].rearrange("c p t -> p c t"), xT[cs][:],
                          r=['xT%d' % cs], w=['x1T'])
            P.barrier()

    def phase_C2(l, chunks=range(NCH)):
        with ExitStack() as st:
            T = {}
            T['sq'] = [salloc('sq%d' % i, (128, 512), BF16, st) for i in range(2)]
            T['rt'] = salloc('rt', (128, 512), F32, st)
            T['rsx'] = salloc('rsx', (128, 512), F32, st)
            T['tmp'] = [salloc('tmp%d' % i, (128, 512), F32, st) for i in range(2)]
            xT = salloc('xT', (128, FC, 512), F32, st)
            xh = salloc('xh', (128, FC, 2), F32, st)
            hT = salloc('hT', (128, FC, 512), BF16, st)
            hTh = salloc('hTh', (128, FC, 2), BF16, st)
            actT = salloc('actT', (128, 44, 512), BF16, st)
            wb = [salloc('wb%d' % i, (128, FC, 512), BF16, st) for i in range(3)]
            Ab = [salloc('Ab%d' % i, (128, 512), F32, st) for i in range(2)]
            sg = [salloc('sg%d' % i, (128, 512), F32, st) for i in range(4)]
            xdst = 'xTb' if l % 2 == 0 else 'xTa'
            items = []
            for k in range(11):
                items += [('u', k, 0), ('u', k, 1)]
            for dg in range(4):
                for fq in range(4):
                    items.append(('d', dg, fq))
            stream = [(c,) + it for c in chunks for it in items]
            loaded = {}

            def wload(i):
                c, kind, a, b = stream[i]
                sl = i % 3
                if kind == 'u':
                    P.dma('sp', wb[sl][:], S['Wup'][l, a + 11 * b], r=['Wscr'], w=['wb%d' % sl])
                else:
                    P.dma('sp', wb[sl][:, 0:11, :], S['Wdn'][l, a, b], r=['Wscr'], w=['wb%d' % sl])
                loaded[i] = sl
            for i in range(min(2, len(stream))):
                wload(i)
            ai = 0
            dbanks = None
            for i, (c, kind, a, b) in enumerate(stream):
                if i + 2 < len(stream):
                    wload(i + 2)
                sl = loaded[i]
                W = wb[sl]
                Wn = 'wb%d' % sl
                v, segs = chunk_info(c)
                tok0 = c * 512
                s_ = segs[0][0]
                hasl = c > 1
                hasr = (c > 0) and (c < NCH - 1)
                if kind == 'u' and a == 0 and b == 0:
                    P.dma('sp', xT[:], S['x1T'][:, :, tok0:tok0 + 512].rearrange("c p t -> p c t"), r=['x1T'], w=['xT'])
                    make_hT(T, xT, 'xT', hT, 'hT', l, v, 2)
                    if c > 0:
                        P.op('dve', lambda e: e.memset(xh[:], 0.0), w=['xh'])
                        if hasl:
                            P.dma('sp', xh[:, :, 0:1], S['x1T'][:, :, tok0 - 1:tok0].rearrange("c p t -> p c t"),
                                  r=['x1T'], w=['xh'], slow=True)
                        if hasr:
                            P.dma('sp', xh[:, :, 1:2], S['x1T'][:, :, tok0 + 512:tok0 + 513].rearrange("c p t -> p c t"),
                                  r=['x1T'], w=['xh'], slow=True)
                        make_hT(T, xh, 'xh', hTh, 'hTh', l, v, 2, N=2)
                if kind == 'u':
                    hb = 6 + (i % 2)
                    for j in range(4):
                        ti = a * 4 + j
                        ct = ti + 44 * b
                        bank = rr['pool'][0]
                        rr['b'] += 1
                        bank = rr['b'] % 6
                        for fc in range(FC):
                            P.op('pe', lambda e, fc=fc, j=j: e.matmul(ps[:, bank, :], lhsT=W[:, fc, j * 128:(j + 1) * 128],
                                                                      rhs=hT[:, fc, :], start=(fc == 0),
                                                                      stop=(fc == FC - 1)), r=[Wn, 'hT'], w=[PSB(bank)])
                        if c > 0:
                            for fc in range(FC):
                                P.op('pe', lambda e, fc=fc, j=j: e.matmul(
                                    ps[:, hb, j * 2:j * 2 + 2], lhsT=W[:, fc, j * 128:(j + 1) * 128], rhs=hTh[:, fc, :],
                                    start=(fc == 0), stop=(fc == FC - 1)), r=[Wn, 'hTh'], w=[PSB(hb)])
                        ai += 1
                        A = Ab[ai % 2]
                        An = 'Ab%d' % (ai % 2)
                        P.op('act', lambda e, ct=ct, A=A: e.activation(out=A[:], in_=ps[:, bank, :], func=AF.Identity,
                                                                       bias=cw(l, 3, ct), scale=cw(l, 1, ct)),
                             r=[PSB(bank), 'CW'], w=[An])
                        for (sq_, col0, n, tl0) in segs:
                            P.op('dve', lambda e, ct=ct, A=A, col0=col0, n=n: e.scalar_tensor_tensor(
                                out=A[:, col0 + 1:col0 + n], in0=ps[:, bank, col0:col0 + n - 1], scalar=cw(l, 0, ct),
                                in1=A[:, col0 + 1:col0 + n], op0=ALU.mult, op1=ALU.add),
                                r=[PSB(bank), An, 'CW'], w=[An])
                            P.op('dve', lambda e, ct=ct, A=A, col0=col0, n=n: e.scalar_tensor_tensor(
                                out=A[:, col0:col0 + n - 1], in0=ps[:, bank, col0 + 1:col0 + n], scalar=cw(l, 2, ct),
                                in1=A[:, col0:col0 + n - 1], op0=ALU.mult, op1=ALU.add),
                                r=[PSB(bank), An, 'CW'], w=[An])
                        if hasl:
                            P.op('dve', lambda e, ct=ct, A=A, j=j: e.scalar_tensor_tensor(
                                out=A[:, 0:1], in0=ps[:, hb, j * 2:j * 2 + 1], scalar=cw(l, 0, ct), in1=A[:, 0:1],
                                op0=ALU.mult, op1=ALU.add), r=[PSB(hb), An, 'CW'], w=[An])
                        if hasr:
                            P.op('dve', lambda e, ct=ct, A=A, j=j: e.scalar_tensor_tensor(
                                out=A[:, 511:512], in0=ps[:, hb, j * 2 + 1:j * 2 + 2], scalar=cw(l, 2, ct),
                                in1=A[:, 511:512], op0=ALU.mult, op1=ALU.add), r=[PSB(hb), An, 'CW'], w=[An])
                        if b == 0:
                            P.op('act', lambda e, A=A, j=j: e.activation(out=sg[j][:], in_=A[:], func=AF.Silu),
                                 r=[An], w=['sg%d' % j])
                        else:
                            P.op('dve', lambda e, A=A, j=j, ti=ti: e.tensor_tensor(out=actT[:, ti, :], in0=A[:],
                                                                                   in1=sg[j][:], op=ALU.mult),
                                 r=[An, 'sg%d' % j], w=['actT'])
                else:
                    dg, fq = a, b
                    base = 0 if dg % 2 == 0 else 4
                    for j in range(4):
                        bank = base + j
                        for kk in range(11):
                            P.op('pe', lambda e, kk=kk, j=j, bank=bank: e.matmul(
                                ps[:, bank, :], lhsT=W[:, kk, j * 128:(j + 1) * 128], rhs=actT[:, fq * 11 + kk, :],
                                start=(fq == 0 and kk == 0), stop=(fq == 3 and kk == 10)),
                                r=[Wn, 'actT'], w=[PSB(bank)])
                    if fq == 3:
                        g2 = prm(l, v, 5)
                        for j in range(4):
                            dc = dg * 4 + j
                            bank = base + j
                            P.op('dve', lambda e, dc=dc, bank=bank, g2=g2: e.scalar_tensor_tensor(
                                out=xT[:, dc, :], in0=ps[:, bank, :], scalar=g2[:, dc:dc + 1], in1=xT[:, dc, :],
                                op0=ALU.mult, op1=ALU.add), r=[PSB(bank), 'xT', 'PRM'], w=['xT'])
                        if dg == 3:
                            P.dma('pool', S[xdst][:, :, tok0:tok0 + 512].rearrange("c p t -> p c t"), xT[:],
                                  r=['xT'], w=[xdst])
            P.barrier()

    def phase_T1(src, tiles=range(NT // 128)):
        with ExitStack() as st:
            xtok = [salloc('xtok%d' % i, (128, D), F32, st) for i in range(2)]
            xTt = [salloc('xTt%d' % i, (128, FC, 128), F32, st) for i in range(2)]
            tiles = list(tiles)

            def t1load(k):
                P.dma('sp', xTt[k % 2][:], S[src][:, :, tiles[k] * 128:tiles[k] * 128 + 128].rearrange("c p t -> p c t"),
                      r=[src], w=['xTt%d' % (k % 2)])
            t1load(0)
            for k, tt in enumerate(tiles):
                slot = k % 2
                t0 = tt * 128
                if k + 1 < len(tiles):
                    t1load(k + 1)
                for q in range(4):
                    bank = nb()
                    for j in range(4):
                        fc = q * 4 + j
                        P.op('pe', lambda e, slot=slot, fc=fc, j=j, bank=bank: e.transpose(
                            out=ps[:, bank, j * 128:(j + 1) * 128], in_=xTt[slot][:, fc, :], identity=identf),
                            r=['xTt%d' % slot, 'cst'], w=[PSB(bank)])
                    copy_op(xtok[slot][:, q * 512:(q + 1) * 512], ps[:, bank, :], r=[PSB(bank)], w=['xtok%d' % slot])
                dst = O['yp'][t0:t0 + 128, :] if t0 < NP else O['ys'][t0 - NP:t0 - NP + 128, :]
                P.dma('pool', dst, xtok[slot][:], r=['xtok%d' % slot], w=['yout'])
            P.barrier()

    dbg_chunks = range(NCH)
    if stop_after in ('pB', 'pAp', 'pC1', 'pC2'):
        phase_A(0, chunks=[0])
        phase_B(0, seqs=(0, 1))
        if stop_after != 'pB':
            phase_Ap(0, chunks=[0])
        if stop_after in ('pC1', 'pC2'):
            phase_C1(0, chunks=[0])
        if stop_after == 'pC2':
            phase_C2(0, chunks=[0])
        return finish(nc, es, P)
    if stop_after in ('sA', 'sB', 'sAp', 'sC1', 'sC2'):
        phase_A(0, chunks=[1, 2])
        if stop_after != 'sA':
            phase_E(0)
            phase_B(0, seqs=(2,))
        if stop_after in ('sAp', 'sC1', 'sC2'):
            phase_Ap(0, chunks=[1, 2])
        if stop_after in ('sC1', 'sC2'):
            phase_C1(0, chunks=[1, 2])
        if stop_after == 'sC2':
            phase_C2(0, chunks=[1, 2])
        return finish(nc, es, P)
    if stop_after == 'E':
        phase_E(0)
        return finish(nc, es, P)
    if stop_after == 'prompt':
        for l in range(L):
            phase_A(l, chunks=[0])
            phase_B(l, seqs=(0, 1))
            phase_Ap(l, chunks=[0])
            phase_C1(l, chunks=[0])
            phase_C2(l, chunks=[0])
        phase_T1('xTa', tiles=range(4))
        return finish(nc, es, P)
    if stop_after is None:
        for l in range(L):
            phase_A(l)
            phase_E(l)
            phase_B(l)
            phase_Ap(l)
            phase_C1(l)
            phase_C2(l)
        phase_T1('xTa')
        return finish(nc, es, P)
    dbg_chunks = range(NCH)
    if stop_after == 'A0c0':
        phase_A(0, chunks=[0])
        return finish(nc, es, P)
    if stop_after == 'A0c01':
        phase_A(0, chunks=[0, 1])
        return finish(nc, es, P)
    raise NotImplementedError


def finish(nc, es, P):
    P.barrier()
    P.flush()
    es.close()
    return nc


_CONSTS = None


def make_in_maps(inp):
    global _CONSTS
    if _CONSTS is None:
        _CONSTS = _host_consts()
    f = lambda a: np.ascontiguousarray(np.asarray(a, dtype=np.float32))
    shared = {k: f(inp[k]) for k in ['norm1_g', 'norm2_g', 'ada_w', 'ada_b', 'w_in', 'diff_qn_g', 'diff_kn_g',
                                     'diff_lam', 'diff_sub_g', 'mla_qa_g', 'mla_kva_g', 'mla_w_uq', 'mla_w_ukv',
                                     'mla_qn_g', 'mla_kn_g', 'na_qn_g', 'na_kn_g', 'na_bias', 'pool_w', 'pool_scale',
                                     'w_out', 'w_up', 'conv_w', 'conv_b', 'w_down']}
    shared.update(_CONSTS)
    maps = []
    for c in range(8):
        m = dict(shared)
        m['xp'] = f(inp['x_prompt'][2 * c:2 * c + 2]).reshape(NP, D)
        m['xs'] = f(inp['x_sample'][c])
        m['cdk'] = f(inp['cache_diff_k'][c]).reshape(L, PAST, 512)
        m['cdv'] = f(inp['cache_diff_v'][c]).reshape(L, PAST, 512)
        m['cckv'] = f(inp['cache_mla_ckv'][c])
        m['ckpe'] = f(inp['cache_mla_kpe'][c])
        m['cnk'] = f(inp['cache_na_k'][c]).reshape(L, PAST, 512)
        m['cnv'] = f(inp['cache_na_v'][c]).reshape(L, PAST, 512)
        m['cvec'] = np.ascontiguousarray(np.stack([f(inp['c_ctx']), f(inp['c'][c])], 0))
        maps.append(m)
    return maps


_NC = None


def kernel(**inp):
    global _NC
    if _NC is None:
        _NC = build_program()
    maps = make_in_maps(inp)
    res = run_bass_kernel_spmd(_NC, maps, core_ids=list(range(8)))
    R = res.results
    yp = np.concatenate([r['yp'].reshape(2, 256, D) for r in R], 0)
    ys = np.stack([r['ys'] for r in R], 0)
    dk = np.concatenate([r['o_dk'].reshape(2, L, 256, 4, 2, 64) for r in R], 0)
    dv = np.concatenate([r['o_dv'].reshape(2, L, 256, 4, 128) for r in R], 0)
    ckv = np.concatenate([r['o_ckv'].reshape(2, L, 256, 128) for r in R], 0)
    kpe = np.concatenate([r['o_kpe'].reshape(2, L, 256, 32) for r in R], 0)
    nk = np.concatenate([r['o_nk'].reshape(2, L, 256, 8, 64) for r in R], 0)
    nv = np.concatenate([r['o_nv'].reshape(2, L, 256, 8, 64) for r in R], 0)
    return tuple(np.ascontiguousarray(a.astype(np.float32)) for a in (yp, ys, dk, dv, ckv, kpe, nk, nv))
```
